# Optimizing a Trainium2 kernel written in Bass

```python
import math
import jax, jax.numpy as jnp
from jax import lax
import numpy as np

D_MODEL = 1024
BATCH = 16
SEQ = 2048
DEPTH = 1

MEM_LEN = 256
MIX_WIDTH = D_MODEL
ML_HEADS = 4
ML_WIDTH = MIX_WIDTH // 2
ML_HEAD_DIM = ML_WIDTH // ML_HEADS
ML_CONV = 4
ML_CHUNK = 64
DSA_HEADS = 8
DSA_WIDTH = MIX_WIDTH - ML_WIDTH
DSA_HEAD_DIM = DSA_WIDTH // DSA_HEADS
DSA_LATENT = D_MODEL // 8
IDX_HEADS = 8
IDX_DIM = 64
INDEX_TOPK = 256
Q_BLOCK = 128
XA_HEADS = 4
XA_HEAD_DIM = D_MODEL // XA_HEADS
D_FF = ((8 * D_MODEL // 3 + 127) // 128) * 128
EPS = 1e-6

IN_SPLITS = (ML_WIDTH, ML_WIDTH, ML_WIDTH, ML_HEADS, ML_HEADS, ML_WIDTH,
             DSA_HEADS * DSA_LATENT, DSA_LATENT, IDX_HEADS * IDX_DIM, IDX_DIM, IDX_HEADS)
D_IN = sum(IN_SPLITS)

kernel_name = "hymba_mlstm_dsa_macaron"

F32 = jnp.float32


def rmsnorm(x, g):
    xf = x.astype(F32)
    y = xf * lax.rsqrt(jnp.mean(xf * xf, axis=-1, keepdims=True) + EPS)
    return (y * g.astype(F32)).astype(x.dtype)


def swiglu(x, w_gate, w_up, w_down):
    return (jax.nn.silu(x @ w_gate) * (x @ w_up)) @ w_down


def causal_dwconv(x, w, b):
    k = w.shape[0]
    y = lax.conv_general_dilated(x, w[:, None, :].astype(x.dtype), window_strides=(1,),
                                 padding=[(k - 1, 0)],
                                 dimension_numbers=('NWC', 'WIO', 'NWC'),
                                 feature_group_count=x.shape[-1])
    return y + b.astype(x.dtype)


def mlstm_chunkwise(q, k, v, ig, lf):
    B, H, S, d = q.shape
    L = ML_CHUNK
    nc = S // L
    ch = lambda t: jnp.moveaxis(t.reshape(B, H, nc, L, *t.shape[3:]), 2, 0)
    causal = jnp.tril(jnp.ones((L, L), dtype=bool))

    def step(carry, inp):
        C, n, m = carry
        qc, kc, vc, ic, fc = inp
        b = jnp.cumsum(fc, axis=-1)
        logw = jnp.where(causal, b[..., :, None] - b[..., None, :] + ic[..., None, :], -jnp.inf)
        inter = b + m[..., None]
        mj = jnp.maximum(inter, jnp.max(logw, axis=-1))
        w = jnp.exp(logw - mj[..., None])
        a = jnp.exp(inter - mj)
        sqk = jnp.einsum('bhjd,bhsd->bhjs', qc, kc) * w
        num = a[..., None] * jnp.einsum('bhed,bhjd->bhje', C, qc) + jnp.einsum('bhjs,bhse->bhje', sqk, vc)
        den = a * jnp.einsum('bhd,bhjd->bhj', n, qc) + jnp.sum(sqk, axis=-1)
        h = num / jnp.maximum(jnp.abs(den), jnp.exp(-mj))[..., None]
        bL = b[..., -1]
        g = bL[..., None] - b + ic
        m_new = jnp.maximum(bL + m, jnp.max(g, axis=-1))
        wg = jnp.exp(g - m_new[..., None])
        dec = jnp.exp(bL + m - m_new)
        C = dec[..., None, None] * C + jnp.einsum('bhs,bhse,bhsd->bhed', wg, vc, kc)
        n = dec[..., None] * n + jnp.einsum('bhs,bhsd->bhd', wg, kc)
        return (C, n, m_new), h

    init = (jnp.zeros((B, H, d, d), F32), jnp.zeros((B, H, d), F32), jnp.zeros((B, H), F32))
    _, hs = lax.scan(step, init, (ch(q), ch(k), ch(v), ch(ig), ch(lf)))
    return jnp.moveaxis(hs, 0, 2).reshape(B, H, S, d)


def mlstm_group(q, k, v, i_pre, f_pre, o_pre, head_g):
    B, S, _ = q.shape
    heads = lambda t: t.reshape(B, S, ML_HEADS, ML_HEAD_DIM).transpose(0, 2, 1, 3).astype(F32)
    qh = heads(q)
    kh = heads(k) * (ML_HEAD_DIM ** -0.5)
    vh = heads(v)
    ig = i_pre.astype(F32).transpose(0, 2, 1)
    lf = jax.nn.log_sigmoid(f_pre.astype(F32)).transpose(0, 2, 1)
    h = mlstm_chunkwise(qh, kh, vh, ig, lf).transpose(0, 2, 1, 3)
    h = rmsnorm(h, head_g.reshape(ML_HEADS, ML_HEAD_DIM))
    h = h.reshape(B, S, ML_WIDTH) * jax.nn.sigmoid(o_pre.astype(F32))
    return h.astype(q.dtype)


def dsa_group(dq, dc, iq, ik, iw, kv_g, idx_g, w_uv):
    B, S, _ = dq.shape
    qm = dq.reshape(B, S, DSA_HEADS, DSA_LATENT)
    ckv = rmsnorm(dc, kv_g)
    qi = iq.reshape(B, S, IDX_HEADS, IDX_DIM)
    ki = rmsnorm(ik, idx_g)
    wi = iw.astype(F32) * (IDX_HEADS ** -0.5)
    topk = min(INDEX_TOPK, S // 4)
    nb = S // Q_BLOCK
    blk = lambda t: jnp.moveaxis(t.reshape(B, nb, Q_BLOCK, *t.shape[2:]), 1, 0)
    key_pos = jnp.arange(S)

    def one_block(args):
        bi, qm_b, qi_b, wi_b = args
        qpos = bi * Q_BLOCK + jnp.arange(Q_BLOCK)
        sc = jnp.einsum('bqhd,bsd->bqhs', qi_b, ki).astype(F32) * (IDX_DIM ** -0.5)
        score = jnp.einsum('bqh,bqhs->bqs', wi_b, jax.nn.relu(sc))
        causal = key_pos[None, :] <= qpos[:, None]
        score = jnp.where(causal[None], score, -jnp.inf)
        _, idx = lax.top_k(score, topk)
        valid = idx <= qpos[None, :, None]
        kv = jax.vmap(lambda c, i: c[i])(ckv, idx)
        lg = jnp.einsum('bqhc,bqkc->bqhk', qm_b, kv).astype(F32) * (DSA_LATENT ** -0.5)
        lg = jnp.where(valid[:, :, None, :], lg, -jnp.inf)
        p = jax.nn.softmax(lg, axis=-1).astype(kv.dtype)
        return jnp.einsum('bqhk,bqkc->bqhc', p, kv)

    o = lax.map(one_block, (jnp.arange(nb), blk(qm), blk(qi), blk(wi)))
    o = jnp.moveaxis(o, 0, 1).reshape(B, S, DSA_HEADS, DSA_LATENT)
    return jnp.einsum('bshc,hcv->bshv', o, w_uv).reshape(B, S, DSA_WIDTH)


def cross_attn(u, memn, w_q, w_kv, w_o):
    B, S, _ = u.shape
    M = memn.shape[1]
    q = (u @ w_q).reshape(B, S, XA_HEADS, XA_HEAD_DIM)
    k, v = jnp.split(memn @ w_kv, 2, axis=-1)
    k = k.reshape(B, M, XA_HEADS, XA_HEAD_DIM)
    v = v.reshape(B, M, XA_HEADS, XA_HEAD_DIM)
    lg = jnp.einsum('bshd,bmhd->bhsm', q, k).astype(F32) * (XA_HEAD_DIM ** -0.5)
    p = jax.nn.softmax(lg, axis=-1).astype(v.dtype)
    o = jnp.einsum('bhsm,bmhd->bshd', p, v).reshape(B, S, D_MODEL)
    return o @ w_o


def setup_inputs(seed: int = 0) -> dict:
    key = jax.random.key(seed)
    ks = jax.random.split(key, 32)
    nrm = lambda k, shape, s: jax.random.normal(k, shape, F32) * s
    gain = lambda k, shape: 1.0 + 0.02 * jax.random.normal(k, shape, F32)
    L = DEPTH
    return {
        "x": nrm(ks[0], (BATCH, SEQ, D_MODEL), 1.0),
        "mem": nrm(ks[1], (BATCH, MEM_LEN, D_MODEL), 1.0),
        "ffn1_norm_g": gain(ks[2], (L, D_MODEL)),
        "ffn1_w_gate": nrm(ks[3], (L, D_MODEL, D_FF), D_MODEL ** -0.5),
        "ffn1_w_up": nrm(ks[4], (L, D_MODEL, D_FF), D_MODEL ** -0.5),
        "ffn1_w_down": nrm(ks[5], (L, D_FF, D_MODEL), D_FF ** -0.5),
        "mix_norm_g": gain(ks[6], (L, D_MODEL)),
        "w_in": nrm(ks[7], (L, D_MODEL, D_IN), D_MODEL ** -0.5),
        "mlstm_conv_w": nrm(ks[8], (L, ML_CONV, 2 * ML_WIDTH), ML_CONV ** -0.5),
        "mlstm_conv_b": nrm(ks[9], (L, 2 * ML_WIDTH), 0.02),
        "mlstm_i_bias": nrm(ks[10], (L, ML_HEADS), 0.1),
        "mlstm_f_bias": jnp.linspace(3.0, 6.0, ML_HEADS, dtype=F32)[None, :] + nrm(ks[11], (L, ML_HEADS), 0.1),
        "mlstm_head_norm_g": gain(ks[12], (L, ML_WIDTH)),
        "dsa_kv_norm_g": gain(ks[13], (L, DSA_LATENT)),
        "idx_k_norm_g": gain(ks[14], (L, IDX_DIM)),
        "dsa_w_uv": nrm(ks[15], (L, DSA_HEADS, DSA_LATENT, DSA_HEAD_DIM), DSA_LATENT ** -0.5),
        "w_out": nrm(ks[16], (L, MIX_WIDTH, D_MODEL), MIX_WIDTH ** -0.5),
        "xattn_norm_g": gain(ks[17], (L, D_MODEL)),
        "mem_norm_g": gain(ks[18], (L, D_MODEL)),
        "xattn_w_q": nrm(ks[19], (L, D_MODEL, D_MODEL), D_MODEL ** -0.5),
        "xattn_w_kv": nrm(ks[20], (L, D_MODEL, 2 * D_MODEL), D_MODEL ** -0.5),
        "xattn_w_o": nrm(ks[21], (L, D_MODEL, D_MODEL), D_MODEL ** -0.5),
        "ffn2_norm_g": gain(ks[22], (L, D_MODEL)),
        "ffn2_w_gate": nrm(ks[23], (L, D_MODEL, D_FF), D_MODEL ** -0.5),
        "ffn2_w_up": nrm(ks[24], (L, D_MODEL, D_FF), D_MODEL ** -0.5),
        "ffn2_w_down": nrm(ks[25], (L, D_FF, D_MODEL), D_FF ** -0.5),
        "final_norm_g": gain(ks[26], (D_MODEL,)),
    }


def reference(x, mem, ffn1_norm_g, ffn1_w_gate, ffn1_w_up, ffn1_w_down, mix_norm_g, w_in,
              mlstm_conv_w, mlstm_conv_b, mlstm_i_bias, mlstm_f_bias, mlstm_head_norm_g,
              dsa_kv_norm_g, idx_k_norm_g, dsa_w_uv, w_out, xattn_norm_g, mem_norm_g,
              xattn_w_q, xattn_w_kv, xattn_w_o, ffn2_norm_g, ffn2_w_gate, ffn2_w_up,
              ffn2_w_down, final_norm_g):
    offsets = [int(c) for c in np.cumsum(IN_SPLITS)[:-1]]
    h = x
    for l in range(DEPTH):
        h = h + 0.5 * swiglu(rmsnorm(h, ffn1_norm_g[l]), ffn1_w_gate[l], ffn1_w_up[l], ffn1_w_down[l])
        u = rmsnorm(h, mix_norm_g[l])
        z = u @ w_in[l]
        mq, mk, mv, mi, mf, mo, dq, dc, iq, ik, iw = jnp.split(z, offsets, axis=-1)
        qk = jax.nn.silu(causal_dwconv(jnp.concatenate([mq, mk], axis=-1), mlstm_conv_w[l], mlstm_conv_b[l]))
        mq, mk = jnp.split(qk, 2, axis=-1)
        y_ml = mlstm_group(mq, mk, mv, mi + mlstm_i_bias[l], mf + mlstm_f_bias[l], mo, mlstm_head_norm_g[l])
        y_dsa = dsa_group(dq, dc, iq, ik, iw, dsa_kv_norm_g[l], idx_k_norm_g[l], dsa_w_uv[l])
        h = h + jnp.concatenate([y_ml, y_dsa], axis=-1) @ w_out[l]
        h = h + cross_attn(rmsnorm(h, xattn_norm_g[l]), rmsnorm(mem, mem_norm_g[l]),
                           xattn_w_q[l], xattn_w_kv[l], xattn_w_o[l])
        h = h + 0.5 * swiglu(rmsnorm(h, ffn2_norm_g[l]), ffn2_w_gate[l], ffn2_w_up[l], ffn2_w_down[l])
    return rmsnorm(h, final_norm_g)
```

```python
from contextlib import ExitStack
import numpy as np
import concourse.bass as bass
import concourse.mybir as mybir
from concourse.bass_utils import run_bass_kernel_spmd

F32 = mybir.dt.float32
BF16 = mybir.dt.bfloat16
AF = mybir.ActivationFunctionType
ALU = mybir.AluOpType
AX = mybir.AxisListType

NTOK = 4096
SEQ = 2048
D = 1024
DFF = 2816
NFC = 22
EPS = 1e-6
NEG = -1.0e30
DEBUG_HOOK = None


class Buf:
    __slots__ = ("w", "r")

    def __init__(self):
        self.w = None
        self.r = {}


class Sync:
    def __init__(self, nc, stack):
        self.nc = nc
        self.stack = stack
        self.engs = {"pe": nc.tensor, "act": nc.scalar, "dve": nc.vector, "pool": nc.gpsimd, "sp": nc.sync}
        self.sem = {k: stack.enter_context(nc.semaphore("s_" + k)) for k in ["pe", "act", "dve", "pool"]}
        self.cnt = {k: 0 for k in self.sem}
        self.waited = {k: {} for k in self.engs}
        self.dsem = {}

    def _wait(self, eng, toks):
        need = {}
        for t in toks:
            if t is None:
                continue
            key, sem, val, src = t
            if src == "pe" and eng == "pe":
                continue
            if self.waited[eng].get(key, 0) >= val:
                continue
            if key not in need or need[key][1] < val:
                need[key] = (sem, val)
        for key, (sem, val) in need.items():
            self.engs[eng].wait_ge(sem, val)
            self.waited[eng][key] = val

    @staticmethod
    def _collect(reads, writes):
        toks = []
        for b in reads:
            toks.append(b.w)
        for b in writes:
            toks.append(b.w)
            toks.extend(b.r.values())
        return toks

    @staticmethod
    def _update(tok, reads, writes):
        key = tok[0]
        for b in reads:
            o = b.r.get(key)
            if o is None or o[2] < tok[2]:
                b.r[key] = tok
        for b in writes:
            b.w = tok
            b.r = {}

    def op(self, eng, fn, reads=(), writes=()):
        self._wait(eng, self._collect(reads, writes))
        ins = fn(self.engs[eng])
        self.cnt[eng] += 1
        ins.then_inc(self.sem[eng], 1)
        tok = (eng, self.sem[eng], self.cnt[eng], eng)
        self._update(tok, reads, writes)
        return tok

    def dma(self, q, out, in_, sname, reads=(), writes=(), **kw):
        self._wait(q, self._collect(reads, writes))
        if sname not in self.dsem:
            self.dsem[sname] = [self.stack.enter_context(self.nc.semaphore("d_" + sname)), 0]
        d = self.dsem[sname]
        ins = self.engs[q].dma_start(out=out, in_=in_, **kw)
        d[1] += 16
        ins.then_inc(d[0], 16)
        tok = ("d_" + sname, d[0], d[1], None)
        self._update(tok, reads, writes)
        return tok

    def wait_all(self, eng, bufs):
        toks = []
        for b in bufs:
            toks.append(b.w)
            toks.extend(b.r.values())
        self._wait(eng, toks)


class Ctx:
    pass


def bc_mid(ap2, n):
    return ap2.unsqueeze(2).to_broadcast([ap2.shape[0], ap2.shape[1], n])


def norm_transpose(C, st_sb, xt_ap, xt_b, gT_ap, dstT, dst_b, col0, tag, ps_bank=2):
    nc, S = C.nc, C.S
    W = C.work
    S.op("act", lambda e: e.activation(out=W["junk"][:], in_=xt_ap, func=AF.Square,
                                       accum_out=W["ss"][:, 0:1]),
         reads=[xt_b], writes=[W["junk_b"], W["ss_b"]])
    S.op("act", lambda e: e.activation(out=W["ss"][:, 1:2], in_=W["ss"][:, 0:1], func=AF.Sqrt,
                                       bias=C.eps_t[:, 0:1], scale=1.0 / D),
         reads=[], writes=[W["ss_b"]])
    S.op("dve", lambda e: e.reciprocal(out=W["ss"][:, 2:3], in_=W["ss"][:, 1:2]), reads=[], writes=[W["ss_b"]])
    xs = W["xs"]
    S.op("pool", lambda e: e.tensor_scalar(out=xs[:], in0=xt_ap, scalar1=W["ss"][:, 2:3], scalar2=0.0,
                                           op0=ALU.mult, op1=ALU.add),
         reads=[xt_b, W["ss_b"]], writes=[W["xs_b"]])
    tp = C.psv_bf[ps_bank]

    def tr(e):
        ins = None
        for kc in range(8):
            ins = e.transpose(out=tp[:, kc, :], in_=xs[:, kc * 128:(kc + 1) * 128], identity=C.ident[:])
        return ins
    S.op("pe", tr, reads=[W["xs_b"]], writes=[C.psb[ps_bank]])
    S.op("dve", lambda e: e.tensor_tensor(out=dstT[:, :, col0:col0 + 128], in0=tp[:, :, :],
                                          in1=bc_mid(gT_ap, 128), op=ALU.mult),
         reads=[C.psb[ps_bank]], writes=[dst_b])


def make_work(C, st, pfx):
    nc = C.nc
    W = {}
    W["junk"] = st.enter_context(nc.sbuf_tensor(pfx + "w_junk", [128, 1024], BF16))
    W["junk_b"] = Buf()
    W["ss"] = st.enter_context(nc.sbuf_tensor(pfx + "w_ss", [128, 4], F32))
    W["ss_b"] = Buf()
    W["xs"] = st.enter_context(nc.sbuf_tensor(pfx + "w_xs", [128, 1024], BF16))
    W["xs_b"] = Buf()
    C.work = W


def ffn_phase(C, pfx, src, dst, gT_ap, wg_d, wu_d, wd_d, fin_g=None):
    nc, S = C.nc, C.S
    with ExitStack() as st:
        def sb(name, shape, dt):
            return st.enter_context(nc.sbuf_tensor(pfx + name, shape, dt))
        xres = sb("xres", [128, 8, 1024], F32)
        xres_b = [Buf() for _ in range(8)]
        xnT = sb("xnT", [128, 8, 1024], BF16)
        xnT_b = Buf()
        hT = sb("hT", [128, NFC, 1024], BF16)
        hT_b = [Buf(), Buf()]
        wd = sb("wd", [128, NFC, 1024], BF16)
        wd_b = Buf()
        wg = [sb("wg%d" % i, [128, 8, 256], BF16) for i in range(2)]
        wu = [sb("wu%d" % i, [128, 8, 256], BF16) for i in range(2)]
        wg_b = [Buf(), Buf()]
        wu_b = [Buf(), Buf()]
        sg = [sb("sg%d" % i, [128, 512], F32) for i in range(2)]
        sg_b = [Buf(), Buf()]
        ot = [sb("ot%d" % i, [128, 1024], F32) for i in range(2)]
        ot_b = [Buf(), Buf()]
        make_work(C, st, pfx)
        W = C.work
        if fin_g is not None:
            fing = sb("fing", [128, 1024], F32)
            fing_b = Buf()
            S.dma("sp", fing[:], fin_g, pfx + "fing", writes=[fing_b])
            ot2 = [sb("ot2%d" % i, [128, 1024], F32) for i in range(2)]
            ot2_b = [Buf(), Buf()]

        wgv = wg_d.rearrange("(kc p) n -> p kc n", p=128)
        wuv = wu_d.rearrange("(kc p) n -> p kc n", p=128)
        wdv = wd_d.rearrange("(fc p) n -> p fc n", p=128)
        for i in range(2):
            S.dma("pool", wd[:, i * 11:(i + 1) * 11, :], wdv[:, i * 11:(i + 1) * 11, :], pfx + "wd", writes=[wd_b])
        blk = 0
        oi = 0
        for t in range(4):
            tok0 = t * 1024
            for j in range(8):
                S.dma("sp", xres[:, j, :], src[tok0 + j * 128: tok0 + (j + 1) * 128, :], pfx + "x%d" % j,
                      writes=[xres_b[j]])
            for j in range(8):
                norm_transpose(C, st, xres[:, j, :], xres_b[j], gT_ap, xnT, xnT_b, j * 128, pfx)
            for fb in range(11):
                sl = blk % 2
                blk += 1
                S.dma("pool", wg[sl][:], wgv[:, :, fb * 256:(fb + 1) * 256], pfx + "wg%d" % sl, writes=[wg_b[sl]])
                S.dma("pool", wu[sl][:], wuv[:, :, fb * 256:(fb + 1) * 256], pfx + "wu%d" % sl, writes=[wu_b[sl]])
                for fcl in range(2):
                    fc = fb * 2 + fcl
                    for s in range(2):
                        pi = (fc * 2 + s) % 2
                        pg, pu = C.ps[3 + pi], C.ps[5 + pi]

                        def mm(e, wt=wg[sl], po=pg, fcl=fcl, s=s):
                            ins = None
                            for kc in range(8):
                                ins = e.matmul(po[:], lhsT=wt[:, kc, fcl * 128:(fcl + 1) * 128],
                                               rhs=xnT[:, kc, s * 512:(s + 1) * 512],
                                               start=(kc == 0), stop=(kc == 7))
                            return ins
                        S.op("pe", mm, reads=[wg_b[sl], xnT_b], writes=[C.psb[3 + pi]])
                        S.op("pe", lambda e, mm=mm, wt=wu[sl], po=pu: mm(e, wt, po),
                             reads=[wu_b[sl], xnT_b], writes=[C.psb[5 + pi]])
                        S.op("act", lambda e, pg=pg, pi=pi: e.activation(out=sg[pi][:], in_=pg[:], func=AF.Silu),
                             reads=[C.psb[3 + pi]], writes=[sg_b[pi]])
                        S.op("dve", lambda e, pu=pu, pi=pi, fc=fc, s=s: e.tensor_tensor(
                            out=hT[:, fc, s * 512:(s + 1) * 512], in0=pu[:], in1=sg[pi][:], op=ALU.mult),
                             reads=[C.psb[5 + pi], sg_b[pi]], writes=[hT_b[s]])
            for j in range(8):
                o = oi % 2
                oi += 1
                for half in range(2):
                    pb = 0 + half
                    po = C.ps[pb]

                    def mmd(e, po=po, j=j, half=half):
                        ins = None
                        for fc in range(NFC):
                            ins = e.matmul(po[:], lhsT=hT[:, fc, j * 128:(j + 1) * 128],
                                           rhs=wd[:, fc, half * 512:(half + 1) * 512],
                                           start=(fc == 0), stop=(fc == NFC - 1))
                        return ins
                    S.op("pe", mmd, reads=[hT_b[j // 4], wd_b], writes=[C.psb[pb]])
                    S.op("dve", lambda e, po=po, o=o, j=j, half=half: e.scalar_tensor_tensor(
                        out=ot[o][:, half * 512:(half + 1) * 512], in0=po[:], scalar=0.5,
                        in1=xres[:, j, half * 512:(half + 1) * 512], op0=ALU.mult, op1=ALU.add),
                         reads=[C.psb[pb], xres_b[j]], writes=[ot_b[o]])
                if fin_g is None:
                    S.dma("sp", dst[tok0 + j * 128: tok0 + (j + 1) * 128, :], ot[o][:], pfx + "o%d" % o,
                          reads=[ot_b[o]])
                else:
                    S.op("act", lambda e, o=o: e.activation(out=W["junk"][:], in_=ot[o][:], func=AF.Square,
                                                            accum_out=W["ss"][:, 0:1]),
                         reads=[ot_b[o]], writes=[W["junk_b"], W["ss_b"]])
                    S.op("act", lambda e: e.activation(out=W["ss"][:, 1:2], in_=W["ss"][:, 0:1], func=AF.Sqrt,
                                                       bias=C.eps_t[:, 0:1], scale=1.0 / D),
                         reads=[], writes=[W["ss_b"]])
                    S.op("dve", lambda e: e.reciprocal(out=W["ss"][:, 2:3], in_=W["ss"][:, 1:2]),
                         reads=[], writes=[W["ss_b"]])
                    S.op("dve", lambda e, o=o: e.scalar_tensor_tensor(
                        out=ot2[o][:], in0=ot[o][:], scalar=W["ss"][:, 2:3], in1=fing[:],
                        op0=ALU.mult, op1=ALU.mult),
                         reads=[ot_b[o], W["ss_b"], fing_b], writes=[ot2_b[o]])
                    S.dma("sp", dst[tok0 + j * 128: tok0 + (j + 1) * 128, :], ot2[o][:], pfx + "o%d" % o,
                          reads=[ot2_b[o]])
        allb = xres_b + [xnT_b, wd_b] + hT_b + wg_b + wu_b + sg_b + ot_b + [W["junk_b"], W["ss_b"], W["xs_b"]]
        if fin_g is not None:
            allb += ot2_b + [fing_b]
        phase_barrier(C, allb)


def phase_barrier(C, bufs):
    S = C.S
    bufs = list(bufs) + C.psb
    for e in ["sp", "pool", "act", "dve", "pe"]:
        S.wait_all(e, bufs)


def inproj_phase(C, h1, w_in, SC, w):
    nc, S = C.nc, C.S
    pfx = "ip"
    with ExitStack() as st:
        def sb(name, shape, dt):
            return st.enter_context(nc.sbuf_tensor(pfx + name, shape, dt))
        make_work(C, st, pfx)
        W = C.work
        uT = sb("uT", [128, 8, SEQ], BF16)
        uT_b = Buf()
        ht = [sb("ht%d" % i, [128, 1024], F32) for i in range(3)]
        ht_b = [Buf() for _ in range(3)]
        wf = [sb("wf%d" % i, [128, 8, 256], BF16) for i in range(2)]
        wf_b = [Buf(), Buf()]
        wA = sb("wA", [128, 8, 512], BF16)
        wB = sb("wB", [128, 8, 512], BF16)
        wC = sb("wC", [128, 8, 208], BF16)
        wt_b = Buf()
        zc = sb("zc", [128, 3 + SEQ], F32)
        zc_b = Buf()
        acc = sb("acc", [128, SEQ], F32)
        acc_b = Buf()
        fo = [sb("fo%d" % i, [128, SEQ], BF16) for i in range(2)]
        fo_b = [Buf(), Buf()]
        vt = [sb("vt%d" % i, [128, 512], BF16) for i in range(2)]
        vt_b = [Buf(), Buf()]
        og = [sb("og%d" % i, [128, 512], BF16) for i in range(2)]
        og_b = [Buf(), Buf()]
        sm = [sb("sm%d" % i, [128, 16], F32) for i in range(2)]
        sm_b = [Buf(), Buf()]
        ck = [sb("ck%d" % i, [128, 128], BF16) for i in range(2)]
        ck_b = [Buf(), Buf()]
        ki = [sb("ki%d" % i, [128, 64], BF16) for i in range(2)]
        ki_b = [Buf(), Buf()]
        st2 = sb("st2", [128, 8], F32)
        st2_b = Buf()
        ckvT = sb("ckvT", [128, SEQ], BF16)
        ckvT_b = Buf()
        kiT = sb("kiT", [64, SEQ], BF16)
        kiT_b = Buf()
        cw = sb("cw", [128, 8, 4], F32)
        cb = sb("cb", [128, 8], F32)
        kvg = sb("kvg", [128, 128], F32)
        idg = sb("idg", [128, 64], F32)
        gbi = sb("gbi", [128, 8], F32)
        cs_b = Buf()
        for t_, s_ in [(cw, "convw"), (cb, "convb"), (kvg, "kvg_bc"), (idg, "idxg_bc"), (gbi, "gbias_bc")]:
            S.dma("sp", t_[:], w[s_], pfx + "cs", writes=[cs_b])
        S.op("dve", lambda e: e.memset(zc[:, 0:3], 0.0), writes=[zc_b])

        wv = w_in.rearrange("(kc p) n -> p kc n", p=128)
        S.dma("pool", wA[:], wv[:, :, 1024:1536], pfx + "wt", writes=[wt_b])
        S.dma("pool", wB[:], wv[:, :, 1544:2056], pfx + "wt", writes=[wt_b])
        S.dma("pool", wC[:, :, 0:8], wv[:, :, 1536:1544], pfx + "wt", writes=[wt_b])
        S.dma("pool", wC[:, :, 8:136], wv[:, :, 3080:3208], pfx + "wt", writes=[wt_b])
        S.dma("pool", wC[:, :, 136:208], wv[:, :, 3720:3792], pfx + "wt", writes=[wt_b])
        gT_ap = C.gts[:, 1, :]
        fm = [(c * 128, "conv", c) for c in range(8)] + [(2056 + c * 128, "dq", c) for c in range(8)] + \
             [(3208 + c * 128, "iq", c) for c in range(4)]
        blk = 0
        foi = 0
        ti = 0
        for sq in range(2):
            t0 = sq * SEQ
            for j in range(16):
                sl = (sq * 16 + j) % 3
                S.dma("sp", ht[sl][:], h1[t0 + j * 128: t0 + (j + 1) * 128, :], pfx + "h%d" % sl, writes=[ht_b[sl]])
                norm_transpose(C, st, ht[sl][:], ht_b[sl], gT_ap, uT, uT_b, j * 128, pfx)
            for j in range(16):
                o = ti % 2
                ti += 1
                tk = slice(t0 + j * 128, t0 + (j + 1) * 128)
                for wt, n, pb in [(wA, 512, 3), (wB, 512, 4), (wC, 208, 5)]:
                    def mm(e, wt=wt, n=n, pb=pb, j=j):
                        ins = None
                        for kc in range(8):
                            ins = e.matmul(C.ps[pb][:, 0:n], lhsT=uT[:, kc, j * 128:(j + 1) * 128], rhs=wt[:, kc, :],
                                           start=(kc == 0), stop=(kc == 7))
                        return ins
                    S.op("pe", mm, reads=[uT_b, wt_b], writes=[C.psb[pb]])
                S.op("act", lambda e, o=o: e.activation(out=vt[o][:], in_=C.ps[3][:], func=AF.Copy),
                     reads=[C.psb[3]], writes=[vt_b[o]])
                S.dma("sp", SC["v"][tk, :], vt[o][:], pfx + "v%d" % o, reads=[vt_b[o]])
                S.op("act", lambda e, o=o: e.activation(out=og[o][:], in_=C.ps[4][:], func=AF.Sigmoid),
                     reads=[C.psb[4]], writes=[og_b[o]])
                S.dma("sp", SC["og"][tk, :], og[o][:], pfx + "og%d" % o, reads=[og_b[o]])
                pc = C.ps[5]
                S.op("dve", lambda e, o=o: e.tensor_tensor(out=sm[o][:, 0:8], in0=pc[:, 0:8], in1=gbi[:], op=ALU.add),
                     reads=[C.psb[5], cs_b], writes=[sm_b[o]])
                S.op("dve", lambda e, o=o: e.tensor_scalar(out=sm[o][:, 8:16], in0=pc[:, 200:208],
                                                           scalar1=float(8 ** -0.5 * 64 ** -0.5), scalar2=None,
                                                           op0=ALU.mult),
                     reads=[C.psb[5]], writes=[sm_b[o]])
                S.dma("sp", SC["small"][tk, :], sm[o][:], pfx + "sm%d" % o, reads=[sm_b[o]])
                S.op("act", lambda e: e.activation(out=W["junk"][:, 0:128], in_=pc[:, 8:136], func=AF.Square,
                                                   accum_out=st2[:, 0:1]),
                     reads=[C.psb[5]], writes=[W["junk_b"], st2_b])
                S.op("act", lambda e: e.activation(out=W["junk"][:, 128:192], in_=pc[:, 136:200], func=AF.Square,
                                                   accum_out=st2[:, 1:2]),
                     reads=[C.psb[5]], writes=[W["junk_b"], st2_b])
                S.op("act", lambda e: e.activation(out=st2[:, 2:3], in_=st2[:, 0:1], func=AF.Sqrt,
                                                   bias=C.eps_t[:, 0:1], scale=1.0 / 128), writes=[st2_b])
                S.op("act", lambda e: e.activation(out=st2[:, 3:4], in_=st2[:, 1:2], func=AF.Sqrt,
                                                   bias=C.eps_t[:, 0:1], scale=1.0 / 64), writes=[st2_b])
                S.op("dve", lambda e: e.reciprocal(out=st2[:, 4:6], in_=st2[:, 2:4]), writes=[st2_b])
                S.op("dve", lambda e, o=o: e.scalar_tensor_tensor(out=ck[o][:], in0=pc[:, 8:136], scalar=st2[:, 4:5],
                                                                   in1=kvg[:], op0=ALU.mult, op1=ALU.mult),
                     reads=[C.psb[5], st2_b, cs_b], writes=[ck_b[o]])
                S.op("dve", lambda e, o=o: e.scalar_tensor_tensor(out=ki[o][:], in0=pc[:, 136:200], scalar=st2[:, 5:6],
                                                                   in1=idg[:], op0=ALU.mult, op1=ALU.mult),
                     reads=[C.psb[5], st2_b, cs_b], writes=[ki_b[o]])
                S.dma("sp", SC["ckv"][tk, :], ck[o][:], pfx + "ck%d" % o, reads=[ck_b[o]])
                tp = C.psv_bf[6]

                def tr(e, o=o):
                    e.transpose(out=tp[:, 0, :], in_=ck[o][:], identity=C.ident[:])
                    return e.transpose(out=tp[0:64, 1, :], in_=ki[o][:], identity=C.ident[:])
                S.op("pe", tr, reads=[ck_b[o], ki_b[o]], writes=[C.psb[6]])
                S.op("act", lambda e, j=j: e.activation(out=ckvT[:, j * 128:(j + 1) * 128], in_=tp[:, 0, :], func=AF.Copy),
                     reads=[C.psb[6]], writes=[ckvT_b])
                S.op("act", lambda e, j=j: e.activation(out=kiT[:, j * 128:(j + 1) * 128], in_=tp[0:64, 1, :], func=AF.Copy),
                     reads=[C.psb[6]], writes=[kiT_b])
            S.dma("sp", SC["ckvT"][:, t0:t0 + SEQ], ckvT[:], pfx + "ckT", reads=[ckvT_b])
            S.dma("sp", SC["kiT"][:, t0:t0 + SEQ], kiT[:], pfx + "kiT", reads=[kiT_b])
            for ci, (c0, kind, dch) in enumerate(fm):
                if ci % 2 == 0:
                    sl = blk % 2
                    blk += 1
                    S.dma("pool", wf[sl][:], wv[:, :, c0:c0 + 256], pfx + "wf%d" % sl, writes=[wf_b[sl]])
                f = foi % 2
                foi += 1
                for s in range(4):
                    pb = s % 2

                    def mm(e, sl=sl, ci=ci, s=s, pb=pb):
                        ins = None
                        for kc in range(8):
                            ins = e.matmul(C.ps[pb][:], lhsT=wf[sl][:, kc, (ci % 2) * 128:(ci % 2 + 1) * 128],
                                           rhs=uT[:, kc, s * 512:(s + 1) * 512], start=(kc == 0), stop=(kc == 7))
                        return ins
                    S.op("pe", mm, reads=[wf_b[sl], uT_b], writes=[C.psb[pb]])
                    if kind == "conv":
                        S.op("act", lambda e, pb=pb, s=s: e.activation(out=zc[:, 3 + s * 512: 3 + (s + 1) * 512],
                                                                     in_=C.ps[pb][:], func=AF.Copy),
                             reads=[C.psb[pb]], writes=[zc_b])
                    elif kind == "dq":
                        S.op("act", lambda e, pb=pb, s=s, f=f: e.activation(out=fo[f][:, s * 512:(s + 1) * 512],
                                                                          in_=C.ps[pb][:], func=AF.Copy,
                                                                          scale=float(128 ** -0.5)),
                             reads=[C.psb[pb]], writes=[fo_b[f]])
                    else:
                        S.op("act", lambda e, pb=pb, s=s, f=f: e.activation(out=fo[f][:, s * 512:(s + 1) * 512],
                                                                          in_=C.ps[pb][:], func=AF.Copy),
                             reads=[C.psb[pb]], writes=[fo_b[f]])
                if kind == "conv":
                    S.op("dve", lambda e, dch=dch: e.tensor_scalar(out=acc[:], in0=zc[:, 0:SEQ], scalar1=cw[:, dch, 0:1],
                                                                    scalar2=cb[:, dch:dch + 1], op0=ALU.mult, op1=ALU.add),
                         reads=[zc_b, cs_b], writes=[acc_b])
                    for jj in range(1, 4):
                        S.op("dve", lambda e, dch=dch, jj=jj: e.scalar_tensor_tensor(
                            out=acc[:], in0=zc[:, jj:jj + SEQ], scalar=cw[:, dch, jj:jj + 1], in1=acc[:],
                            op0=ALU.mult, op1=ALU.add), reads=[zc_b, cs_b], writes=[acc_b])
                    S.op("act", lambda e, f=f: e.activation(out=fo[f][:], in_=acc[:], func=AF.Silu),
                         reads=[acc_b], writes=[fo_b[f]])
                    dstd = SC["qkT"][dch, :, t0:t0 + SEQ]
                elif kind == "dq":
                    dstd = SC["dqT"][dch, :, t0:t0 + SEQ]
                else:
                    dstd = SC["iqT"][dch, :, t0:t0 + SEQ]
                S.dma("sp", dstd, fo[f][:], pfx + "fo%d" % f, reads=[fo_b[f]])
        allb = [uT_b, wt_b, zc_b, acc_b, st2_b, ckvT_b, kiT_b, cs_b, W["junk_b"], W["ss_b"], W["xs_b"]] + ht_b + wf_b + \
            fo_b + vt_b + og_b + sm_b + ck_b + ki_b
        phase_barrier(C, allb)


def mlstm_phase(C, SC, w):
    nc, S = C.nc, C.S
    pfx = "ml"
    SC_ = float(128 ** -0.5)
    with ExitStack() as st:
        def sb(name, shape, dt):
            return st.enter_context(nc.sbuf_tensor(pfx + name, shape, dt))
        qT = sb("qT", [128, 4, SEQ], BF16)
        kT = sb("kT", [128, 4, SEQ], BF16)
        v = sb("v", [128, 16, 512], BF16)
        G = sb("G", [128, 16, 512], BF16)
        smt = sb("smt", [128, 16, 16], F32)
        ld_b = Buf()
        hg = sb("hg", [128, 512], F32)
        triu = sb("triu", [128, 128], F32)
        onesf = sb("onesf", [128, 128], F32)
        masku = sb("masku", [128, 128], F32)
        cs_b = Buf()
        ktm = sb("ktm", [128, 16, 512], BF16)
        ktm_b = Buf()
        lf = sb("lf", [128, 64], F32)
        ebp = sb("ebp", [128, 64], F32)
        eg = sb("eg", [128, 64], F32)
        ebL = sb("ebL", [128, 64], F32)
        pre_b = Buf()
        vaug = [sb("vaug%d" % i, [128, 4, 130], BF16) for i in range(2)]
        vaug_b = [Buf(), Buf()]
        Pm = [sb("Pm%d" % i, [128, 4, 128], BF16) for i in range(2)]
        Pm_b = [Buf(), Buf()]
        C32 = [sb("C32_%d" % h, [128, 130], F32) for h in range(4)]
        C32_b = [Buf() for _ in range(4)]
        Cbf = [sb("Cbf_%d" % h, [128, 130], BF16) for h in range(4)]
        Cbf_b = [Buf() for _ in range(4)]
        t4 = sb("t4", [128, 16], F32)
        t4_b = Buf()
        hv = sb("hv", [128, 4, 128], F32)
        hv_b = Buf()
        sq = sb("sq", [128, 4, 128], F32)
        sq_b = Buf()
        yb = [sb("yb%d" % i, [128, 512], BF16) for i in range(2)]
        yb_b = [Buf(), Buf()]
        ymT = sb("ymT", [128, 4, SEQ], BF16)
        ymT_b = Buf()
        S.dma("sp", hg[:], w["hg_bc"], pfx + "cs", writes=[cs_b])
        S.dma("sp", triu[:], w["c_triu"], pfx + "cs", writes=[cs_b])
        S.dma("sp", onesf[:], w["c_ones"], pfx + "cs", writes=[cs_b])
        S.dma("sp", masku[:], w["c_triu"], pfx + "cs", writes=[cs_b])
        vi = 0
        for sq_i in range(2):
            t0 = sq_i * SEQ
            for h in range(4):
                S.dma("sp", qT[:, h, :], SC["qkT"][h, :, t0:t0 + SEQ], pfx + "ld", writes=[ld_b])
                S.dma("sp", kT[:, h, :], SC["qkT"][4 + h, :, t0:t0 + SEQ], pfx + "ld", writes=[ld_b])
            S.dma("sp", v[:], SC["v"][t0:t0 + SEQ, :].rearrange("(c p) f -> p c f", p=128), pfx + "ld", writes=[ld_b])
            S.dma("sp", G[:], SC["og"][t0:t0 + SEQ, :].rearrange("(c p) f -> p c f", p=128), pfx + "ld", writes=[ld_b])
            S.dma("sp", smt[:], SC["small"][t0:t0 + SEQ, :].rearrange("(c p) f -> p c f", p=128), pfx + "ld",
                  writes=[ld_b])
            lf3 = lf[:].rearrange("p (c h) -> p c h", h=4)
            S.op("act", lambda e: e.activation(out=lf3, in_=smt[:, :, 4:8], func=AF.Exp, scale=-1.0),
                 reads=[ld_b], writes=[pre_b])
            S.op("dve", lambda e: e.tensor_scalar(out=lf[:], in0=lf[:], scalar1=1.0, scalar2=None, op0=ALU.add),
                 writes=[pre_b])
            S.op("act", lambda e: e.activation(out=lf[:], in_=lf[:], func=AF.Ln), writes=[pre_b])
            S.op("dve", lambda e: e.tensor_scalar(out=lf[:], in0=lf[:], scalar1=-1.0, scalar2=None, op0=ALU.mult),
                 writes=[pre_b])
            pbb, pbl = C.ps[6], C.ps[7]
            S.op("pe", lambda e: e.matmul(pbb[:, 0:64], lhsT=triu[:], rhs=lf[:], start=True, stop=True),
                 reads=[pre_b, cs_b], writes=[C.psb[6]])
            S.op("pe", lambda e: e.matmul(pbl[:, 0:64], lhsT=onesf[:], rhs=lf[:], start=True, stop=True),
                 reads=[pre_b, cs_b], writes=[C.psb[7]])
            S.op("act", lambda e: e.activation(out=ebp[:], in_=pbb[:, 0:64], func=AF.Exp),
                 reads=[C.psb[6]], writes=[pre_b])
            S.op("dve", lambda e: e.tensor_scalar(out=ebp[:], in0=ebp[:], scalar1=SC_, scalar2=None, op0=ALU.mult),
                 writes=[pre_b])
            eg3 = eg[:].rearrange("p (c h) -> p c h", h=4)
            S.op("dve", lambda e: e.tensor_tensor(out=eg3, in0=smt[:, :, 0:4],
                                                  in1=pbb[:, 0:64].rearrange("p (c h) -> p c h", h=4), op=ALU.subtract),
                 reads=[ld_b, C.psb[6]], writes=[pre_b])
            S.op("act", lambda e: e.activation(out=eg[:], in_=eg[:], func=AF.Exp), writes=[pre_b])
            S.op("act", lambda e: e.activation(out=ebL[:], in_=pbl[:, 0:64], func=AF.Exp),
                 reads=[C.psb[7]], writes=[pre_b])
            S.op("dve", lambda e: e.tensor_tensor(out=G[:], in0=G[:], in1=hg[:].unsqueeze(1).to_broadcast([128, 16, 512]),
                                                  op=ALU.mult), reads=[cs_b], writes=[ld_b])
            tp = C.psv_bf[5]
            for c in range(16):
                def trk(e, c=c):
                    ins = None
                    for h in range(4):
                        ins = e.transpose(out=tp[:, h, :], in_=kT[:, h, c * 128:(c + 1) * 128], identity=C.ident[:])
                    return ins
                S.op("pe", trk, reads=[ld_b], writes=[C.psb[5]])
                S.op("act", lambda e, c=c: e.activation(out=ktm[:, c, :], in_=tp[:, 0:4, :].rearrange("p a b -> p (a b)"),
                                                        func=AF.Copy),
                     reads=[C.psb[5]], writes=[ktm_b])
            for c in range(16):
                cs4 = slice(c * 4, (c + 1) * 4)
                tsl = slice(c * 128, (c + 1) * 128)
                vb = vi % 2
                vi += 1
                S.op("dve", lambda e, vb=vb, c=c, cs4=cs4: e.tensor_tensor(
                    out=vaug[vb][:, :, 0:128], in0=v[:, c, :].rearrange("p (h d) -> p h d", d=128),
                    in1=bc_mid(eg[:, cs4], 128), op=ALU.mult), reads=[ld_b, pre_b], writes=[vaug_b[vb]])
                S.op("dve", lambda e, vb=vb, cs4=cs4: e.tensor_copy(out=vaug[vb][:, :, 128:129],
                                                                     in_=eg[:, cs4].unsqueeze(2)),
                     reads=[pre_b], writes=[vaug_b[vb]])
                pa = C.ps[0]

                def mst(e, tsl=tsl):
                    ins = None
                    for h in range(4):
                        ins = e.matmul(pa[:, h * 128:(h + 1) * 128], lhsT=kT[:, h, tsl], rhs=qT[:, h, tsl],
                                       start=True, stop=True)
                    return ins
                S.op("pe", mst, reads=[ld_b], writes=[C.psb[0]])
                S.op("dve", lambda e, vb=vb: e.tensor_tensor(
                    out=Pm[vb][:], in0=pa[:].rearrange("p (h j) -> p h j", j=128),
                    in1=masku[:].unsqueeze(1).to_broadcast([128, 4, 128]), op=ALU.mult),
                     reads=[C.psb[0], cs_b], writes=[Pm_b[vb]])
                for k2 in range(2):
                    def mr(e, k2=k2, vb=vb, c=c, tsl=tsl):
                        ins = None
                        for hh in range(2):
                            h = k2 * 2 + hh
                            o_ = C.ps[1 + k2][:, hh * 256: hh * 256 + 129]
                            if c > 0:
                                e.matmul(o_, lhsT=qT[:, h, tsl], rhs=Cbf[h][:, 0:129], start=True, stop=False)
                            ins = e.matmul(o_, lhsT=Pm[vb][:, h, :], rhs=vaug[vb][:, h, 0:129], start=(c == 0), stop=True)
                        return ins
                    S.op("pe", mr, reads=[ld_b, Pm_b[vb], vaug_b[vb]] + [Cbf_b[k2 * 2], Cbf_b[k2 * 2 + 1]],
                         writes=[C.psb[1 + k2]])
                for k2 in range(2):
                    def mu(e, k2=k2, vb=vb, c=c):
                        ins = None
                        for hh in range(2):
                            h = k2 * 2 + hh
                            ins = e.matmul(C.ps[3 + k2][:, hh * 256: hh * 256 + 129],
                                           lhsT=ktm[:, c, h * 128:(h + 1) * 128], rhs=vaug[vb][:, h, 0:129],
                                           start=True, stop=True)
                        return ins
                    S.op("pe", mu, reads=[ktm_b, vaug_b[vb]], writes=[C.psb[3 + k2]])
                R3 = [C.ps[1 + k2][:].rearrange("p (a b) -> p a b", b=256) for k2 in range(2)]
                U3 = [C.ps[3 + k2][:].rearrange("p (a b) -> p a b", b=256) for k2 in range(2)]
                for k2 in range(2):
                    S.op("dve", lambda e, k2=k2, c=c: e.tensor_tensor(
                        out=t4[:, 2 * k2:2 * k2 + 2].unsqueeze(2), in0=R3[k2][:, :, 128:129],
                        in1=ebp[:, c * 4 + 2 * k2: c * 4 + 2 * k2 + 2].unsqueeze(2), op=ALU.mult),
                         reads=[C.psb[1 + k2], pre_b], writes=[t4_b])
                S.op("dve", lambda e: e.tensor_scalar(out=t4[:, 4:8], in0=t4[:, 0:4], scalar1=-1.0, scalar2=None,
                                                      op0=ALU.mult), writes=[t4_b])
                S.op("dve", lambda e: e.tensor_tensor(out=t4[:, 4:8], in0=t4[:, 4:8], in1=t4[:, 0:4], op=ALU.max),
                     writes=[t4_b])
                S.op("dve", lambda e: e.tensor_scalar(out=t4[:, 4:8], in0=t4[:, 4:8], scalar1=1.0, scalar2=None,
                                                      op0=ALU.max), writes=[t4_b])
                S.op("dve", lambda e: e.reciprocal(out=t4[:, 8:12], in_=t4[:, 4:8]), writes=[t4_b])
                S.op("dve", lambda e, cs4=cs4: e.tensor_tensor(out=t4[:, 12:16], in0=t4[:, 8:12], in1=ebp[:, cs4],
                                                               op=ALU.mult), reads=[pre_b], writes=[t4_b])
                for k2 in range(2):
                    S.op("dve", lambda e, k2=k2: e.tensor_tensor(
                        out=hv[:, 2 * k2:2 * k2 + 2, :], in0=R3[k2][:, :, 0:128],
                        in1=bc_mid(t4[:, 12 + 2 * k2: 14 + 2 * k2], 128), op=ALU.mult),
                         reads=[C.psb[1 + k2], t4_b], writes=[hv_b])
                S.op("pool", lambda e: e.tensor_tensor(out=sq[:], in0=hv[:], in1=hv[:], op=ALU.mult),
                     reads=[hv_b], writes=[sq_b])
                S.op("dve", lambda e: e.tensor_reduce(out=t4[:, 0:4], in_=sq[:], axis=AX.X, op=ALU.add),
                     reads=[sq_b], writes=[t4_b])
                S.op("act", lambda e: e.activation(out=t4[:, 4:8], in_=t4[:, 0:4], func=AF.Sqrt,
                                                   bias=C.eps_t[:, 0:1], scale=1.0 / 128), writes=[t4_b])
                S.op("dve", lambda e: e.reciprocal(out=t4[:, 8:12], in_=t4[:, 4:8]), writes=[t4_b])
                S.op("dve", lambda e: e.tensor_tensor(out=hv[:], in0=hv[:], in1=bc_mid(t4[:, 8:12], 128), op=ALU.mult),
                     reads=[t4_b, sq_b], writes=[hv_b])
                S.op("dve", lambda e, vb=vb, c=c: e.tensor_tensor(out=yb[vb][:], in0=hv[:].rearrange("p a b -> p (a b)"),
                                                                   in1=G[:, c, :], op=ALU.mult),
                     reads=[hv_b, ld_b], writes=[yb_b[vb]])

                def try_(e, vb=vb):
                    ins = None
                    for h in range(4):
                        ins = e.transpose(out=tp[:, h, :], in_=yb[vb][:, h * 128:(h + 1) * 128], identity=C.ident[:])
                    return ins
                S.op("pe", try_, reads=[yb_b[vb]], writes=[C.psb[5]])
                S.op("act", lambda e, tsl=tsl: e.activation(out=ymT[:, :, tsl], in_=tp[:, 0:4, :], func=AF.Copy),
                     reads=[C.psb[5]], writes=[ymT_b])
                for h in range(4):
                    k2, hh = h // 2, h % 2
                    if c == 0:
                        S.op("act", lambda e, h=h, k2=k2, hh=hh: e.activation(out=C32[h][:, 0:129],
                                                                              in_=U3[k2][:, hh, 0:129], func=AF.Copy),
                             reads=[C.psb[3 + k2]], writes=[C32_b[h]])
                    else:
                        S.op("dve", lambda e, h=h, k2=k2, hh=hh, c=c: e.scalar_tensor_tensor(
                            out=C32[h][:, 0:129], in0=C32[h][:, 0:129], scalar=ebL[:, (c - 1) * 4 + h:(c - 1) * 4 + h + 1],
                            in1=U3[k2][:, hh, 0:129], op0=ALU.mult, op1=ALU.add),
                             reads=[C.psb[3 + k2], pre_b], writes=[C32_b[h]])
                    if c < 15:
                        S.op("act", lambda e, h=h, c=c: e.activation(out=Cbf[h][:, 0:129], in_=C32[h][:, 0:129],
                                                                     func=AF.Copy, scale=ebL[:, c * 4 + h: c * 4 + h + 1]),
                             reads=[C32_b[h], pre_b], writes=[Cbf_b[h]])
            for h in range(4):
                S.dma("sp", SC["ymT"][h, :, t0:t0 + SEQ], ymT[:, h, :], pfx + "st", reads=[ymT_b])
        allb = [ld_b, cs_b, ktm_b, pre_b, t4_b, hv_b, sq_b, ymT_b] + vaug_b + Pm_b + C32_b + Cbf_b + yb_b
        phase_barrier(C, allb)


def dsa_phase(C, SC, w):
    nc, S = C.nc, C.S
    pfx = "ds"
    with ExitStack() as st:
        def sb(name, shape, dt):
            return st.enter_context(nc.sbuf_tensor(pfx + name, shape, dt))
        dqT = sb("dqT", [128, 8, SEQ], BF16)
        iqT = sb("iqT", [128, 4, SEQ], BF16)
        kiT2 = sb("kiT2", [128, SEQ], BF16)
        ckv = sb("ckv", [128, 16, 128], BF16)
        ckvT = sb("ckvT", [128, SEQ], BF16)
        smt = sb("smt", [128, 16, 16], F32)
        ld_b = Buf()
        wuv = sb("wuv", [128, 8, 64], BF16)
        cneg = sb("cneg", [128, 128], F32)
        onesf = sb("onesf", [128, 128], F32)
        onesb = sb("onesb", [128, 128], BF16)
        cs_b = Buf()
        score = sb("score", [128, SEQ], F32)
        score_b = Buf()
        work = sb("work", [128, SEQ], F32)
        work_b = Buf()
        m8 = sb("m8", [128, 8], F32)
        m8_b = Buf()
        rl = [sb("rl%d" % i, [128, 512], F32) for i in range(2)]
        rl_b = [Buf(), Buf()]
        sel = sb("sel", [128, SEQ], BF16)
        sel_b = Buf()
        negT = sb("negT", [128, 16, 128], BF16)
        negT_b = Buf()
        pT = [sb("pT%d" % i, [128, 512], BF16) for i in range(2)]
        pT_b = [Buf(), Buf()]
        rec = sb("rec", [128, 512], F32)
        rec_b = Buf()
        oTn = sb("oTn", [128, 8, 128], BF16)
        oTn_b = Buf()
        ydT = sb("ydT", [128, 4, SEQ], BF16)
        ydT_b = Buf()
        S.dma("pool", wuv[:], w["wuv"], pfx + "cs", writes=[cs_b])
        S.dma("sp", cneg[:], w["c_cneg"], pfx + "cs", writes=[cs_b])
        S.dma("sp", onesf[:], w["c_ones"], pfx + "cs", writes=[cs_b])
        S.op("dve", lambda e: e.tensor_copy(out=onesb[:], in_=onesf[:]), reads=[cs_b], writes=[cs_b])
        pti = 0
        lti = 0
        for sq_i in range(2):
            t0 = sq_i * SEQ
            for h in range(8):
                S.dma("sp", dqT[:, h, :], SC["dqT"][h, :, t0:t0 + SEQ], pfx + "ld", writes=[ld_b])
            for c in range(4):
                S.dma("sp", iqT[:, c, :], SC["iqT"][c, :, t0:t0 + SEQ], pfx + "ld", writes=[ld_b])
            S.dma("sp", kiT2[0:64, :], SC["kiT"][:, t0:t0 + SEQ], pfx + "ld", writes=[ld_b])
            S.dma("sp", kiT2[64:128, :], SC["kiT"][:, t0:t0 + SEQ], pfx + "ld", writes=[ld_b])
            S.dma("sp", ckvT[:], SC["ckvT"][:, t0:t0 + SEQ], pfx + "ld", writes=[ld_b])
            S.dma("sp", ckv[:], SC["ckv"][t0:t0 + SEQ, :].rearrange("(c p) f -> p c f", p=128), pfx + "ld", writes=[ld_b])
            S.dma("sp", smt[:], SC["small"][t0:t0 + SEQ, :].rearrange("(c p) f -> p c f", p=128), pfx + "ld",
                  writes=[ld_b])
            for bi in range(16):
                nk = (bi + 1) * 128
                nkc = bi + 1
                tsl = slice(bi * 128, (bi + 1) * 128)
                for ks in range((nk + 511) // 512):
                    wd_ = min(512, nk - ks * 512)
                    ksl = slice(ks * 512, ks * 512 + wd_)
                    for h in range(8):
                        pb = h % 2
                        p0 = (h % 2) * 64
                        S.op("pe", lambda e, pb=pb, p0=p0, h=h, ksl=ksl, wd_=wd_, tsl=tsl: e.matmul(
                            C.ps[pb][:, 0:wd_], lhsT=iqT[p0:p0 + 64, h // 2, tsl], rhs=kiT2[p0:p0 + 64, ksl],
                            start=True, stop=True), reads=[ld_b], writes=[C.psb[pb]])
                        S.op("act", lambda e, pb=pb, wd_=wd_: e.activation(out=rl[pb][:, 0:wd_], in_=C.ps[pb][:, 0:wd_],
                                                                          func=AF.Relu),
                             reads=[C.psb[pb]], writes=[rl_b[pb]])
                        wcol = smt[:, bi, 8 + h: 9 + h]
                        if h == 0:
                            S.op("dve", lambda e, pb=pb, wd_=wd_, ksl=ksl, wcol=wcol: e.tensor_scalar(
                                out=score[:, ksl], in0=rl[pb][:, 0:wd_], scalar1=wcol, scalar2=None, op0=ALU.mult),
                                 reads=[rl_b[pb], ld_b], writes=[score_b])
                        else:
                            S.op("dve", lambda e, pb=pb, wd_=wd_, ksl=ksl, wcol=wcol: e.scalar_tensor_tensor(
                                out=score[:, ksl], in0=rl[pb][:, 0:wd_], scalar=wcol, in1=score[:, ksl],
                                op0=ALU.mult, op1=ALU.add), reads=[rl_b[pb], ld_b], writes=[score_b])
                S.op("dve", lambda e, tsl=tsl: e.tensor_tensor(out=score[:, tsl], in0=score[:, tsl], in1=cneg[:],
                                                               op=ALU.add), reads=[cs_b], writes=[score_b])
                if bi >= 2:
                    for r in range(32):
                        src_ = score if r == 0 else work
                        S.op("dve", lambda e, src_=src_, nk=nk: e.max(out=m8[:], in_=src_[:, 0:nk]),
                             reads=[score_b, work_b], writes=[m8_b])
                        if r < 31:
                            S.op("dve", lambda e, src_=src_, nk=nk: e.match_replace(
                                out=work[:, 0:nk], in_to_replace=m8[:], in_values=src_[:, 0:nk], imm_value=NEG),
                                 reads=[score_b, m8_b], writes=[work_b])
                    S.op("dve", lambda e, nk=nk: e.tensor_scalar(out=sel[:, 0:nk], in0=score[:, 0:nk],
                                                                 scalar1=m8[:, 7:8], scalar2=None, op0=ALU.is_ge),
                         reads=[score_b, m8_b], writes=[sel_b])
                else:
                    S.op("dve", lambda e, nk=nk: e.tensor_scalar(out=sel[:, 0:nk], in0=score[:, 0:nk],
                                                                 scalar1=-1.0e29, scalar2=None, op0=ALU.is_ge),
                         reads=[score_b], writes=[sel_b])
                tp = C.psv_bf[2]
                for k0 in range(0, nkc, 8):
                    n_ = min(8, nkc - k0)

                    def trs(e, k0=k0, n_=n_):
                        ins = None
                        for i in range(n_):
                            ins = e.transpose(out=tp[:, i, :], in_=sel[:, (k0 + i) * 128:(k0 + i + 1) * 128],
                                              identity=C.ident[:])
                        return ins
                    S.op("pe", trs, reads=[sel_b], writes=[C.psb[2]])
                    S.op("dve", lambda e, k0=k0, n_=n_: e.tensor_scalar(
                        out=negT[:, k0:k0 + n_, :], in0=tp[:, 0:n_, :], scalar1=-1.0, scalar2=30000.0,
                        op0=ALU.add, op1=ALU.mult), reads=[C.psb[2]], writes=[negT_b])
                for g in range(2):
                    ob = 5 + g
                    for kc in range(nkc):
                        lb = 3 + (lti % 2)
                        lti += 1
                        ps_ = pti % 2
                        pti += 1
                        ksl = slice(kc * 128, (kc + 1) * 128)

                        def mlt(e, lb=lb, ksl=ksl, g=g, kc=kc, tsl=tsl):
                            o3 = C.ps[lb][:].rearrange("p (a b) -> p a b", b=128)
                            e.matmul(o3, lhsT=ckvT[:, ksl], rhs=dqT[:, 4 * g:4 * g + 4, tsl], start=True, stop=False)
                            return e.matmul(o3, lhsT=C.ident[:],
                                            rhs=negT[:, kc, :].unsqueeze(1).to_broadcast([128, 4, 128]),
                                            start=False, stop=True)
                        S.op("pe", mlt, reads=[ld_b, negT_b], writes=[C.psb[lb]])
                        S.op("act", lambda e, lb=lb, ps_=ps_: e.activation(out=pT[ps_][:], in_=C.ps[lb][:], func=AF.Exp),
                             reads=[C.psb[lb]], writes=[pT_b[ps_]])

                        def mpv(e, ob=ob, kc=kc, ps_=ps_, nkc=nkc):
                            e.matmul(C.ps[ob][:], lhsT=ckv[:, kc, :], rhs=pT[ps_][:], start=(kc == 0),
                                     stop=(kc == nkc - 1))
                            return e.matmul(C.ps[7][:], lhsT=onesb[:], rhs=pT[ps_][:], start=(kc == 0),
                                            stop=(kc == nkc - 1))
                        S.op("pe", mpv, reads=[ld_b, cs_b, pT_b[ps_]], writes=[C.psb[ob], C.psb[7]])
                    S.op("dve", lambda e: e.reciprocal(out=rec[:], in_=C.ps[7][:]), reads=[C.psb[7]], writes=[rec_b])
                    S.op("dve", lambda e, ob=ob, g=g: e.tensor_tensor(
                        out=oTn[:, 4 * g:4 * g + 4, :].rearrange("p a b -> p (a b)"), in0=C.ps[ob][:], in1=rec[:],
                        op=ALU.mult), reads=[C.psb[ob], rec_b], writes=[oTn_b])
                def mup(e):
                    ins = None
                    for h in range(8):
                        p0 = (h % 2) * 64
                        ins = e.matmul(C.ps[2][p0:p0 + 64, (h // 2) * 128:(h // 2 + 1) * 128], lhsT=wuv[:, h, :],
                                       rhs=oTn[:, h, :], start=True, stop=True)
                    return ins
                S.op("pe", mup, reads=[oTn_b, cs_b], writes=[C.psb[2]])
                S.op("act", lambda e, tsl=tsl: e.activation(out=ydT[:, :, tsl],
                                                            in_=C.ps[2][:].rearrange("p (a b) -> p a b", b=128),
                                                            func=AF.Copy), reads=[C.psb[2]], writes=[ydT_b])
            for c in range(4):
                S.dma("sp", SC["ydT"][c, :, t0:t0 + SEQ], ydT[:, c, :], pfx + "st", reads=[ydT_b])
        allb = [ld_b, cs_b, score_b, work_b, m8_b, sel_b, negT_b, rec_b, oTn_b, ydT_b] + rl_b + pT_b
        phase_barrier(C, allb)


def post_phase(C, h1, mem, h3, SC, w):
    nc, S = C.nc, C.S
    pfx = "po"
    with ExitStack() as st:
        def sb(name, shape, dt):
            return st.enter_context(nc.sbuf_tensor(pfx + name, shape, dt))
        make_work(C, st, pfx)
        wout = sb("wout", [128, 8, 1024], BF16)
        wq = sb("wq", [128, 8, 1024], BF16)
        wo = sb("wo", [128, 8, 1024], BF16)
        wr_b = Buf()
        wkv = [sb("wkv%d" % i, [128, 8, 512], BF16) for i in range(2)]
        wkv_b = [Buf(), Buf()]
        onesf = sb("onesf", [128, 128], F32)
        onesb = sb("onesb", [128, 128], BF16)
        cs_b = Buf()
        mt = [sb("mt%d" % i, [128, 1024], F32) for i in range(2)]
        mt_b = [Buf(), Buf()]
        memT = sb("memT", [128, 8, 256], BF16)
        memT_b = Buf()
        kTx = sb("kTx", [128, 8, 256], BF16)
        kTx_b = Buf()
        vx = sb("vx", [128, 2, 1024], BF16)
        vx_b = Buf()
        h2t = sb("h2t", [128, 4, 1024], F32)
        h2t_b = [Buf() for _ in range(4)]
        ycat = sb("ycat", [128, 8, 512], BF16)
        ycat_b = Buf()
        u3T = sb("u3T", [128, 8, 512], BF16)
        u3T_b = Buf()
        qTx = sb("qTx", [128, 8, 512], BF16)
        qTx_b = Buf()
        pTx = [sb("pTx%d" % i, [128, 512], BF16) for i in range(2)]
        pTx_b = [Buf(), Buf()]
        rec = sb("rec", [128, 512], F32)
        rec_b = Buf()
        oTx = sb("oTx", [128, 8, 512], BF16)
        oTx_b = Buf()
        ot = [sb("ot%d" % i, [128, 1024], F32) for i in range(2)]
        ot_b = [Buf(), Buf()]
        vw = lambda a: a.rearrange("(kc p) n -> p kc n", p=128)
        S.dma("pool", wout[:], vw(w["w_out"]), pfx + "wr", writes=[wr_b])
        S.dma("pool", wq[:], vw(w["xattn_w_q"]), pfx + "wr", writes=[wr_b])
        S.dma("pool", wo[:], vw(w["xattn_w_o"]), pfx + "wr", writes=[wr_b])
        S.dma("sp", onesf[:], w["c_ones"], pfx + "cs", writes=[cs_b])
        S.op("dve", lambda e: e.tensor_copy(out=onesb[:], in_=onesf[:]), reads=[cs_b], writes=[cs_b])
        wkvv = vw(w["xattn_w_kv"])
        oi = 0
        for sq_i in range(2):
            t0 = sq_i * SEQ
            for m in range(2):
                S.dma("sp", mt[m][:], mem[sq_i * 256 + m * 128: sq_i * 256 + (m + 1) * 128, :], pfx + "mt%d" % m,
                      writes=[mt_b[m]])
                norm_transpose(C, st, mt[m][:], mt_b[m], C.gts[:, 3, :], memT, memT_b, m * 128, pfx)
            for piece in range(4):
                sl = piece % 2
                S.dma("pool", wkv[sl][:], wkvv[:, :, piece * 512:(piece + 1) * 512], pfx + "wkv%d" % sl,
                      writes=[wkv_b[sl]])
                if piece < 2:
                    for c4 in range(4):
                        ch = piece * 4 + c4
                        pb = ch % 2

                        def mk(e, sl=sl, c4=c4, pb=pb):
                            ins = None
                            for kc in range(8):
                                ins = e.matmul(C.ps[pb][:, 0:256], lhsT=wkv[sl][:, kc, c4 * 128:(c4 + 1) * 128],
                                               rhs=memT[:, kc, :], start=(kc == 0), stop=(kc == 7))
                            return ins
                        S.op("pe", mk, reads=[wkv_b[sl], memT_b], writes=[C.psb[pb]])
                        S.op("act", lambda e, ch=ch, pb=pb: e.activation(out=kTx[:, ch, :], in_=C.ps[pb][:, 0:256],
                                                                         func=AF.Copy, scale=float(256 ** -0.5)),
                             reads=[C.psb[pb]], writes=[kTx_b])
                else:
                    half = piece - 2
                    for mc in range(2):
                        pb = mc

                        def mv_(e, sl=sl, mc=mc, pb=pb):
                            ins = None
                            for kc in range(8):
                                ins = e.matmul(C.ps[pb][:], lhsT=memT[:, kc, mc * 128:(mc + 1) * 128],
                                               rhs=wkv[sl][:, kc, :], start=(kc == 0), stop=(kc == 7))
                            return ins
                        S.op("pe", mv_, reads=[wkv_b[sl], memT_b], writes=[C.psb[pb]])
                        S.op("act", lambda e, mc=mc, pb=pb, half=half: e.activation(
                            out=vx[:, mc, half * 512:(half + 1) * 512], in_=C.ps[pb][:], func=AF.Copy),
                             reads=[C.psb[pb]], writes=[vx_b])
            for s4 in range(4):
                ts0 = t0 + s4 * 512
                for c in range(4):
                    S.dma("sp", ycat[:, c, :], SC["ymT"][c, :, ts0:ts0 + 512], pfx + "yc", writes=[ycat_b])
                    S.dma("sp", ycat[:, 4 + c, :], SC["ydT"][c, :, ts0:ts0 + 512], pfx + "yc", writes=[ycat_b])
                for j in range(4):
                    S.dma("sp", h2t[:, j, :], h1[ts0 + j * 128: ts0 + (j + 1) * 128, :], pfx + "h%d" % j,
                          writes=[h2t_b[j]])
                for j in range(4):
                    for half in range(2):
                        pb = half

                        def mo_(e, j=j, half=half, pb=pb):
                            ins = None
                            for kc in range(8):
                                ins = e.matmul(C.ps[pb][:], lhsT=ycat[:, kc, j * 128:(j + 1) * 128],
                                               rhs=wout[:, kc, half * 512:(half + 1) * 512], start=(kc == 0),
                                               stop=(kc == 7))
                            return ins
                        S.op("pe", mo_, reads=[ycat_b, wr_b], writes=[C.psb[pb]])
                        S.op("dve", lambda e, j=j, half=half, pb=pb: e.tensor_tensor(
                            out=h2t[:, j, half * 512:(half + 1) * 512], in0=C.ps[pb][:],
                            in1=h2t[:, j, half * 512:(half + 1) * 512], op=ALU.add),
                             reads=[C.psb[pb]], writes=[h2t_b[j]])
                    norm_transpose(C, st, h2t[:, j, :], h2t_b[j], C.gts[:, 2, :], u3T, u3T_b, j * 128, pfx)
                for ch in range(8):
                    pb = ch % 2

                    def mq_(e, ch=ch, pb=pb):
                        ins = None
                        for kc in range(8):
                            ins = e.matmul(C.ps[pb][:], lhsT=wq[:, kc, ch * 128:(ch + 1) * 128], rhs=u3T[:, kc, :],
                                           start=(kc == 0), stop=(kc == 7))
                        return ins
                    S.op("pe", mq_, reads=[wr_b, u3T_b], writes=[C.psb[pb]])
                    S.op("act", lambda e, ch=ch, pb=pb: e.activation(out=qTx[:, ch, :], in_=C.ps[pb][:], func=AF.Copy),
                         reads=[C.psb[pb]], writes=[qTx_b])
                for h in range(4):
                    for mc in range(2):
                        pb = 3 + mc

                        def ml_(e, h=h, mc=mc, pb=pb):
                            ins = None
                            for dc in range(2):
                                ins = e.matmul(C.ps[pb][:], lhsT=kTx[:, 2 * h + dc, mc * 128:(mc + 1) * 128],
                                               rhs=qTx[:, 2 * h + dc, :], start=(dc == 0), stop=(dc == 1))
                            return ins
                        S.op("pe", ml_, reads=[kTx_b, qTx_b], writes=[C.psb[pb]])
                        S.op("act", lambda e, mc=mc, pb=pb: e.activation(out=pTx[mc][:], in_=C.ps[pb][:], func=AF.Exp),
                             reads=[C.psb[pb]], writes=[pTx_b[mc]])

                    def md_(e):
                        e.matmul(C.ps[7][:], lhsT=onesb[:], rhs=pTx[0][:], start=True, stop=False)
                        return e.matmul(C.ps[7][:], lhsT=onesb[:], rhs=pTx[1][:], start=False, stop=True)
                    S.op("pe", md_, reads=[cs_b] + pTx_b, writes=[C.psb[7]])
                    S.op("dve", lambda e: e.reciprocal(out=rec[:], in_=C.ps[7][:]), reads=[C.psb[7]], writes=[rec_b])
                    for dc in range(2):
                        pb = 5 + dc

                        def mo2(e, h=h, dc=dc, pb=pb):
                            ins = None
                            for mc in range(2):
                                ins = e.matmul(C.ps[pb][:], lhsT=vx[:, mc, (2 * h + dc) * 128:(2 * h + dc + 1) * 128],
                                               rhs=pTx[mc][:], start=(mc == 0), stop=(mc == 1))
                            return ins
                        S.op("pe", mo2, reads=[vx_b] + pTx_b, writes=[C.psb[pb]])
                        S.op("dve", lambda e, h=h, dc=dc, pb=pb: e.tensor_tensor(
                            out=oTx[:, 2 * h + dc, :], in0=C.ps[pb][:], in1=rec[:], op=ALU.mult),
                             reads=[C.psb[pb], rec_b], writes=[oTx_b])
                for j in range(4):
                    o = oi % 2
                    oi += 1
                    for half in range(2):
                        pb = half

                        def mf_(e, j=j, half=half, pb=pb):
                            ins = None
                            for kc in range(8):
                                ins = e.matmul(C.ps[pb][:], lhsT=oTx[:, kc, j * 128:(j + 1) * 128],
                                               rhs=wo[:, kc, half * 512:(half + 1) * 512], start=(kc == 0),
                                               stop=(kc == 7))
                            return ins
                        S.op("pe", mf_, reads=[oTx_b, wr_b], writes=[C.psb[pb]])
                        S.op("dve", lambda e, j=j, half=half, pb=pb, o=o: e.tensor_tensor(
                            out=ot[o][:, half * 512:(half + 1) * 512], in0=C.ps[pb][:],
                            in1=h2t[:, j, half * 512:(half + 1) * 512], op=ALU.add),
                             reads=[C.psb[pb], h2t_b[j]], writes=[ot_b[o]])
                    S.dma("sp", h3[ts0 + j * 128: ts0 + (j + 1) * 128, :], ot[o][:], pfx + "o%d" % o, reads=[ot_b[o]])
        W = C.work
        allb = [wr_b, cs_b, memT_b, kTx_b, vx_b, ycat_b, u3T_b, qTx_b, rec_b, oTx_b, W["junk_b"], W["ss_b"], W["xs_b"]] + \
            wkv_b + mt_b + h2t_b + pTx_b + ot_b
        phase_barrier(C, allb)


DBG_OUT = [("h1", [NTOK, D], F32), ("h3", [NTOK, D], F32), ("qkT", [8, 128, NTOK], BF16), ("dqT", [8, 128, NTOK], BF16),
           ("iqT", [4, 128, NTOK], BF16), ("v", [NTOK, 512], BF16), ("og", [NTOK, 512], BF16),
           ("ckv", [NTOK, 128], BF16), ("small", [NTOK, 16], F32), ("ckvT", [128, NTOK], BF16),
           ("kiT", [64, NTOK], BF16), ("ymT", [4, 128, NTOK], BF16), ("ydT", [4, 128, NTOK], BF16)]


def build(stop):
    nc = bass.Bass("TRN2", target_bir_lowering=False)
    C = Ctx()
    C.nc = nc

    def din(name, shape):
        return nc.dram_tensor(name, shape, F32, kind="ExternalInput").ap()

    x = din("x", [NTOK, D])
    mem = din("mem", [512, D])
    w = {}
    for name, shape in [("ffn1_w_gate", [D, DFF]), ("ffn1_w_up", [D, DFF]), ("ffn1_w_down", [DFF, D]),
                        ("ffn2_w_gate", [D, DFF]), ("ffn2_w_up", [D, DFF]), ("ffn2_w_down", [DFF, D]),
                        ("w_in", [D, 3792]), ("w_out", [D, D]), ("xattn_w_q", [D, D]),
                        ("xattn_w_kv", [D, 2 * D]), ("xattn_w_o", [D, D]),
                        ("gts", [128, 5, 8]), ("fin_g_bc", [128, D]), ("hg_bc", [128, 512]),
                        ("kvg_bc", [128, 128]), ("idxg_bc", [128, 64]), ("gbias_bc", [128, 8]),
                        ("convw", [128, 8, 4]), ("convb", [128, 8]), ("wuv", [128, 8, 64]),
                        ("c_ident", [128, 128]), ("c_triu", [128, 128]), ("c_ones", [128, 128]),
                        ("c_cneg", [128, 128])]:
        w[name] = din(name, shape)
    y = nc.dram_tensor("y", [NTOK, D], F32, kind="ExternalOutput").ap()
    SC = {}
    for name, shape, dt in DBG_OUT:
        kind = "ExternalOutput" if stop == 9 else "Internal"
        SC[name] = nc.dram_tensor("s_" + name, shape, dt, kind=kind).ap()
    h1, h3 = SC["h1"], SC["h3"]

    with ExitStack() as gst:
        S = Sync(nc, gst)
        C.S = S
        C.ps = [gst.enter_context(nc.psum_tensor("psb%d" % i, [128, 512], F32)) for i in range(8)]
        C.psb = [Buf() for _ in range(8)]
        C.psv_bf = [p[:].bitcast(BF16).rearrange("p (a b) -> p a b", b=128) for p in C.ps]
        cst = Buf()
        identf = gst.enter_context(nc.sbuf_tensor("identf", [128, 128], F32))
        C.ident = gst.enter_context(nc.sbuf_tensor("ident", [128, 128], BF16))
        C.gts = gst.enter_context(nc.sbuf_tensor("gts_sb", [128, 5, 8], F32))
        C.eps_t = gst.enter_context(nc.sbuf_tensor("eps_t", [128, 1], F32))
        S.dma("sp", identf[:], w["c_ident"], "c0", writes=[cst])
        S.dma("sp", C.gts[:], w["gts"], "c1", writes=[cst])
        S.op("dve", lambda e: e.tensor_copy(out=C.ident[:], in_=identf[:]), reads=[cst], writes=[cst])
        S.op("dve", lambda e: e.memset(C.eps_t[:], EPS), reads=[], writes=[cst])
        phase_barrier(C, [cst])

        ffn_phase(C, "f1", x, h1, C.gts[:, 0, :], w["ffn1_w_gate"], w["ffn1_w_up"], w["ffn1_w_down"])
        inproj_phase(C, h1, w["w_in"], SC, w)
        mlstm_phase(C, SC, w)
        dsa_phase(C, SC, w)
        post_phase(C, h1, mem, h3, SC, w)
        ffn_phase(C, "f2", h3, y, C.gts[:, 4, :], w["ffn2_w_gate"], w["ffn2_w_up"], w["ffn2_w_down"],
                  fin_g=w["fin_g_bc"])
    return nc


def host_layout(inp):
    f = lambda a: np.ascontiguousarray(np.asarray(a, dtype=np.float32))
    sh = {}
    for k in ["ffn1_w_gate", "ffn1_w_up", "ffn1_w_down", "ffn2_w_gate", "ffn2_w_up", "ffn2_w_down",
              "w_in", "w_out", "xattn_w_q", "xattn_w_kv", "xattn_w_o"]:
        sh[k] = f(inp[k][0])
    gt = lambda g: f(np.asarray(g).reshape(8, 128).T)
    sh["gts"] = f(np.stack([gt(inp["ffn1_norm_g"][0]), gt(inp["mix_norm_g"][0]), gt(inp["xattn_norm_g"][0]),
                            gt(inp["mem_norm_g"][0]), gt(inp["ffn2_norm_g"][0])], axis=1))
    bc = lambda v: f(np.broadcast_to(np.asarray(v).reshape(1, -1), (128, np.asarray(v).size)))
    sh["fin_g_bc"] = bc(inp["final_norm_g"])
    sh["hg_bc"] = bc(inp["mlstm_head_norm_g"][0])
    sh["kvg_bc"] = bc(inp["dsa_kv_norm_g"][0])
    sh["idxg_bc"] = bc(inp["idx_k_norm_g"][0])
    sh["gbias_bc"] = bc(np.concatenate([np.asarray(inp["mlstm_i_bias"][0]), np.asarray(inp["mlstm_f_bias"][0])]))
    sh["convw"] = f(np.asarray(inp["mlstm_conv_w"][0]).reshape(4, 8, 128).transpose(2, 1, 0))
    sh["convb"] = f(np.asarray(inp["mlstm_conv_b"][0]).reshape(8, 128).T)
    sh["wuv"] = f(np.asarray(inp["dsa_w_uv"][0]).transpose(1, 0, 2))
    p = np.arange(128)
    sh["c_ident"] = f(np.eye(128))
    sh["c_triu"] = f(p[:, None] <= p[None, :])
    sh["c_ones"] = f(np.ones((128, 128)))
    sh["c_cneg"] = f(np.where(p[None, :] <= p[:, None], 0.0, NEG))
    return sh


def kernel(**inputs):
    stop = 9 if DEBUG_HOOK is not None else 0
    shared = host_layout(inputs)
    xs = np.asarray(inputs["x"], dtype=np.float32).reshape(8, NTOK, D)
    ms = np.asarray(inputs["mem"], dtype=np.float32).reshape(8, 512, D)
    nc = build(stop)
    in_maps = []
    for c in range(8):
        m = dict(shared)
        m["x"] = np.ascontiguousarray(xs[c])
        m["mem"] = np.ascontiguousarray(ms[c])
        in_maps.append(m)
    res = run_bass_kernel_spmd(nc, in_maps, core_ids=list(range(8)))
    if DEBUG_HOOK is not None:
        DEBUG_HOOK(res)
    out = np.stack([np.asarray(r["y"], dtype=np.float32) for r in res.results], axis=0)
    return out.reshape(16, SEQ, D)
```

```python
from contextlib import ExitStack
import numpy as np
import concourse.bass as bass
import concourse.mybir as mybir
from concourse.bass_utils import run_bass_kernel_spmd

F32 = mybir.dt.float32
BF16 = mybir.dt.bfloat16
AF = mybir.ActivationFunctionType
ALU = mybir.AluOpType
AX = mybir.AxisListType

NTOK = 4096
SEQ = 2048
D = 1024
DFF = 2816
NFC = 22
EPS = 1e-6
NEG = -1.0e30
DEBUG_HOOK = None


class Buf:
    __slots__ = ("w", "r")

    def __init__(self):
        self.w = None
        self.r = {}


class Sync:
    def __init__(self, nc, stack):
        self.nc = nc
        self.stack = stack
        self.engs = {"pe": nc.tensor, "act": nc.scalar, "dve": nc.vector, "pool": nc.gpsimd, "sp": nc.sync}
        self.sem = {k: stack.enter_context(nc.semaphore("s_" + k)) for k in ["pe", "act", "dve", "pool"]}
        self.cnt = {k: 0 for k in self.sem}
        self.waited = {k: {} for k in self.engs}
        self.dsem = {}

    def _wait(self, eng, toks):
        need = {}
        for t in toks:
            if t is None:
                continue
            key, sem, val, src = t
            if src == "pe" and eng == "pe":
                continue
            if self.waited[eng].get(key, 0) >= val:
                continue
            if key not in need or need[key][1] < val:
                need[key] = (sem, val)
        for key, (sem, val) in need.items():
            self.engs[eng].wait_ge(sem, val)
            self.waited[eng][key] = val

    @staticmethod
    def _collect(reads, writes):
        toks = []
        for b in reads:
            toks.append(b.w)
        for b in writes:
            toks.append(b.w)
            toks.extend(b.r.values())
        return toks

    @staticmethod
    def _update(tok, reads, writes):
        key = tok[0]
        for b in reads:
            o = b.r.get(key)
            if o is None or o[2] < tok[2]:
                b.r[key] = tok
        for b in writes:
            b.w = tok
            b.r = {}

    def op(self, eng, fn, reads=(), writes=()):
        self._wait(eng, self._collect(reads, writes))
        ins = fn(self.engs[eng])
        self.cnt[eng] += 1
        ins.then_inc(self.sem[eng], 1)
        tok = (eng, self.sem[eng], self.cnt[eng], eng)
        self._update(tok, reads, writes)
        return tok

    def dma(self, q, out, in_, sname, reads=(), writes=(), **kw):
        self._wait(q, self._collect(reads, writes))
        if sname not in self.dsem:
            self.dsem[sname] = [self.stack.enter_context(self.nc.semaphore("d_" + sname)), 0]
        d = self.dsem[sname]
        ins = self.engs[q].dma_start(out=out, in_=in_, **kw)
        d[1] += 16
        ins.then_inc(d[0], 16)
        tok = ("d_" + sname, d[0], d[1], None)
        self._update(tok, reads, writes)
        return tok

    def wait_all(self, eng, bufs):
        toks = []
        for b in bufs:
            toks.append(b.w)
            toks.extend(b.r.values())
        self._wait(eng, toks)


class Ctx:
    pass


def bc_mid(ap2, n):
    return ap2.unsqueeze(2).to_broadcast([ap2.shape[0], ap2.shape[1], n])


def norm_transpose(C, st_sb, xt_ap, xt_b, gT_ap, dstT, dst_b, col0, tag, ps_bank=2):
    nc, S = C.nc, C.S
    W = C.work
    S.op("act", lambda e: e.activation(out=W["junk"][:], in_=xt_ap, func=AF.Square,
                                       accum_out=W["ss"][:, 0:1]),
         reads=[xt_b], writes=[W["junk_b"], W["ss_b"]])
    S.op("act", lambda e: e.activation(out=W["ss"][:, 1:2], in_=W["ss"][:, 0:1], func=AF.Sqrt,
                                       bias=C.eps_t[:, 0:1], scale=1.0 / D),
         reads=[], writes=[W["ss_b"]])
    S.op("dve", lambda e: e.reciprocal(out=W["ss"][:, 2:3], in_=W["ss"][:, 1:2]), reads=[], writes=[W["ss_b"]])
    xs = W["xs"]
    S.op("pool", lambda e: e.tensor_scalar(out=xs[:], in0=xt_ap, scalar1=W["ss"][:, 2:3], scalar2=0.0,
                                           op0=ALU.mult, op1=ALU.add),
         reads=[xt_b, W["ss_b"]], writes=[W["xs_b"]])
    tp = C.psv_bf[ps_bank]

    def tr(e):
        ins = None
        for kc in range(8):
            ins = e.transpose(out=tp[:, kc, :], in_=xs[:, kc * 128:(kc + 1) * 128], identity=C.ident[:])
        return ins
    S.op("pe", tr, reads=[W["xs_b"]], writes=[C.psb[ps_bank]])
    S.op("dve", lambda e: e.tensor_tensor(out=dstT[:, :, col0:col0 + 128], in0=tp[:, :, :],
                                          in1=bc_mid(gT_ap, 128), op=ALU.mult),
         reads=[C.psb[ps_bank]], writes=[dst_b])


def make_work(C, st, pfx):
    nc = C.nc
    W = {}
    W["junk"] = st.enter_context(nc.sbuf_tensor(pfx + "w_junk", [128, 1024], BF16))
    W["junk_b"] = Buf()
    W["ss"] = st.enter_context(nc.sbuf_tensor(pfx + "w_ss", [128, 4], F32))
    W["ss_b"] = Buf()
    W["xs"] = st.enter_context(nc.sbuf_tensor(pfx + "w_xs", [128, 1024], BF16))
    W["xs_b"] = Buf()
    C.work = W


def ffn_phase(C, pfx, src, dst, gT_ap, wg_d, wu_d, wd_d, fin_g=None):
    nc, S = C.nc, C.S
    with ExitStack() as st:
        def sb(name, shape, dt):
            return st.enter_context(nc.sbuf_tensor(pfx + name, shape, dt))
        xres = sb("xres", [128, 8, 1024], F32)
        xres_b = [Buf() for _ in range(8)]
        xnT = sb("xnT", [128, 8, 1024], BF16)
        xnT_b = Buf()
        hT = sb("hT", [128, NFC, 1024], BF16)
        hT_b = [Buf(), Buf()]
        wd = sb("wd", [128, NFC, 1024], BF16)
        wd_b = Buf()
        wg = [sb("wg%d" % i, [128, 8, 256], BF16) for i in range(2)]
        wu = [sb("wu%d" % i, [128, 8, 256], BF16) for i in range(2)]
        wg_b = [Buf(), Buf()]
        wu_b = [Buf(), Buf()]
        sg = [sb("sg%d" % i, [128, 512], F32) for i in range(2)]
        sg_b = [Buf(), Buf()]
        ot = [sb("ot%d" % i, [128, 1024], F32) for i in range(2)]
        ot_b = [Buf(), Buf()]
        make_work(C, st, pfx)
        W = C.work
        if fin_g is not None:
            fing = sb("fing", [128, 1024], F32)
            fing_b = Buf()
            S.dma("sp", fing[:], fin_g, pfx + "fing", writes=[fing_b])
            ot2 = [sb("ot2%d" % i, [128, 1024], F32) for i in range(2)]
            ot2_b = [Buf(), Buf()]

        wgv = wg_d.rearrange("(kc p) n -> p kc n", p=128)
        wuv = wu_d.rearrange("(kc p) n -> p kc n", p=128)
        wdv = wd_d.rearrange("(fc p) n -> p fc n", p=128)
        for i in range(2):
            S.dma("pool", wd[:, i * 11:(i + 1) * 11, :], wdv[:, i * 11:(i + 1) * 11, :], pfx + "wd", writes=[wd_b])
        blk = 0
        oi = 0
        for t in range(4):
            tok0 = t * 1024
            for j in range(8):
                S.dma("sp", xres[:, j, :], src[tok0 + j * 128: tok0 + (j + 1) * 128, :], pfx + "x%d" % j,
                      writes=[xres_b[j]])
            for j in range(8):
                norm_transpose(C, st, xres[:, j, :], xres_b[j], gT_ap, xnT, xnT_b, j * 128, pfx)
            for fb in range(11):
                sl = blk % 2
                blk += 1
                S.dma("pool", wg[sl][:], wgv[:, :, fb * 256:(fb + 1) * 256], pfx + "wg%d" % sl, writes=[wg_b[sl]])
                S.dma("pool", wu[sl][:], wuv[:, :, fb * 256:(fb + 1) * 256], pfx + "wu%d" % sl, writes=[wu_b[sl]])
                for fcl in range(2):
                    fc = fb * 2 + fcl
                    for s in range(2):
                        pi = (fc * 2 + s) % 2
                        pg, pu = C.ps[3 + pi], C.ps[5 + pi]

                        def mm(e, wt=wg[sl], po=pg, fcl=fcl, s=s):
                            ins = None
                            for kc in range(8):
                                ins = e.matmul(po[:], lhsT=wt[:, kc, fcl * 128:(fcl + 1) * 128],
                                               rhs=xnT[:, kc, s * 512:(s + 1) * 512],
                                               start=(kc == 0), stop=(kc == 7))
                            return ins
                        S.op("pe", mm, reads=[wg_b[sl], xnT_b], writes=[C.psb[3 + pi]])
                        S.op("pe", lambda e, mm=mm, wt=wu[sl], po=pu: mm(e, wt, po),
                             reads=[wu_b[sl], xnT_b], writes=[C.psb[5 + pi]])
                        S.op("act", lambda e, pg=pg, pi=pi: e.activation(out=sg[pi][:], in_=pg[:], func=AF.Silu),
                             reads=[C.psb[3 + pi]], writes=[sg_b[pi]])
                        S.op("dve", lambda e, pu=pu, pi=pi, fc=fc, s=s: e.tensor_tensor(
                            out=hT[:, fc, s * 512:(s + 1) * 512], in0=pu[:], in1=sg[pi][:], op=ALU.mult),
                             reads=[C.psb[5 + pi], sg_b[pi]], writes=[hT_b[s]])
            for j in range(8):
                o = oi % 2
                oi += 1
                for half in range(2):
                    pb = 0 + half
                    po = C.ps[pb]

                    def mmd(e, po=po, j=j, half=half):
                        ins = None
                        for fc in range(NFC):
                            ins = e.matmul(po[:], lhsT=hT[:, fc, j * 128:(j + 1) * 128],
                                           rhs=wd[:, fc, half * 512:(half + 1) * 512],
                                           start=(fc == 0), stop=(fc == NFC - 1))
                        return ins
                    S.op("pe", mmd, reads=[hT_b[j // 4], wd_b], writes=[C.psb[pb]])
                    S.op("dve", lambda e, po=po, o=o, j=j, half=half: e.scalar_tensor_tensor(
                        out=ot[o][:, half * 512:(half + 1) * 512], in0=po[:], scalar=0.5,
                        in1=xres[:, j, half * 512:(half + 1) * 512], op0=ALU.mult, op1=ALU.add),
                         reads=[C.psb[pb], xres_b[j]], writes=[ot_b[o]])
                if fin_g is None:
                    S.dma("sp", dst[tok0 + j * 128: tok0 + (j + 1) * 128, :], ot[o][:], pfx + "o%d" % o,
                          reads=[ot_b[o]])
                else:
                    S.op("act", lambda e, o=o: e.activation(out=W["junk"][:], in_=ot[o][:], func=AF.Square,
                                                            accum_out=W["ss"][:, 0:1]),
                         reads=[ot_b[o]], writes=[W["junk_b"], W["ss_b"]])
                    S.op("act", lambda e: e.activation(out=W["ss"][:, 1:2], in_=W["ss"][:, 0:1], func=AF.Sqrt,
                                                       bias=C.eps_t[:, 0:1], scale=1.0 / D),
                         reads=[], writes=[W["ss_b"]])
                    S.op("dve", lambda e: e.reciprocal(out=W["ss"][:, 2:3], in_=W["ss"][:, 1:2]),
                         reads=[], writes=[W["ss_b"]])
                    S.op("dve", lambda e, o=o: e.scalar_tensor_tensor(
                        out=ot2[o][:], in0=ot[o][:], scalar=W["ss"][:, 2:3], in1=fing[:],
                        op0=ALU.mult, op1=ALU.mult),
                         reads=[ot_b[o], W["ss_b"], fing_b], writes=[ot2_b[o]])
                    S.dma("sp", dst[tok0 + j * 128: tok0 + (j + 1) * 128, :], ot2[o][:], pfx + "o%d" % o,
                          reads=[ot2_b[o]])
        allb = xres_b + [xnT_b, wd_b] + hT_b + wg_b + wu_b + sg_b + ot_b + [W["junk_b"], W["ss_b"], W["xs_b"]]
        if fin_g is not None:
            allb += ot2_b + [fing_b]
        phase_barrier(C, allb)


def phase_barrier(C, bufs):
    S = C.S
    bufs = list(bufs) + C.psb
    for e in ["sp", "pool", "act", "dve", "pe"]:
        S.wait_all(e, bufs)


def inproj_phase(C, h1, w_in, SC, w):
    nc, S = C.nc, C.S
    pfx = "ip"
    with ExitStack() as st:
        def sb(name, shape, dt):
            return st.enter_context(nc.sbuf_tensor(pfx + name, shape, dt))
        make_work(C, st, pfx)
        W = C.work
        uT = sb("uT", [128, 8, SEQ], BF16)
        uT_b = Buf()
        ht = [sb("ht%d" % i, [128, 1024], F32) for i in range(3)]
        ht_b = [Buf() for _ in range(3)]
        wf = [sb("wf%d" % i, [128, 8, 256], BF16) for i in range(2)]
        wf_b = [Buf(), Buf()]
        wA = sb("wA", [128, 8, 512], BF16)
        wB = sb("wB", [128, 8, 512], BF16)
        wC = sb("wC", [128, 8, 208], BF16)
        wt_b = Buf()
        zc = sb("zc", [128, 3 + SEQ], F32)
        zc_b = Buf()
        acc = sb("acc", [128, SEQ], F32)
        acc_b = Buf()
        fo = [sb("fo%d" % i, [128, SEQ], BF16) for i in range(2)]
        fo_b = [Buf(), Buf()]
        vt = [sb("vt%d" % i, [128, 512], BF16) for i in range(2)]
        vt_b = [Buf(), Buf()]
        og = [sb("og%d" % i, [128, 512], BF16) for i in range(2)]
        og_b = [Buf(), Buf()]
        sm = [sb("sm%d" % i, [128, 16], F32) for i in range(2)]
        sm_b = [Buf(), Buf()]
        ck = [sb("ck%d" % i, [128, 128], BF16) for i in range(2)]
        ck_b = [Buf(), Buf()]
        ki = [sb("ki%d" % i, [128, 64], BF16) for i in range(2)]
        ki_b = [Buf(), Buf()]
        st2 = sb("st2", [128, 8], F32)
        st2_b = Buf()
        ckvT = sb("ckvT", [128, SEQ], BF16)
        ckvT_b = Buf()
        kiT = sb("kiT", [64, SEQ], BF16)
        kiT_b = Buf()
        cw = sb("cw", [128, 8, 4], F32)
        cb = sb("cb", [128, 8], F32)
        kvg = sb("kvg", [128, 128], F32)
        idg = sb("idg", [128, 64], F32)
        gbi = sb("gbi", [128, 8], F32)
        cs_b = Buf()
        for t_, s_ in [(cw, "convw"), (cb, "convb"), (kvg, "kvg_bc"), (idg, "idxg_bc"), (gbi, "gbias_bc")]:
            S.dma("sp", t_[:], w[s_], pfx + "cs", writes=[cs_b])
        S.op("dve", lambda e: e.memset(zc[:, 0:3], 0.0), writes=[zc_b])

        wv = w_in.rearrange("(kc p) n -> p kc n", p=128)
        S.dma("pool", wA[:], wv[:, :, 1024:1536], pfx + "wt", writes=[wt_b])
        S.dma("pool", wB[:], wv[:, :, 1544:2056], pfx + "wt", writes=[wt_b])
        S.dma("pool", wC[:, :, 0:8], wv[:, :, 1536:1544], pfx + "wt", writes=[wt_b])
        S.dma("pool", wC[:, :, 8:136], wv[:, :, 3080:3208], pfx + "wt", writes=[wt_b])
        S.dma("pool", wC[:, :, 136:208], wv[:, :, 3720:3792], pfx + "wt", writes=[wt_b])
        gT_ap = C.gts[:, 1, :]
        fm = [(c * 128, "conv", c) for c in range(8)] + [(2056 + c * 128, "dq", c) for c in range(8)] + \
             [(3208 + c * 128, "iq", c) for c in range(4)]
        blk = 0
        foi = 0
        ti = 0
        for sq in range(2):
            t0 = sq * SEQ
            for j in range(16):
                sl = (sq * 16 + j) % 3
                S.dma("sp", ht[sl][:], h1[t0 + j * 128: t0 + (j + 1) * 128, :], pfx + "h%d" % sl, writes=[ht_b[sl]])
                norm_transpose(C, st, ht[sl][:], ht_b[sl], gT_ap, uT, uT_b, j * 128, pfx)
            for j in range(16):
                o = ti % 2
                ti += 1
                tk = slice(t0 + j * 128, t0 + (j + 1) * 128)
                for wt, n, pb in [(wA, 512, 3), (wB, 512, 4), (wC, 208, 5)]:
                    def mm(e, wt=wt, n=n, pb=pb, j=j):
                        ins = None
                        for kc in range(8):
                            ins = e.matmul(C.ps[pb][:, 0:n], lhsT=uT[:, kc, j * 128:(j + 1) * 128], rhs=wt[:, kc, :],
                                           start=(kc == 0), stop=(kc == 7))
                        return ins
                    S.op("pe", mm, reads=[uT_b, wt_b], writes=[C.psb[pb]])
                S.op("act", lambda e, o=o: e.activation(out=vt[o][:], in_=C.ps[3][:], func=AF.Copy),
                     reads=[C.psb[3]], writes=[vt_b[o]])
                S.dma("sp", SC["v"][tk, :], vt[o][:], pfx + "v%d" % o, reads=[vt_b[o]])
                S.op("act", lambda e, o=o: e.activation(out=og[o][:], in_=C.ps[4][:], func=AF.Sigmoid),
                     reads=[C.psb[4]], writes=[og_b[o]])
                S.dma("sp", SC["og"][tk, :], og[o][:], pfx + "og%d" % o, reads=[og_b[o]])
                pc = C.ps[5]
                S.op("dve", lambda e, o=o: e.tensor_tensor(out=sm[o][:, 0:8], in0=pc[:, 0:8], in1=gbi[:], op=ALU.add),
                     reads=[C.psb[5], cs_b], writes=[sm_b[o]])
                S.op("dve", lambda e, o=o: e.tensor_scalar(out=sm[o][:, 8:16], in0=pc[:, 200:208],
                                                           scalar1=float(8 ** -0.5 * 64 ** -0.5), scalar2=None,
                                                           op0=ALU.mult),
                     reads=[C.psb[5]], writes=[sm_b[o]])
                S.dma("sp", SC["small"][tk, :], sm[o][:], pfx + "sm%d" % o, reads=[sm_b[o]])
                S.op("act", lambda e: e.activation(out=W["junk"][:, 0:128], in_=pc[:, 8:136], func=AF.Square,
                                                   accum_out=st2[:, 0:1]),
                     reads=[C.psb[5]], writes=[W["junk_b"], st2_b])
                S.op("act", lambda e: e.activation(out=W["junk"][:, 128:192], in_=pc[:, 136:200], func=AF.Square,
                                                   accum_out=st2[:, 1:2]),
                     reads=[C.psb[5]], writes=[W["junk_b"], st2_b])
                S.op("act", lambda e: e.activation(out=st2[:, 2:3], in_=st2[:, 0:1], func=AF.Sqrt,
                                                   bias=C.eps_t[:, 0:1], scale=1.0 / 128), writes=[st2_b])
                S.op("act", lambda e: e.activation(out=st2[:, 3:4], in_=st2[:, 1:2], func=AF.Sqrt,
                                                   bias=C.eps_t[:, 0:1], scale=1.0 / 64), writes=[st2_b])
                S.op("dve", lambda e: e.reciprocal(out=st2[:, 4:6], in_=st2[:, 2:4]), writes=[st2_b])
                S.op("dve", lambda e, o=o: e.scalar_tensor_tensor(out=ck[o][:], in0=pc[:, 8:136], scalar=st2[:, 4:5],
                                                                   in1=kvg[:], op0=ALU.mult, op1=ALU.mult),
                     reads=[C.psb[5], st2_b, cs_b], writes=[ck_b[o]])
                S.op("dve", lambda e, o=o: e.scalar_tensor_tensor(out=ki[o][:], in0=pc[:, 136:200], scalar=st2[:, 5:6],
                                                                   in1=idg[:], op0=ALU.mult, op1=ALU.mult),
                     reads=[C.psb[5], st2_b, cs_b], writes=[ki_b[o]])
                S.dma("sp", SC["ckv"][tk, :], ck[o][:], pfx + "ck%d" % o, reads=[ck_b[o]])
                tp = C.psv_bf[6]

                def tr(e, o=o):
                    e.transpose(out=tp[:, 0, :], in_=ck[o][:], identity=C.ident[:])
                    return e.transpose(out=tp[0:64, 1, :], in_=ki[o][:], identity=C.ident[:])
                S.op("pe", tr, reads=[ck_b[o], ki_b[o]], writes=[C.psb[6]])
                S.op("act", lambda e, j=j: e.activation(out=ckvT[:, j * 128:(j + 1) * 128], in_=tp[:, 0, :], func=AF.Copy),
                     reads=[C.psb[6]], writes=[ckvT_b])
                S.op("act", lambda e, j=j: e.activation(out=kiT[:, j * 128:(j + 1) * 128], in_=tp[0:64, 1, :], func=AF.Copy),
                     reads=[C.psb[6]], writes=[kiT_b])
            S.dma("sp", SC["ckvT"][:, t0:t0 + SEQ], ckvT[:], pfx + "ckT", reads=[ckvT_b])
            S.dma("sp", SC["kiT"][:, t0:t0 + SEQ], kiT[:], pfx + "kiT", reads=[kiT_b])
            for ci, (c0, kind, dch) in enumerate(fm):
                if ci % 2 == 0:
                    sl = blk % 2
                    blk += 1
                    S.dma("pool", wf[sl][:], wv[:, :, c0:c0 + 256], pfx + "wf%d" % sl, writes=[wf_b[sl]])
                f = foi % 2
                foi += 1
                for s in range(4):
                    pb = s % 2

                    def mm(e, sl=sl, ci=ci, s=s, pb=pb):
                        ins = None
                        for kc in range(8):
                            ins = e.matmul(C.ps[pb][:], lhsT=wf[sl][:, kc, (ci % 2) * 128:(ci % 2 + 1) * 128],
                                           rhs=uT[:, kc, s * 512:(s + 1) * 512], start=(kc == 0), stop=(kc == 7))
                        return ins
                    S.op("pe", mm, reads=[wf_b[sl], uT_b], writes=[C.psb[pb]])
                    if kind == "conv":
                        S.op("act", lambda e, pb=pb, s=s: e.activation(out=zc[:, 3 + s * 512: 3 + (s + 1) * 512],
                                                                     in_=C.ps[pb][:], func=AF.Copy),
                             reads=[C.psb[pb]], writes=[zc_b])
                    elif kind == "dq":
                        S.op("act", lambda e, pb=pb, s=s, f=f: e.activation(out=fo[f][:, s * 512:(s + 1) * 512],
                                                                          in_=C.ps[pb][:], func=AF.Copy,
                                                                          scale=float(128 ** -0.5)),
                             reads=[C.psb[pb]], writes=[fo_b[f]])
                    else:
                        S.op("act", lambda e, pb=pb, s=s, f=f: e.activation(out=fo[f][:, s * 512:(s + 1) * 512],
                                                                          in_=C.ps[pb][:], func=AF.Copy),
                             reads=[C.psb[pb]], writes=[fo_b[f]])
                if kind == "conv":
                    S.op("dve", lambda e, dch=dch: e.tensor_scalar(out=acc[:], in0=zc[:, 0:SEQ], scalar1=cw[:, dch, 0:1],
                                                                    scalar2=cb[:, dch:dch + 1], op0=ALU.mult, op1=ALU.add),
                         reads=[zc_b, cs_b], writes=[acc_b])
                    for jj in range(1, 4):
                        S.op("dve", lambda e, dch=dch, jj=jj: e.scalar_tensor_tensor(
                            out=acc[:], in0=zc[:, jj:jj + SEQ], scalar=cw[:, dch, jj:jj + 1], in1=acc[:],
                            op0=ALU.mult, op1=ALU.add), reads=[zc_b, cs_b], writes=[acc_b])
                    S.op("act", lambda e, f=f: e.activation(out=fo[f][:], in_=acc[:], func=AF.Silu),
                         reads=[acc_b], writes=[fo_b[f]])
                    dstd = SC["qkT"][dch, :, t0:t0 + SEQ]
                elif kind == "dq":
                    dstd = SC["dqT"][dch, :, t0:t0 + SEQ]
                else:
                    dstd = SC["iqT"][dch, :, t0:t0 + SEQ]
                S.dma("sp", dstd, fo[f][:], pfx + "fo%d" % f, reads=[fo_b[f]])
        allb = [uT_b, wt_b, zc_b, acc_b, st2_b, ckvT_b, kiT_b, cs_b, W["junk_b"], W["ss_b"], W["xs_b"]] + ht_b + wf_b + \
            fo_b + vt_b + og_b + sm_b + ck_b + ki_b
        phase_barrier(C, allb)


def mlstm_phase(C, SC, w):
    nc, S = C.nc, C.S
    pfx = "ml"
    SC_ = float(128 ** -0.5)
    with ExitStack() as st:
        def sb(name, shape, dt):
            return st.enter_context(nc.sbuf_tensor(pfx + name, shape, dt))
        qT = sb("qT", [128, 4, SEQ], BF16)
        kT = sb("kT", [128, 4, SEQ], BF16)
        v = sb("v", [128, 16, 512], BF16)
        G = sb("G", [128, 16, 512], BF16)
        smt = sb("smt", [128, 16, 16], F32)
        ld_b = Buf()
        hg = sb("hg", [128, 512], F32)
        triu = sb("triu", [128, 128], F32)
        onesf = sb("onesf", [128, 128], F32)
        masku = sb("masku", [128, 128], F32)
        cs_b = Buf()
        ktm = sb("ktm", [128, 16, 512], BF16)
        ktm_b = Buf()
        lf = sb("lf", [128, 64], F32)
        ebp = sb("ebp", [128, 64], F32)
        eg = sb("eg", [128, 64], F32)
        ebL = sb("ebL", [128, 64], F32)
        pre_b = Buf()
        vaug = [sb("vaug%d" % i, [128, 4, 130], BF16) for i in range(2)]
        vaug_b = [Buf(), Buf()]
        Pm = [sb("Pm%d" % i, [128, 4, 128], BF16) for i in range(2)]
        Pm_b = [Buf(), Buf()]
        C32 = [sb("C32_%d" % h, [128, 130], F32) for h in range(4)]
        C32_b = [Buf() for _ in range(4)]
        Cbf = [sb("Cbf_%d" % h, [128, 130], BF16) for h in range(4)]
        Cbf_b = [Buf() for _ in range(4)]
        t4 = sb("t4", [128, 16], F32)
        t4_b = Buf()
        hv = sb("hv", [128, 4, 128], F32)
        hv_b = Buf()
        sq = sb("sq", [128, 4, 128], F32)
        sq_b = Buf()
        yb = [sb("yb%d" % i, [128, 512], BF16) for i in range(2)]
        yb_b = [Buf(), Buf()]
        ymT = sb("ymT", [128, 4, SEQ], BF16)
        ymT_b = Buf()
        S.dma("sp", hg[:], w["hg_bc"], pfx + "cs", writes=[cs_b])
        S.dma("sp", triu[:], w["c_triu"], pfx + "cs", writes=[cs_b])
        S.dma("sp", onesf[:], w["c_ones"], pfx + "cs", writes=[cs_b])
        S.dma("sp", masku[:], w["c_triu"], pfx + "cs", writes=[cs_b])
        vi = 0
        for sq_i in range(2):
            t0 = sq_i * SEQ
            for h in range(4):
                S.dma("sp", qT[:, h, :], SC["qkT"][h, :, t0:t0 + SEQ], pfx + "ld", writes=[ld_b])
                S.dma("sp", kT[:, h, :], SC["qkT"][4 + h, :, t0:t0 + SEQ], pfx + "ld", writes=[ld_b])
            S.dma("sp", v[:], SC["v"][t0:t0 + SEQ, :].rearrange("(c p) f -> p c f", p=128), pfx + "ld", writes=[ld_b])
            S.dma("sp", G[:], SC["og"][t0:t0 + SEQ, :].rearrange("(c p) f -> p c f", p=128), pfx + "ld", writes=[ld_b])
            S.dma("sp", smt[:], SC["small"][t0:t0 + SEQ, :].rearrange("(c p) f -> p c f", p=128), pfx + "ld",
                  writes=[ld_b])
            lf3 = lf[:].rearrange("p (c h) -> p c h", h=4)
            S.op("act", lambda e: e.activation(out=lf3, in_=smt[:, :, 4:8], func=AF.Exp, scale=-1.0),
                 reads=[ld_b], writes=[pre_b])
            S.op("dve", lambda e: e.tensor_scalar(out=lf[:], in0=lf[:], scalar1=1.0, scalar2=None, op0=ALU.add),
                 writes=[pre_b])
            S.op("act", lambda e: e.activation(out=lf[:], in_=lf[:], func=AF.Ln), writes=[pre_b])
            S.op("dve", lambda e: e.tensor_scalar(out=lf[:], in0=lf[:], scalar1=-1.0, scalar2=None, op0=ALU.mult),
                 writes=[pre_b])
            pbb, pbl = C.ps[6], C.ps[7]
            S.op("pe", lambda e: e.matmul(pbb[:, 0:64], lhsT=triu[:], rhs=lf[:], start=True, stop=True),
                 reads=[pre_b, cs_b], writes=[C.psb[6]])
            S.op("pe", lambda e: e.matmul(pbl[:, 0:64], lhsT=onesf[:], rhs=lf[:], start=True, stop=True),
                 reads=[pre_b, cs_b], writes=[C.psb[7]])
            S.op("act", lambda e: e.activation(out=ebp[:], in_=pbb[:, 0:64], func=AF.Exp),
                 reads=[C.psb[6]], writes=[pre_b])
            S.op("dve", lambda e: e.tensor_scalar(out=ebp[:], in0=ebp[:], scalar1=SC_, scalar2=None, op0=ALU.mult),
                 writes=[pre_b])
            eg3 = eg[:].rearrange("p (c h) -> p c h", h=4)
            S.op("dve", lambda e: e.tensor_tensor(out=eg3, in0=smt[:, :, 0:4],
                                                  in1=pbb[:, 0:64].rearrange("p (c h) -> p c h", h=4), op=ALU.subtract),
                 reads=[ld_b, C.psb[6]], writes=[pre_b])
            S.op("act", lambda e: e.activation(out=eg[:], in_=eg[:], func=AF.Exp), writes=[pre_b])
            S.op("act", lambda e: e.activation(out=ebL[:], in_=pbl[:, 0:64], func=AF.Exp),
                 reads=[C.psb[7]], writes=[pre_b])
            S.op("dve", lambda e: e.tensor_tensor(out=G[:], in0=G[:], in1=hg[:].unsqueeze(1).to_broadcast([128, 16, 512]),
                                                  op=ALU.mult), reads=[cs_b], writes=[ld_b])
            tp = C.psv_bf[5]
            for c in range(16):
                def trk(e, c=c):
                    ins = None
                    for h in range(4):
                        ins = e.transpose(out=tp[:, h, :], in_=kT[:, h, c * 128:(c + 1) * 128], identity=C.ident[:])
                    return ins
                S.op("pe", trk, reads=[ld_b], writes=[C.psb[5]])
                S.op("act", lambda e, c=c: e.activation(out=ktm[:, c, :], in_=tp[:, 0:4, :].rearrange("p a b -> p (a b)"),
                                                        func=AF.Copy),
                     reads=[C.psb[5]], writes=[ktm_b])
            for c in range(16):
                cs4 = slice(c * 4, (c + 1) * 4)
                tsl = slice(c * 128, (c + 1) * 128)
                vb = vi % 2
                vi += 1
                S.op("dve", lambda e, vb=vb, c=c, cs4=cs4: e.tensor_tensor(
                    out=vaug[vb][:, :, 0:128], in0=v[:, c, :].rearrange("p (h d) -> p h d", d=128),
                    in1=bc_mid(eg[:, cs4], 128), op=ALU.mult), reads=[ld_b, pre_b], writes=[vaug_b[vb]])
                S.op("dve", lambda e, vb=vb, cs4=cs4: e.tensor_copy(out=vaug[vb][:, :, 128:129],
                                                                     in_=eg[:, cs4].unsqueeze(2)),
                     reads=[pre_b], writes=[vaug_b[vb]])
                pa = C.ps[0]

                def mst(e, tsl=tsl):
                    ins = None
                    for h in range(4):
                        ins = e.matmul(pa[:, h * 128:(h + 1) * 128], lhsT=kT[:, h, tsl], rhs=qT[:, h, tsl],
                                       start=True, stop=True)
                    return ins
                S.op("pe", mst, reads=[ld_b], writes=[C.psb[0]])
                S.op("dve", lambda e, vb=vb: e.tensor_tensor(
                    out=Pm[vb][:], in0=pa[:].rearrange("p (h j) -> p h j", j=128),
                    in1=masku[:].unsqueeze(1).to_broadcast([128, 4, 128]), op=ALU.mult),
                     reads=[C.psb[0], cs_b], writes=[Pm_b[vb]])
                for k2 in range(2):
                    def mr(e, k2=k2, vb=vb, c=c, tsl=tsl):
                        ins = None
                        for hh in range(2):
                            h = k2 * 2 + hh
                            o_ = C.ps[1 + k2][:, hh * 256: hh * 256 + 129]
                            if c > 0:
                                e.matmul(o_, lhsT=qT[:, h, tsl], rhs=Cbf[h][:, 0:129], start=True, stop=False)
                            ins = e.matmul(o_, lhsT=Pm[vb][:, h, :], rhs=vaug[vb][:, h, 0:129], start=(c == 0), stop=True)
                        return ins
                    S.op("pe", mr, reads=[ld_b, Pm_b[vb], vaug_b[vb]] + [Cbf_b[k2 * 2], Cbf_b[k2 * 2 + 1]],
                         writes=[C.psb[1 + k2]])
                for k2 in range(2):
                    def mu(e, k2=k2, vb=vb, c=c):
                        ins = None
                        for hh in range(2):
                            h = k2 * 2 + hh
                            ins = e.matmul(C.ps[3 + k2][:, hh * 256: hh * 256 + 129],
                                           lhsT=ktm[:, c, h * 128:(h + 1) * 128], rhs=vaug[vb][:, h, 0:129],
                                           start=True, stop=True)
                        return ins
                    S.op("pe", mu, reads=[ktm_b, vaug_b[vb]], writes=[C.psb[3 + k2]])
                R3 = [C.ps[1 + k2][:].rearrange("p (a b) -> p a b", b=256) for k2 in range(2)]
                U3 = [C.ps[3 + k2][:].rearrange("p (a b) -> p a b", b=256) for k2 in range(2)]
                for k2 in range(2):
                    S.op("dve", lambda e, k2=k2, c=c: e.tensor_tensor(
                        out=t4[:, 2 * k2:2 * k2 + 2].unsqueeze(2), in0=R3[k2][:, :, 128:129],
                        in1=ebp[:, c * 4 + 2 * k2: c * 4 + 2 * k2 + 2].unsqueeze(2), op=ALU.mult),
                         reads=[C.psb[1 + k2], pre_b], writes=[t4_b])
                S.op("dve", lambda e: e.tensor_scalar(out=t4[:, 4:8], in0=t4[:, 0:4], scalar1=-1.0, scalar2=None,
                                                      op0=ALU.mult), writes=[t4_b])
                S.op("dve", lambda e: e.tensor_tensor(out=t4[:, 4:8], in0=t4[:, 4:8], in1=t4[:, 0:4], op=ALU.max),
                     writes=[t4_b])
                S.op("dve", lambda e: e.tensor_scalar(out=t4[:, 4:8], in0=t4[:, 4:8], scalar1=1.0, scalar2=None,
                                                      op0=ALU.max), writes=[t4_b])
                S.op("dve", lambda e: e.reciprocal(out=t4[:, 8:12], in_=t4[:, 4:8]), writes=[t4_b])
                S.op("dve", lambda e, cs4=cs4: e.tensor_tensor(out=t4[:, 12:16], in0=t4[:, 8:12], in1=ebp[:, cs4],
                                                               op=ALU.mult), reads=[pre_b], writes=[t4_b])
                for k2 in range(2):
                    S.op("dve", lambda e, k2=k2: e.tensor_tensor(
                        out=hv[:, 2 * k2:2 * k2 + 2, :], in0=R3[k2][:, :, 0:128],
                        in1=bc_mid(t4[:, 12 + 2 * k2: 14 + 2 * k2], 128), op=ALU.mult),
                         reads=[C.psb[1 + k2], t4_b], writes=[hv_b])
                S.op("pool", lambda e: e.tensor_tensor(out=sq[:], in0=hv[:], in1=hv[:], op=ALU.mult),
                     reads=[hv_b], writes=[sq_b])
                S.op("dve", lambda e: e.tensor_reduce(out=t4[:, 0:4], in_=sq[:], axis=AX.X, op=ALU.add),
                     reads=[sq_b], writes=[t4_b])
                S.op("act", lambda e: e.activation(out=t4[:, 4:8], in_=t4[:, 0:4], func=AF.Sqrt,
                                                   bias=C.eps_t[:, 0:1], scale=1.0 / 128), writes=[t4_b])
                S.op("dve", lambda e: e.reciprocal(out=t4[:, 8:12], in_=t4[:, 4:8]), writes=[t4_b])
                S.op("dve", lambda e: e.tensor_tensor(out=hv[:], in0=hv[:], in1=bc_mid(t4[:, 8:12], 128), op=ALU.mult),
                     reads=[t4_b, sq_b], writes=[hv_b])
                S.op("dve", lambda e, vb=vb, c=c: e.tensor_tensor(out=yb[vb][:], in0=hv[:].rearrange("p a b -> p (a b)"),
                                                                   in1=G[:, c, :], op=ALU.mult),
                     reads=[hv_b, ld_b], writes=[yb_b[vb]])

                def try_(e, vb=vb):
                    ins = None
                    for h in range(4):
                        ins = e.transpose(out=tp[:, h, :], in_=yb[vb][:, h * 128:(h + 1) * 128], identity=C.ident[:])
                    return ins
                S.op("pe", try_, reads=[yb_b[vb]], writes=[C.psb[5]])
                S.op("act", lambda e, tsl=tsl: e.activation(out=ymT[:, :, tsl], in_=tp[:, 0:4, :], func=AF.Copy),
                     reads=[C.psb[5]], writes=[ymT_b])
                for h in range(4):
                    k2, hh = h // 2, h % 2
                    if c == 0:
                        S.op("act", lambda e, h=h, k2=k2, hh=hh: e.activation(out=C32[h][:, 0:129],
                                                                              in_=U3[k2][:, hh, 0:129], func=AF.Copy),
                             reads=[C.psb[3 + k2]], writes=[C32_b[h]])
                    else:
                        S.op("dve", lambda e, h=h, k2=k2, hh=hh, c=c: e.scalar_tensor_tensor(
                            out=C32[h][:, 0:129], in0=C32[h][:, 0:129], scalar=ebL[:, (c - 1) * 4 + h:(c - 1) * 4 + h + 1],
                            in1=U3[k2][:, hh, 0:129], op0=ALU.mult, op1=ALU.add),
                             reads=[C.psb[3 + k2], pre_b], writes=[C32_b[h]])
                    if c < 15:
                        S.op("act", lambda e, h=h, c=c: e.activation(out=Cbf[h][:, 0:129], in_=C32[h][:, 0:129],
                                                                     func=AF.Copy, scale=ebL[:, c * 4 + h: c * 4 + h + 1]),
                             reads=[C32_b[h], pre_b], writes=[Cbf_b[h]])
            for h in range(4):
                S.dma("sp", SC["ymT"][h, :, t0:t0 + SEQ], ymT[:, h, :], pfx + "st", reads=[ymT_b])
        allb = [ld_b, cs_b, ktm_b, pre_b, t4_b, hv_b, sq_b, ymT_b] + vaug_b + Pm_b + C32_b + Cbf_b + yb_b
        phase_barrier(C, allb)


TOPK_BISECT = True
NBIS = 24


def dsa_phase(C, SC, w):
    nc, S = C.nc, C.S
    pfx = "ds"
    with ExitStack() as st:
        def sb(name, shape, dt):
            return st.enter_context(nc.sbuf_tensor(pfx + name, shape, dt))
        dqT = sb("dqT", [128, 8, SEQ], BF16)
        iqT = sb("iqT", [128, 4, SEQ], BF16)
        kiT2 = sb("kiT2", [128, SEQ], BF16)
        ckv = sb("ckv", [128, 16, 128], BF16)
        ckvT = sb("ckvT", [128, SEQ], BF16)
        smt = sb("smt", [128, 16, 16], F32)
        ld_b = Buf()
        wuv = sb("wuv", [128, 8, 64], BF16)
        cneg = sb("cneg", [128, 128], F32)
        onesf = sb("onesf", [128, 128], F32)
        onesb = sb("onesb", [128, 128], BF16)
        cs_b = Buf()
        score = [sb("score%d" % i, [128, SEQ], F32) for i in range(2)]
        score_b = [Buf(), Buf()]
        work = [sb("work%d" % i, [128, SEQ], F32 if not TOPK_BISECT else BF16) for i in range(2)]
        work_b = [Buf(), Buf()]
        m8 = [sb("m8%d" % i, [128, 8], F32) for i in range(2)]
        m8_b = [Buf(), Buf()]
        bs = [sb("bs%d" % i, [128, 8], F32) for i in range(2)]
        bs_b = [Buf(), Buf()]
        rl = [sb("rl%d" % i, [128, 512], F32) for i in range(2)]
        rl_b = [Buf(), Buf()]
        sel = [sb("sel%d" % i, [128, SEQ], BF16) for i in range(4)]
        sel_b = [Buf() for _ in range(4)]
        negT = [sb("negT%d" % i, [128, 16, 128], BF16) for i in range(2)]
        negT_b = [Buf(), Buf()]
        pT = [sb("pT%d" % i, [128, 512], BF16) for i in range(2)]
        pT_b = [Buf(), Buf()]
        rec = sb("rec", [128, 512], F32)
        rec_b = Buf()
        oTn = sb("oTn", [128, 8, 128], BF16)
        oTn_b = Buf()
        ydT = sb("ydT", [128, 4, SEQ], BF16)
        ydT_b = Buf()
        S.dma("pool", wuv[:], w["wuv"], pfx + "cs", writes=[cs_b])
        S.dma("sp", cneg[:], w["c_cneg"], pfx + "cs", writes=[cs_b])
        S.dma("sp", onesf[:], w["c_ones"], pfx + "cs", writes=[cs_b])
        S.op("dve", lambda e: e.tensor_copy(out=onesb[:], in_=onesf[:]), reads=[cs_b], writes=[cs_b])
        cnts = {"pt": 0, "lt": 0}

        def s1_scores(bi, p):
            nk = (bi + 1) * 128
            tsl = slice(bi * 128, (bi + 1) * 128)
            sc = score[p]
            for ks in range((nk + 511) // 512):
                wd_ = min(512, nk - ks * 512)
                ksl = slice(ks * 512, ks * 512 + wd_)
                for h in range(8):
                    pb = h % 2
                    p0 = (h % 2) * 64
                    S.op("pe", lambda e, pb=pb, p0=p0, h=h, ksl=ksl, wd_=wd_: e.matmul(
                        C.ps[pb][:, 0:wd_], lhsT=iqT[p0:p0 + 64, h // 2, tsl], rhs=kiT2[p0:p0 + 64, ksl],
                        start=True, stop=True), reads=[ld_b], writes=[C.psb[pb]])
                    S.op("act", lambda e, pb=pb, wd_=wd_: e.activation(out=rl[pb][:, 0:wd_], in_=C.ps[pb][:, 0:wd_],
                                                                      func=AF.Relu),
                         reads=[C.psb[pb]], writes=[rl_b[pb]])
                    wcol = smt[:, bi, 8 + h: 9 + h]
                    if h == 0:
                        S.op("dve", lambda e, pb=pb, wd_=wd_, ksl=ksl, wcol=wcol: e.tensor_scalar(
                            out=sc[:, ksl], in0=rl[pb][:, 0:wd_], scalar1=wcol, scalar2=None, op0=ALU.mult),
                             reads=[rl_b[pb], ld_b], writes=[score_b[p]])
                    else:
                        S.op("dve", lambda e, pb=pb, wd_=wd_, ksl=ksl, wcol=wcol: e.scalar_tensor_tensor(
                            out=sc[:, ksl], in0=rl[pb][:, 0:wd_], scalar=wcol, in1=sc[:, ksl],
                            op0=ALU.mult, op1=ALU.add), reads=[rl_b[pb], ld_b], writes=[score_b[p]])
            if TOPK_BISECT and bi >= 2:
                S.op("dve", lambda e: e.tensor_reduce(out=bs[p][:, 5:6], in_=sc[:, 0:nk], axis=AX.X, op=ALU.min),
                     reads=[score_b[p]], writes=[bs_b[p]])
            S.op("dve", lambda e: e.tensor_tensor(out=sc[:, tsl], in0=sc[:, tsl], in1=cneg[:], op=ALU.add),
                 reads=[cs_b], writes=[score_b[p]])

        def s1_topk_pair(blocks):
            if TOPK_BISECT:
                for (bi, p) in blocks:
                    nk = (bi + 1) * 128
                    S.op("dve", lambda e, p=p, nk=nk: e.max(out=m8[p][:], in_=score[p][:, 0:nk]),
                         reads=[score_b[p]], writes=[m8_b[p]])
                    S.op("dve", lambda e, p=p: e.tensor_copy(out=bs[p][:, 0:1], in_=bs[p][:, 5:6]), writes=[bs_b[p]])
                    S.op("dve", lambda e, p=p: e.tensor_tensor(out=bs[p][:, 1:2], in0=m8[p][:, 0:1], in1=bs[p][:, 5:6],
                                                               op=ALU.subtract), reads=[m8_b[p]], writes=[bs_b[p]])
                for k in range(1, NBIS + 1):
                    ck = float(2.0 ** (-k))
                    for (bi, p) in blocks:
                        S.op("dve", lambda e, p=p, ck=ck: e.scalar_tensor_tensor(
                            out=bs[p][:, 2:3], in0=bs[p][:, 1:2], scalar=ck, in1=bs[p][:, 0:1],
                            op0=ALU.mult, op1=ALU.add), writes=[bs_b[p]])
                    for (bi, p) in blocks:
                        nk = (bi + 1) * 128
                        S.op("dve", lambda e, p=p, nk=nk: e.tensor_scalar(
                            out=work[p][:, 0:nk], in0=score[p][:, 0:nk], scalar1=bs[p][:, 2:3], scalar2=None,
                            op0=ALU.is_ge, op1=ALU.add, accum_out=bs[p][:, 3:4]),
                             reads=[score_b[p]], writes=[work_b[p], bs_b[p]])
                    for (bi, p) in blocks:
                        S.op("dve", lambda e, p=p: e.tensor_scalar(
                            out=bs[p][:, 4:5], in0=bs[p][:, 3:4], scalar1=255.5, scalar2=bs[p][:, 1:2],
                            op0=ALU.is_ge, op1=ALU.mult), writes=[bs_b[p]])
                    for (bi, p) in blocks:
                        S.op("dve", lambda e, p=p, ck=ck: e.scalar_tensor_tensor(
                            out=bs[p][:, 0:1], in0=bs[p][:, 4:5], scalar=ck, in1=bs[p][:, 0:1],
                            op0=ALU.mult, op1=ALU.add), writes=[bs_b[p]])
            else:
                for r in range(32):
                    for (bi, p) in blocks:
                        nk = (bi + 1) * 128
                        src_ = score[p] if r == 0 else work[p]
                        S.op("dve", lambda e, src_=src_, nk=nk, p=p: e.max(out=m8[p][:], in_=src_[:, 0:nk]),
                             reads=[score_b[p], work_b[p]], writes=[m8_b[p]])
                    if r < 31:
                        for (bi, p) in blocks:
                            nk = (bi + 1) * 128
                            src_ = score[p] if r == 0 else work[p]
                            S.op("dve", lambda e, src_=src_, nk=nk, p=p: e.match_replace(
                                out=work[p][:, 0:nk], in_to_replace=m8[p][:], in_values=src_[:, 0:nk], imm_value=NEG),
                                 reads=[score_b[p], m8_b[p]], writes=[work_b[p]])

        def s1_sel(bi, p, slot):
            nk = (bi + 1) * 128
            if bi >= 2:
                thr = bs[p][:, 0:1] if TOPK_BISECT else m8[p][:, 7:8]
                S.op("dve", lambda e: e.tensor_scalar(out=sel[slot][:, 0:nk], in0=score[p][:, 0:nk],
                                                      scalar1=thr, scalar2=None, op0=ALU.is_ge),
                     reads=[score_b[p], m8_b[p], bs_b[p]], writes=[sel_b[slot]])
            else:
                S.op("dve", lambda e: e.tensor_scalar(out=sel[slot][:, 0:nk], in0=score[p][:, 0:nk],
                                                      scalar1=-1.0e29, scalar2=None, op0=ALU.is_ge),
                     reads=[score_b[p]], writes=[sel_b[slot]])

        def s2(bi, slot):
            nkc = bi + 1
            tsl = slice(bi * 128, (bi + 1) * 128)
            ng = bi % 2
            tp = C.psv_bf[2]
            for k0 in range(0, nkc, 8):
                n_ = min(8, nkc - k0)

                def trs(e, k0=k0, n_=n_):
                    ins = None
                    for i in range(n_):
                        ins = e.transpose(out=tp[:, i, :], in_=sel[slot][:, (k0 + i) * 128:(k0 + i + 1) * 128],
                                          identity=C.ident[:])
                    return ins
                S.op("pe", trs, reads=[sel_b[slot]], writes=[C.psb[2]])
                S.op("dve", lambda e, k0=k0, n_=n_: e.tensor_scalar(
                    out=negT[ng][:, k0:k0 + n_, :], in0=tp[:, 0:n_, :], scalar1=-1.0, scalar2=30000.0,
                    op0=ALU.add, op1=ALU.mult), reads=[C.psb[2]], writes=[negT_b[ng]])
            for g in range(2):
                ob = 5 + g
                for kc in range(nkc):
                    lb = 3 + (cnts["lt"] % 2)
                    cnts["lt"] += 1
                    ps_ = cnts["pt"] % 2
                    cnts["pt"] += 1
                    ksl = slice(kc * 128, (kc + 1) * 128)

                    def mlt(e, lb=lb, ksl=ksl, g=g, kc=kc):
                        o3 = C.ps[lb][:].rearrange("p (a b) -> p a b", b=128)
                        e.matmul(o3, lhsT=ckvT[:, ksl], rhs=dqT[:, 4 * g:4 * g + 4, tsl], start=True, stop=False)
                        return e.matmul(o3, lhsT=C.ident[:],
                                        rhs=negT[ng][:, kc, :].unsqueeze(1).to_broadcast([128, 4, 128]),
                                        start=False, stop=True)
                    S.op("pe", mlt, reads=[ld_b, negT_b[ng]], writes=[C.psb[lb]])
                    S.op("act", lambda e, lb=lb, ps_=ps_: e.activation(out=pT[ps_][:], in_=C.ps[lb][:], func=AF.Exp),
                         reads=[C.psb[lb]], writes=[pT_b[ps_]])

                    def mpv(e, ob=ob, kc=kc, ps_=ps_):
                        e.matmul(C.ps[ob][:], lhsT=ckv[:, kc, :], rhs=pT[ps_][:], start=(kc == 0),
                                 stop=(kc == nkc - 1))
                        return e.matmul(C.ps[7][:], lhsT=onesb[:], rhs=pT[ps_][:], start=(kc == 0),
                                        stop=(kc == nkc - 1))
                    S.op("pe", mpv, reads=[ld_b, cs_b, pT_b[ps_]], writes=[C.psb[ob], C.psb[7]])
                S.op("dve", lambda e: e.reciprocal(out=rec[:], in_=C.ps[7][:]), reads=[C.psb[7]], writes=[rec_b])
                S.op("dve", lambda e, ob=ob, g=g: e.tensor_tensor(
                    out=oTn[:, 4 * g:4 * g + 4, :].rearrange("p a b -> p (a b)"), in0=C.ps[ob][:], in1=rec[:],
                    op=ALU.mult), reads=[C.psb[ob], rec_b], writes=[oTn_b])

            def mup(e):
                ins = None
                for h in range(8):
                    p0 = (h % 2) * 64
                    ins = e.matmul(C.ps[2][p0:p0 + 64, (h // 2) * 128:(h // 2 + 1) * 128], lhsT=wuv[:, h, :],
                                   rhs=oTn[:, h, :], start=True, stop=True)
                return ins
            S.op("pe", mup, reads=[oTn_b, cs_b], writes=[C.psb[2]])
            S.op("act", lambda e: e.activation(out=ydT[:, :, tsl], in_=C.ps[2][:].rearrange("p (a b) -> p a b", b=128),
                                               func=AF.Copy), reads=[C.psb[2]], writes=[ydT_b])

        for sq_i in range(2):
            t0 = sq_i * SEQ
            for h in range(8):
                S.dma("sp", dqT[:, h, :], SC["dqT"][h, :, t0:t0 + SEQ], pfx + "ld", writes=[ld_b])
            for c in range(4):
                S.dma("sp", iqT[:, c, :], SC["iqT"][c, :, t0:t0 + SEQ], pfx + "ld", writes=[ld_b])
            S.dma("sp", kiT2[0:64, :], SC["kiT"][:, t0:t0 + SEQ], pfx + "ld", writes=[ld_b])
            S.dma("sp", kiT2[64:128, :], SC["kiT"][:, t0:t0 + SEQ], pfx + "ld", writes=[ld_b])
            S.dma("sp", ckvT[:], SC["ckvT"][:, t0:t0 + SEQ], pfx + "ld", writes=[ld_b])
            S.dma("sp", ckv[:], SC["ckv"][t0:t0 + SEQ, :].rearrange("(c p) f -> p c f", p=128), pfx + "ld", writes=[ld_b])
            S.dma("sp", smt[:], SC["small"][t0:t0 + SEQ, :].rearrange("(c p) f -> p c f", p=128), pfx + "ld",
                  writes=[ld_b])

            def stage1(k):
                blocks = [(2 * k, 0), (2 * k + 1, 1)]
                for (bi, p) in blocks:
                    s1_scores(bi, p)
                if k >= 1:
                    s1_topk_pair(blocks)
                for (bi, p) in blocks:
                    s1_sel(bi, p, (k % 2) * 2 + p)

            def stage2(k):
                for p in range(2):
                    s2(2 * k + p, (k % 2) * 2 + p)
            stage1(0)
            for k in range(1, 8):
                stage1(k)
                stage2(k - 1)
            stage2(7)
            for c in range(4):
                S.dma("sp", SC["ydT"][c, :, t0:t0 + SEQ], ydT[:, c, :], pfx + "st", reads=[ydT_b])
        allb = [ld_b, cs_b, rec_b, oTn_b, ydT_b] + rl_b + pT_b + score_b + work_b + m8_b + bs_b + sel_b + negT_b
        phase_barrier(C, allb)


def post_phase(C, h1, mem, h3, SC, w):
    nc, S = C.nc, C.S
    pfx = "po"
    with ExitStack() as st:
        def sb(name, shape, dt):
            return st.enter_context(nc.sbuf_tensor(pfx + name, shape, dt))
        make_work(C, st, pfx)
        wout = sb("wout", [128, 8, 1024], BF16)
        wq = sb("wq", [128, 8, 1024], BF16)
        wo = sb("wo", [128, 8, 1024], BF16)
        wr_b = Buf()
        wkv = [sb("wkv%d" % i, [128, 8, 512], BF16) for i in range(2)]
        wkv_b = [Buf(), Buf()]
        onesf = sb("onesf", [128, 128], F32)
        onesb = sb("onesb", [128, 128], BF16)
        cs_b = Buf()
        mt = [sb("mt%d" % i, [128, 1024], F32) for i in range(2)]
        mt_b = [Buf(), Buf()]
        memT = sb("memT", [128, 8, 256], BF16)
        memT_b = Buf()
        kTx = sb("kTx", [128, 8, 256], BF16)
        kTx_b = Buf()
        vx = sb("vx", [128, 2, 1024], BF16)
        vx_b = Buf()
        h2t = sb("h2t", [128, 4, 1024], F32)
        h2t_b = [Buf() for _ in range(4)]
        ycat = sb("ycat", [128, 8, 512], BF16)
        ycat_b = Buf()
        u3T = sb("u3T", [128, 8, 512], BF16)
        u3T_b = Buf()
        qTx = sb("qTx", [128, 8, 512], BF16)
        qTx_b = Buf()
        pTx = [sb("pTx%d" % i, [128, 512], BF16) for i in range(2)]
        pTx_b = [Buf(), Buf()]
        rec = sb("rec", [128, 512], F32)
        rec_b = Buf()
        oTx = sb("oTx", [128, 8, 512], BF16)
        oTx_b = Buf()
        ot = [sb("ot%d" % i, [128, 1024], F32) for i in range(2)]
        ot_b = [Buf(), Buf()]
        vw = lambda a: a.rearrange("(kc p) n -> p kc n", p=128)
        S.dma("pool", wout[:], vw(w["w_out"]), pfx + "wr", writes=[wr_b])
        S.dma("pool", wq[:], vw(w["xattn_w_q"]), pfx + "wr", writes=[wr_b])
        S.dma("pool", wo[:], vw(w["xattn_w_o"]), pfx + "wr", writes=[wr_b])
        S.dma("sp", onesf[:], w["c_ones"], pfx + "cs", writes=[cs_b])
        S.op("dve", lambda e: e.tensor_copy(out=onesb[:], in_=onesf[:]), reads=[cs_b], writes=[cs_b])
        wkvv = vw(w["xattn_w_kv"])
        oi = 0
        for sq_i in range(2):
            t0 = sq_i * SEQ
            for m in range(2):
                S.dma("sp", mt[m][:], mem[sq_i * 256 + m * 128: sq_i * 256 + (m + 1) * 128, :], pfx + "mt%d" % m,
                      writes=[mt_b[m]])
                norm_transpose(C, st, mt[m][:], mt_b[m], C.gts[:, 3, :], memT, memT_b, m * 128, pfx)
            for piece in range(4):
                sl = piece % 2
                S.dma("pool", wkv[sl][:], wkvv[:, :, piece * 512:(piece + 1) * 512], pfx + "wkv%d" % sl,
                      writes=[wkv_b[sl]])
                if piece < 2:
                    for c4 in range(4):
                        ch = piece * 4 + c4
                        pb = ch % 2

                        def mk(e, sl=sl, c4=c4, pb=pb):
                            ins = None
                            for kc in range(8):
                                ins = e.matmul(C.ps[pb][:, 0:256], lhsT=wkv[sl][:, kc, c4 * 128:(c4 + 1) * 128],
                                               rhs=memT[:, kc, :], start=(kc == 0), stop=(kc == 7))
                            return ins
                        S.op("pe", mk, reads=[wkv_b[sl], memT_b], writes=[C.psb[pb]])
                        S.op("act", lambda e, ch=ch, pb=pb: e.activation(out=kTx[:, ch, :], in_=C.ps[pb][:, 0:256],
                                                                         func=AF.Copy, scale=float(256 ** -0.5)),
                             reads=[C.psb[pb]], writes=[kTx_b])
                else:
                    half = piece - 2
                    for mc in range(2):
                        pb = mc

                        def mv_(e, sl=sl, mc=mc, pb=pb):
                            ins = None
                            for kc in range(8):
                                ins = e.matmul(C.ps[pb][:], lhsT=memT[:, kc, mc * 128:(mc + 1) * 128],
                                               rhs=wkv[sl][:, kc, :], start=(kc == 0), stop=(kc == 7))
                            return ins
                        S.op("pe", mv_, reads=[wkv_b[sl], memT_b], writes=[C.psb[pb]])
                        S.op("act", lambda e, mc=mc, pb=pb, half=half: e.activation(
                            out=vx[:, mc, half * 512:(half + 1) * 512], in_=C.ps[pb][:], func=AF.Copy),
                             reads=[C.psb[pb]], writes=[vx_b])
            for s4 in range(4):
                ts0 = t0 + s4 * 512
                for c in range(4):
                    S.dma("sp", ycat[:, c, :], SC["ymT"][c, :, ts0:ts0 + 512], pfx + "yc", writes=[ycat_b])
                    S.dma("sp", ycat[:, 4 + c, :], SC["ydT"][c, :, ts0:ts0 + 512], pfx + "yc", writes=[ycat_b])
                for j in range(4):
                    S.dma("sp", h2t[:, j, :], h1[ts0 + j * 128: ts0 + (j + 1) * 128, :], pfx + "h%d" % j,
                          writes=[h2t_b[j]])
                for j in range(4):
                    for half in range(2):
                        pb = half

                        def mo_(e, j=j, half=half, pb=pb):
                            ins = None
                            for kc in range(8):
                                ins = e.matmul(C.ps[pb][:], lhsT=ycat[:, kc, j * 128:(j + 1) * 128],
                                               rhs=wout[:, kc, half * 512:(half + 1) * 512], start=(kc == 0),
                                               stop=(kc == 7))
                            return ins
                        S.op("pe", mo_, reads=[ycat_b, wr_b], writes=[C.psb[pb]])
                        S.op("dve", lambda e, j=j, half=half, pb=pb: e.tensor_tensor(
                            out=h2t[:, j, half * 512:(half + 1) * 512], in0=C.ps[pb][:],
                            in1=h2t[:, j, half * 512:(half + 1) * 512], op=ALU.add),
                             reads=[C.psb[pb]], writes=[h2t_b[j]])
                    norm_transpose(C, st, h2t[:, j, :], h2t_b[j], C.gts[:, 2, :], u3T, u3T_b, j * 128, pfx)
                for ch in range(8):
                    pb = ch % 2

                    def mq_(e, ch=ch, pb=pb):
                        ins = None
                        for kc in range(8):
                            ins = e.matmul(C.ps[pb][:], lhsT=wq[:, kc, ch * 128:(ch + 1) * 128], rhs=u3T[:, kc, :],
                                           start=(kc == 0), stop=(kc == 7))
                        return ins
                    S.op("pe", mq_, reads=[wr_b, u3T_b], writes=[C.psb[pb]])
                    S.op("act", lambda e, ch=ch, pb=pb: e.activation(out=qTx[:, ch, :], in_=C.ps[pb][:], func=AF.Copy),
                         reads=[C.psb[pb]], writes=[qTx_b])
                for h in range(4):
                    for mc in range(2):
                        pb = 3 + mc

                        def ml_(e, h=h, mc=mc, pb=pb):
                            ins = None
                            for dc in range(2):
                                ins = e.matmul(C.ps[pb][:], lhsT=kTx[:, 2 * h + dc, mc * 128:(mc + 1) * 128],
                                               rhs=qTx[:, 2 * h + dc, :], start=(dc == 0), stop=(dc == 1))
                            return ins
                        S.op("pe", ml_, reads=[kTx_b, qTx_b], writes=[C.psb[pb]])
                        S.op("act", lambda e, mc=mc, pb=pb: e.activation(out=pTx[mc][:], in_=C.ps[pb][:], func=AF.Exp),
                             reads=[C.psb[pb]], writes=[pTx_b[mc]])

                    def md_(e):
                        e.matmul(C.ps[7][:], lhsT=onesb[:], rhs=pTx[0][:], start=True, stop=False)
                        return e.matmul(C.ps[7][:], lhsT=onesb[:], rhs=pTx[1][:], start=False, stop=True)
                    S.op("pe", md_, reads=[cs_b] + pTx_b, writes=[C.psb[7]])
                    S.op("dve", lambda e: e.reciprocal(out=rec[:], in_=C.ps[7][:]), reads=[C.psb[7]], writes=[rec_b])
                    for dc in range(2):
                        pb = 5 + dc

                        def mo2(e, h=h, dc=dc, pb=pb):
                            ins = None
                            for mc in range(2):
                                ins = e.matmul(C.ps[pb][:], lhsT=vx[:, mc, (2 * h + dc) * 128:(2 * h + dc + 1) * 128],
                                               rhs=pTx[mc][:], start=(mc == 0), stop=(mc == 1))
                            return ins
                        S.op("pe", mo2, reads=[vx_b] + pTx_b, writes=[C.psb[pb]])
                        S.op("dve", lambda e, h=h, dc=dc, pb=pb: e.tensor_tensor(
                            out=oTx[:, 2 * h + dc, :], in0=C.ps[pb][:], in1=rec[:], op=ALU.mult),
                             reads=[C.psb[pb], rec_b], writes=[oTx_b])
                for j in range(4):
                    o = oi % 2
                    oi += 1
                    for half in range(2):
                        pb = half

                        def mf_(e, j=j, half=half, pb=pb):
                            ins = None
                            for kc in range(8):
                                ins = e.matmul(C.ps[pb][:], lhsT=oTx[:, kc, j * 128:(j + 1) * 128],
                                               rhs=wo[:, kc, half * 512:(half + 1) * 512], start=(kc == 0),
                                               stop=(kc == 7))
                            return ins
                        S.op("pe", mf_, reads=[oTx_b, wr_b], writes=[C.psb[pb]])
                        S.op("dve", lambda e, j=j, half=half, pb=pb, o=o: e.tensor_tensor(
                            out=ot[o][:, half * 512:(half + 1) * 512], in0=C.ps[pb][:],
                            in1=h2t[:, j, half * 512:(half + 1) * 512], op=ALU.add),
                             reads=[C.psb[pb], h2t_b[j]], writes=[ot_b[o]])
                    S.dma("sp", h3[ts0 + j * 128: ts0 + (j + 1) * 128, :], ot[o][:], pfx + "o%d" % o, reads=[ot_b[o]])
        W = C.work
        allb = [wr_b, cs_b, memT_b, kTx_b, vx_b, ycat_b, u3T_b, qTx_b, rec_b, oTx_b, W["junk_b"], W["ss_b"], W["xs_b"]] + \
            wkv_b + mt_b + h2t_b + pTx_b + ot_b
        phase_barrier(C, allb)


DBG_OUT = [("h1", [NTOK, D], F32), ("h3", [NTOK, D], F32), ("qkT", [8, 128, NTOK], BF16), ("dqT", [8, 128, NTOK], BF16),
           ("iqT", [4, 128, NTOK], BF16), ("v", [NTOK, 512], BF16), ("og", [NTOK, 512], BF16),
           ("ckv", [NTOK, 128], BF16), ("small", [NTOK, 16], F32), ("ckvT", [128, NTOK], BF16),
           ("kiT", [64, NTOK], BF16), ("ymT", [4, 128, NTOK], BF16), ("ydT", [4, 128, NTOK], BF16)]


def build(stop):
    nc = bass.Bass("TRN2", target_bir_lowering=False)
    C = Ctx()
    C.nc = nc

    def din(name, shape):
        return nc.dram_tensor(name, shape, F32, kind="ExternalInput").ap()

    x = din("x", [NTOK, D])
    mem = din("mem", [512, D])
    w = {}
    for name, shape in [("ffn1_w_gate", [D, DFF]), ("ffn1_w_up", [D, DFF]), ("ffn1_w_down", [DFF, D]),
                        ("ffn2_w_gate", [D, DFF]), ("ffn2_w_up", [D, DFF]), ("ffn2_w_down", [DFF, D]),
                        ("w_in", [D, 3792]), ("w_out", [D, D]), ("xattn_w_q", [D, D]),
                        ("xattn_w_kv", [D, 2 * D]), ("xattn_w_o", [D, D]),
                        ("gts", [128, 5, 8]), ("fin_g_bc", [128, D]), ("hg_bc", [128, 512]),
                        ("kvg_bc", [128, 128]), ("idxg_bc", [128, 64]), ("gbias_bc", [128, 8]),
                        ("convw", [128, 8, 4]), ("convb", [128, 8]), ("wuv", [128, 8, 64]),
                        ("c_ident", [128, 128]), ("c_triu", [128, 128]), ("c_ones", [128, 128]),
                        ("c_cneg", [128, 128])]:
        w[name] = din(name, shape)
    y = nc.dram_tensor("y", [NTOK, D], F32, kind="ExternalOutput").ap()
    SC = {}
    for name, shape, dt in DBG_OUT:
        kind = "ExternalOutput" if stop == 9 else "Internal"
        SC[name] = nc.dram_tensor("s_" + name, shape, dt, kind=kind).ap()
    h1, h3 = SC["h1"], SC["h3"]

    with ExitStack() as gst:
        S = Sync(nc, gst)
        C.S = S
        C.ps = [gst.enter_context(nc.psum_tensor("psb%d" % i, [128, 512], F32)) for i in range(8)]
        C.psb = [Buf() for _ in range(8)]
        C.psv_bf = [p[:].bitcast(BF16).rearrange("p (a b) -> p a b", b=128) for p in C.ps]
        cst = Buf()
        identf = gst.enter_context(nc.sbuf_tensor("identf", [128, 128], F32))
        C.ident = gst.enter_context(nc.sbuf_tensor("ident", [128, 128], BF16))
        C.gts = gst.enter_context(nc.sbuf_tensor("gts_sb", [128, 5, 8], F32))
        C.eps_t = gst.enter_context(nc.sbuf_tensor("eps_t", [128, 1], F32))
        S.dma("sp", identf[:], w["c_ident"], "c0", writes=[cst])
        S.dma("sp", C.gts[:], w["gts"], "c1", writes=[cst])
        S.op("dve", lambda e: e.tensor_copy(out=C.ident[:], in_=identf[:]), reads=[cst], writes=[cst])
        S.op("dve", lambda e: e.memset(C.eps_t[:], EPS), reads=[], writes=[cst])
        phase_barrier(C, [cst])

        ffn_phase(C, "f1", x, h1, C.gts[:, 0, :], w["ffn1_w_gate"], w["ffn1_w_up"], w["ffn1_w_down"])
        inproj_phase(C, h1, w["w_in"], SC, w)
        mlstm_phase(C, SC, w)
        dsa_phase(C, SC, w)
        post_phase(C, h1, mem, h3, SC, w)
        ffn_phase(C, "f2", h3, y, C.gts[:, 4, :], w["ffn2_w_gate"], w["ffn2_w_up"], w["ffn2_w_down"],
                  fin_g=w["fin_g_bc"])
    return nc


def host_layout(inp):
    f = lambda a: np.ascontiguousarray(np.asarray(a, dtype=np.float32))
    sh = {}
    for k in ["ffn1_w_gate", "ffn1_w_up", "ffn1_w_down", "ffn2_w_gate", "ffn2_w_up", "ffn2_w_down",
              "w_in", "w_out", "xattn_w_q", "xattn_w_kv", "xattn_w_o"]:
        sh[k] = f(inp[k][0])
    gt = lambda g: f(np.asarray(g).reshape(8, 128).T)
    sh["gts"] = f(np.stack([gt(inp["ffn1_norm_g"][0]), gt(inp["mix_norm_g"][0]), gt(inp["xattn_norm_g"][0]),
                            gt(inp["mem_norm_g"][0]), gt(inp["ffn2_norm_g"][0])], axis=1))
    bc = lambda v: f(np.broadcast_to(np.asarray(v).reshape(1, -1), (128, np.asarray(v).size)))
    sh["fin_g_bc"] = bc(inp["final_norm_g"])
    sh["hg_bc"] = bc(inp["mlstm_head_norm_g"][0])
    sh["kvg_bc"] = bc(inp["dsa_kv_norm_g"][0])
    sh["idxg_bc"] = bc(inp["idx_k_norm_g"][0])
    sh["gbias_bc"] = bc(np.concatenate([np.asarray(inp["mlstm_i_bias"][0]), np.asarray(inp["mlstm_f_bias"][0])]))
    sh["convw"] = f(np.asarray(inp["mlstm_conv_w"][0]).reshape(4, 8, 128).transpose(2, 1, 0))
    sh["convb"] = f(np.asarray(inp["mlstm_conv_b"][0]).reshape(8, 128).T)
    sh["wuv"] = f(np.asarray(inp["dsa_w_uv"][0]).transpose(1, 0, 2))
    p = np.arange(128)
    sh["c_ident"] = f(np.eye(128))
    sh["c_triu"] = f(p[:, None] <= p[None, :])
    sh["c_ones"] = f(np.ones((128, 128)))
    sh["c_cneg"] = f(np.where(p[None, :] <= p[:, None], 0.0, NEG))
    return sh


def kernel(**inputs):
    stop = 9 if DEBUG_HOOK is not None else 0
    shared = host_layout(inputs)
    xs = np.asarray(inputs["x"], dtype=np.float32).reshape(8, NTOK, D)
    ms = np.asarray(inputs["mem"], dtype=np.float32).reshape(8, 512, D)
    nc = build(stop)
    in_maps = []
    for c in range(8):
        m = dict(shared)
        m["x"] = np.ascontiguousarray(xs[c])
        m["mem"] = np.ascontiguousarray(ms[c])
        in_maps.append(m)
    res = run_bass_kernel_spmd(nc, in_maps, core_ids=list(range(8)))
    if DEBUG_HOOK is not None:
        DEBUG_HOOK(res)
    out = np.stack([np.asarray(r["y"], dtype=np.float32) for r in res.results], axis=0)
    return out.reshape(16, SEQ, D)
```

```python
from contextlib import ExitStack
import numpy as np
import concourse.bass as bass
import concourse.mybir as mybir
from concourse.bass_utils import run_bass_kernel_spmd

F32 = mybir.dt.float32
BF16 = mybir.dt.bfloat16
AF = mybir.ActivationFunctionType
ALU = mybir.AluOpType
AX = mybir.AxisListType

NTOK = 4096
SEQ = 2048
D = 1024
DFF = 2816
NFC = 22
EPS = 1e-6
NEG = -1.0e30
DEBUG_HOOK = None


class Buf:
    __slots__ = ("w", "r")

    def __init__(self):
        self.w = None
        self.r = {}


class Sync:
    def __init__(self, nc, stack):
        self.nc = nc
        self.stack = stack
        self.engs = {"pe": nc.tensor, "act": nc.scalar, "dve": nc.vector, "pool": nc.gpsimd, "sp": nc.sync}
        self.sem = {k: stack.enter_context(nc.semaphore("s_" + k)) for k in ["pe", "act", "dve", "pool"]}
        self.cnt = {k: 0 for k in self.sem}
        self.waited = {k: {} for k in self.engs}
        self.dsem = {}

    def _wait(self, eng, toks):
        need = {}
        for t in toks:
            if t is None:
                continue
            key, sem, val, src = t
            if src == "pe" and eng == "pe":
                continue
            if self.waited[eng].get(key, 0) >= val:
                continue
            if key not in need or need[key][1] < val:
                need[key] = (sem, val)
        for key, (sem, val) in need.items():
            self.engs[eng].wait_ge(sem, val)
            self.waited[eng][key] = val

    @staticmethod
    def _collect(reads, writes):
        toks = []
        for b in reads:
            toks.append(b.w)
        for b in writes:
            toks.append(b.w)
            toks.extend(b.r.values())
        return toks

    @staticmethod
    def _update(tok, reads, writes):
        key = tok[0]
        for b in reads:
            o = b.r.get(key)
            if o is None or o[2] < tok[2]:
                b.r[key] = tok
        for b in writes:
            b.w = tok
            b.r = {}

    def op(self, eng, fn, reads=(), writes=()):
        self._wait(eng, self._collect(reads, writes))
        ins = fn(self.engs[eng])
        self.cnt[eng] += 1
        ins.then_inc(self.sem[eng], 1)
        tok = (eng, self.sem[eng], self.cnt[eng], eng)
        self._update(tok, reads, writes)
        return tok

    def dma(self, q, out, in_, sname, reads=(), writes=(), **kw):
        self._wait(q, self._collect(reads, writes))
        if sname not in self.dsem:
            self.dsem[sname] = [self.stack.enter_context(self.nc.semaphore("d_" + sname)), 0]
        d = self.dsem[sname]
        ins = self.engs[q].dma_start(out=out, in_=in_, **kw)
        d[1] += 16
        ins.then_inc(d[0], 16)
        tok = ("d_" + sname, d[0], d[1], None)
        self._update(tok, reads, writes)
        return tok

    def wait_all(self, eng, bufs):
        toks = []
        for b in bufs:
            toks.append(b.w)
            toks.extend(b.r.values())
        self._wait(eng, toks)


class Ctx:
    pass


def bc_mid(ap2, n):
    return ap2.unsqueeze(2).to_broadcast([ap2.shape[0], ap2.shape[1], n])


def norm_transpose(C, st_sb, xt_ap, xt_b, gT_ap, dstT, dst_b, col0, tag, ps_bank=2):
    nc, S = C.nc, C.S
    W = C.work
    i = W["i"] % 3
    W["i"] += 1
    junk, junk_b, ss, ss_b, xs, xs_b = W["junk3"][i], W["junk3_b"][i], W["ss3"][i], W["ss3_b"][i], W["xs3"][i], W["xs3_b"][i]
    S.op("act", lambda e: e.activation(out=junk[:], in_=xt_ap, func=AF.Square, accum_out=ss[:, 0:1]),
         reads=[xt_b], writes=[junk_b, ss_b])
    S.op("act", lambda e: e.activation(out=ss[:, 1:2], in_=ss[:, 0:1], func=AF.Sqrt,
                                       bias=C.eps_t[:, 0:1], scale=1.0 / D),
         reads=[], writes=[ss_b])
    S.op("dve", lambda e: e.reciprocal(out=ss[:, 2:3], in_=ss[:, 1:2]), reads=[], writes=[ss_b])
    S.op("pool", lambda e: e.tensor_scalar(out=xs[:], in0=xt_ap, scalar1=ss[:, 2:3], scalar2=0.0,
                                           op0=ALU.mult, op1=ALU.add),
         reads=[xt_b, ss_b], writes=[xs_b])
    pbk = ps_bank if (W["i"] % 2 == 0) else W["alt_bank"]
    tp = C.psv_bf[pbk]

    def tr(e):
        ins = None
        for kc in range(8):
            ins = e.transpose(out=tp[:, kc, :], in_=xs[:, kc * 128:(kc + 1) * 128], identity=C.ident[:])
        return ins
    S.op("pe", tr, reads=[xs_b], writes=[C.psb[pbk]])
    S.op("dve", lambda e: e.tensor_tensor(out=dstT[:, :, col0:col0 + 128], in0=tp[:, :, :],
                                          in1=bc_mid(gT_ap, 128), op=ALU.mult),
         reads=[C.psb[pbk]], writes=[dst_b])


def make_work(C, st, pfx, alt_bank=7):
    nc = C.nc
    W = {"i": 0, "alt_bank": alt_bank}
    W["junk3"] = [st.enter_context(nc.sbuf_tensor(pfx + "w_junk%d" % i, [128, 1024], BF16)) for i in range(3)]
    W["junk3_b"] = [Buf() for _ in range(3)]
    W["ss3"] = [st.enter_context(nc.sbuf_tensor(pfx + "w_ss%d" % i, [128, 4], F32)) for i in range(3)]
    W["ss3_b"] = [Buf() for _ in range(3)]
    W["xs3"] = [st.enter_context(nc.sbuf_tensor(pfx + "w_xs%d" % i, [128, 1024], BF16)) for i in range(3)]
    W["xs3_b"] = [Buf() for _ in range(3)]
    W["junk"], W["junk_b"], W["ss"], W["ss_b"], W["xs_b"] = W["junk3"][0], W["junk3_b"][0], W["ss3"][0], W["ss3_b"][0], W["xs3_b"][0]
    C.work = W
    C.work_bufs = W["junk3_b"] + W["ss3_b"] + W["xs3_b"]


def ffn_phase(C, pfx, src, dst, gT_ap, wg_d, wu_d, wd_d, fin_g=None):
    nc, S = C.nc, C.S
    with ExitStack() as st:
        def sb(name, shape, dt):
            return st.enter_context(nc.sbuf_tensor(pfx + name, shape, dt))
        xres = sb("xres", [128, 8, 1024], F32)
        xres_b = [Buf() for _ in range(8)]
        xnT = sb("xnT", [128, 8, 1024], BF16)
        xnT_b = Buf()
        hT = sb("hT", [128, NFC, 1024], BF16)
        hT_b = [Buf(), Buf()]
        wd = sb("wd", [128, NFC, 1024], BF16)
        wd_b = Buf()
        wg = [sb("wg%d" % i, [128, 8, 256], BF16) for i in range(2)]
        wu = [sb("wu%d" % i, [128, 8, 256], BF16) for i in range(2)]
        wg_b = [Buf(), Buf()]
        wu_b = [Buf(), Buf()]
        sg = [sb("sg%d" % i, [128, 512], F32) for i in range(2)]
        sg_b = [Buf(), Buf()]
        ot = [sb("ot%d" % i, [128, 1024], F32) for i in range(2)]
        ot_b = [Buf(), Buf()]
        make_work(C, st, pfx)
        W = C.work
        if fin_g is not None:
            fing = sb("fing", [128, 1024], F32)
            fing_b = Buf()
            S.dma("sp", fing[:], fin_g, pfx + "fing", writes=[fing_b])
            ot2 = [sb("ot2%d" % i, [128, 1024], F32) for i in range(2)]
            ot2_b = [Buf(), Buf()]

        wgv = wg_d.rearrange("(kc p) n -> p kc n", p=128)
        wuv = wu_d.rearrange("(kc p) n -> p kc n", p=128)
        wdv = wd_d.rearrange("(fc p) n -> p fc n", p=128)
        for i in range(2):
            S.dma("pool", wd[:, i * 11:(i + 1) * 11, :], wdv[:, i * 11:(i + 1) * 11, :], pfx + "wd", writes=[wd_b])
        blk = 0
        oi = 0
        for t in range(4):
            tok0 = t * 1024
            for j in range(8):
                S.dma("sp", xres[:, j, :], src[tok0 + j * 128: tok0 + (j + 1) * 128, :], pfx + "x%d" % j,
                      writes=[xres_b[j]])
            for j in range(8):
                norm_transpose(C, st, xres[:, j, :], xres_b[j], gT_ap, xnT, xnT_b, j * 128, pfx)
            for fb in range(11):
                sl = blk % 2
                blk += 1
                S.dma("pool", wg[sl][:], wgv[:, :, fb * 256:(fb + 1) * 256], pfx + "wg%d" % sl, writes=[wg_b[sl]])
                S.dma("pool", wu[sl][:], wuv[:, :, fb * 256:(fb + 1) * 256], pfx + "wu%d" % sl, writes=[wu_b[sl]])
                for fcl in range(2):
                    fc = fb * 2 + fcl
                    for s in range(2):
                        pi = (fc * 2 + s) % 2
                        pg, pu = C.ps[3 + pi], C.ps[5 + pi]

                        def mm(e, wt=wg[sl], po=pg, fcl=fcl, s=s):
                            ins = None
                            for kc in range(8):
                                ins = e.matmul(po[:], lhsT=wt[:, kc, fcl * 128:(fcl + 1) * 128],
                                               rhs=xnT[:, kc, s * 512:(s + 1) * 512],
                                               start=(kc == 0), stop=(kc == 7))
                            return ins
                        S.op("pe", mm, reads=[wg_b[sl], xnT_b], writes=[C.psb[3 + pi]])
                        S.op("pe", lambda e, mm=mm, wt=wu[sl], po=pu: mm(e, wt, po),
                             reads=[wu_b[sl], xnT_b], writes=[C.psb[5 + pi]])
                        S.op("act", lambda e, pg=pg, pi=pi: e.activation(out=sg[pi][:], in_=pg[:], func=AF.Silu),
                             reads=[C.psb[3 + pi]], writes=[sg_b[pi]])
                        S.op("dve", lambda e, pu=pu, pi=pi, fc=fc, s=s: e.tensor_tensor(
                            out=hT[:, fc, s * 512:(s + 1) * 512], in0=pu[:], in1=sg[pi][:], op=ALU.mult),
                             reads=[C.psb[5 + pi], sg_b[pi]], writes=[hT_b[s]])
            for j in range(8):
                o = oi % 2
                oi += 1
                for half in range(2):
                    pb = 0 + half
                    po = C.ps[pb]

                    def mmd(e, po=po, j=j, half=half):
                        ins = None
                        for fc in range(NFC):
                            ins = e.matmul(po[:], lhsT=hT[:, fc, j * 128:(j + 1) * 128],
                                           rhs=wd[:, fc, half * 512:(half + 1) * 512],
                                           start=(fc == 0), stop=(fc == NFC - 1))
                        return ins
                    S.op("pe", mmd, reads=[hT_b[j // 4], wd_b], writes=[C.psb[pb]])
                    S.op("dve", lambda e, po=po, o=o, j=j, half=half: e.scalar_tensor_tensor(
                        out=ot[o][:, half * 512:(half + 1) * 512], in0=po[:], scalar=0.5,
                        in1=xres[:, j, half * 512:(half + 1) * 512], op0=ALU.mult, op1=ALU.add),
                         reads=[C.psb[pb], xres_b[j]], writes=[ot_b[o]])
                if fin_g is None:
                    S.dma("sp", dst[tok0 + j * 128: tok0 + (j + 1) * 128, :], ot[o][:], pfx + "o%d" % o,
                          reads=[ot_b[o]])
                else:
                    S.op("act", lambda e, o=o: e.activation(out=W["junk"][:], in_=ot[o][:], func=AF.Square,
                                                            accum_out=W["ss"][:, 0:1]),
                         reads=[ot_b[o]], writes=[W["junk_b"], W["ss_b"]])
                    S.op("act", lambda e: e.activation(out=W["ss"][:, 1:2], in_=W["ss"][:, 0:1], func=AF.Sqrt,
                                                       bias=C.eps_t[:, 0:1], scale=1.0 / D),
                         reads=[], writes=[W["ss_b"]])
                    S.op("dve", lambda e: e.reciprocal(out=W["ss"][:, 2:3], in_=W["ss"][:, 1:2]),
                         reads=[], writes=[W["ss_b"]])
                    S.op("dve", lambda e, o=o: e.scalar_tensor_tensor(
                        out=ot2[o][:], in0=ot[o][:], scalar=W["ss"][:, 2:3], in1=fing[:],
                        op0=ALU.mult, op1=ALU.mult),
                         reads=[ot_b[o], W["ss_b"], fing_b], writes=[ot2_b[o]])
                    S.dma("sp", dst[tok0 + j * 128: tok0 + (j + 1) * 128, :], ot2[o][:], pfx + "o%d" % o,
                          reads=[ot2_b[o]])
        allb = xres_b + [xnT_b, wd_b] + hT_b + wg_b + wu_b + sg_b + ot_b + [W["junk_b"], W["ss_b"], W["xs_b"]]
        if fin_g is not None:
            allb += ot2_b + [fing_b]
        phase_barrier(C, allb)


def phase_barrier(C, bufs):
    S = C.S
    bufs = list(bufs) + C.psb + list(getattr(C, "work_bufs", []))
    for e in ["sp", "pool", "act", "dve", "pe"]:
        S.wait_all(e, bufs)


def inproj_phase(C, h1, w_in, SC, w):
    nc, S = C.nc, C.S
    pfx = "ip"
    with ExitStack() as st:
        def sb(name, shape, dt):
            return st.enter_context(nc.sbuf_tensor(pfx + name, shape, dt))
        make_work(C, st, pfx)
        W = C.work
        uT = sb("uT", [128, 8, SEQ], BF16)
        uT_b = Buf()
        ht = [sb("ht%d" % i, [128, 1024], F32) for i in range(3)]
        ht_b = [Buf() for _ in range(3)]
        wf = [sb("wf%d" % i, [128, 8, 256], BF16) for i in range(2)]
        wf_b = [Buf(), Buf()]
        wA = sb("wA", [128, 8, 512], BF16)
        wB = sb("wB", [128, 8, 512], BF16)
        wC = sb("wC", [128, 8, 208], BF16)
        wt_b = Buf()
        zc = sb("zc", [128, 3 + SEQ], F32)
        zc_b = Buf()
        acc = sb("acc", [128, SEQ], F32)
        acc_b = Buf()
        fo = [sb("fo%d" % i, [128, SEQ], BF16) for i in range(2)]
        fo_b = [Buf(), Buf()]
        vt = [sb("vt%d" % i, [128, 512], BF16) for i in range(2)]
        vt_b = [Buf(), Buf()]
        og = [sb("og%d" % i, [128, 512], BF16) for i in range(2)]
        og_b = [Buf(), Buf()]
        sm = [sb("sm%d" % i, [128, 16], F32) for i in range(2)]
        sm_b = [Buf(), Buf()]
        ck = [sb("ck%d" % i, [128, 128], BF16) for i in range(2)]
        ck_b = [Buf(), Buf()]
        ki = [sb("ki%d" % i, [128, 64], BF16) for i in range(2)]
        ki_b = [Buf(), Buf()]
        st2 = sb("st2", [128, 8], F32)
        st2_b = Buf()
        ckvT = sb("ckvT", [128, SEQ], BF16)
        ckvT_b = Buf()
        kiT = sb("kiT", [64, SEQ], BF16)
        kiT_b = Buf()
        cw = sb("cw", [128, 8, 4], F32)
        cb = sb("cb", [128, 8], F32)
        kvg = sb("kvg", [128, 128], F32)
        idg = sb("idg", [128, 64], F32)
        gbi = sb("gbi", [128, 8], F32)
        cs_b = Buf()
        for t_, s_ in [(cw, "convw"), (cb, "convb"), (kvg, "kvg_bc"), (idg, "idxg_bc"), (gbi, "gbias_bc")]:
            S.dma("sp", t_[:], w[s_], pfx + "cs", writes=[cs_b])
        S.op("dve", lambda e: e.memset(zc[:, 0:3], 0.0), writes=[zc_b])

        wv = w_in.rearrange("(kc p) n -> p kc n", p=128)
        S.dma("pool", wA[:], wv[:, :, 1024:1536], pfx + "wt", writes=[wt_b])
        S.dma("pool", wB[:], wv[:, :, 1544:2056], pfx + "wt", writes=[wt_b])
        S.dma("pool", wC[:, :, 0:8], wv[:, :, 1536:1544], pfx + "wt", writes=[wt_b])
        S.dma("pool", wC[:, :, 8:136], wv[:, :, 3080:3208], pfx + "wt", writes=[wt_b])
        S.dma("pool", wC[:, :, 136:208], wv[:, :, 3720:3792], pfx + "wt", writes=[wt_b])
        gT_ap = C.gts[:, 1, :]
        fm = [(c * 128, "conv", c) for c in range(8)] + [(2056 + c * 128, "dq", c) for c in range(8)] + \
             [(3208 + c * 128, "iq", c) for c in range(4)]
        blk = 0
        foi = 0
        ti = 0
        for sq in range(2):
            t0 = sq * SEQ
            for j in range(16):
                sl = (sq * 16 + j) % 3
                S.dma("sp", ht[sl][:], h1[t0 + j * 128: t0 + (j + 1) * 128, :], pfx + "h%d" % sl, writes=[ht_b[sl]])
                norm_transpose(C, st, ht[sl][:], ht_b[sl], gT_ap, uT, uT_b, j * 128, pfx)
            for j in range(16):
                o = ti % 2
                ti += 1
                tk = slice(t0 + j * 128, t0 + (j + 1) * 128)
                for wt, n, pb in [(wA, 512, 3), (wB, 512, 4), (wC, 208, 5)]:
                    def mm(e, wt=wt, n=n, pb=pb, j=j):
                        ins = None
                        for kc in range(8):
                            ins = e.matmul(C.ps[pb][:, 0:n], lhsT=uT[:, kc, j * 128:(j + 1) * 128], rhs=wt[:, kc, :],
                                           start=(kc == 0), stop=(kc == 7))
                        return ins
                    S.op("pe", mm, reads=[uT_b, wt_b], writes=[C.psb[pb]])
                S.op("act", lambda e, o=o: e.activation(out=vt[o][:], in_=C.ps[3][:], func=AF.Copy),
                     reads=[C.psb[3]], writes=[vt_b[o]])
                S.dma("sp", SC["v"][tk, :], vt[o][:], pfx + "v%d" % o, reads=[vt_b[o]])
                S.op("act", lambda e, o=o: e.activation(out=og[o][:], in_=C.ps[4][:], func=AF.Sigmoid),
                     reads=[C.psb[4]], writes=[og_b[o]])
                S.dma("sp", SC["og"][tk, :], og[o][:], pfx + "og%d" % o, reads=[og_b[o]])
                pc = C.ps[5]
                S.op("dve", lambda e, o=o: e.tensor_tensor(out=sm[o][:, 0:8], in0=pc[:, 0:8], in1=gbi[:], op=ALU.add),
                     reads=[C.psb[5], cs_b], writes=[sm_b[o]])
                S.op("dve", lambda e, o=o: e.tensor_scalar(out=sm[o][:, 8:16], in0=pc[:, 200:208],
                                                           scalar1=float(8 ** -0.5 * 64 ** -0.5), scalar2=None,
                                                           op0=ALU.mult),
                     reads=[C.psb[5]], writes=[sm_b[o]])
                S.dma("sp", SC["small"][tk, :], sm[o][:], pfx + "sm%d" % o, reads=[sm_b[o]])
                S.op("act", lambda e: e.activation(out=W["junk"][:, 0:128], in_=pc[:, 8:136], func=AF.Square,
                                                   accum_out=st2[:, 0:1]),
                     reads=[C.psb[5]], writes=[W["junk_b"], st2_b])
                S.op("act", lambda e: e.activation(out=W["junk"][:, 128:192], in_=pc[:, 136:200], func=AF.Square,
                                                   accum_out=st2[:, 1:2]),
                     reads=[C.psb[5]], writes=[W["junk_b"], st2_b])
                S.op("act", lambda e: e.activation(out=st2[:, 2:3], in_=st2[:, 0:1], func=AF.Sqrt,
                                                   bias=C.eps_t[:, 0:1], scale=1.0 / 128), writes=[st2_b])
                S.op("act", lambda e: e.activation(out=st2[:, 3:4], in_=st2[:, 1:2], func=AF.Sqrt,
                                                   bias=C.eps_t[:, 0:1], scale=1.0 / 64), writes=[st2_b])
                S.op("dve", lambda e: e.reciprocal(out=st2[:, 4:6], in_=st2[:, 2:4]), writes=[st2_b])
                S.op("dve", lambda e, o=o: e.scalar_tensor_tensor(out=ck[o][:], in0=pc[:, 8:136], scalar=st2[:, 4:5],
                                                                   in1=kvg[:], op0=ALU.mult, op1=ALU.mult),
                     reads=[C.psb[5], st2_b, cs_b], writes=[ck_b[o]])
                S.op("dve", lambda e, o=o: e.scalar_tensor_tensor(out=ki[o][:], in0=pc[:, 136:200], scalar=st2[:, 5:6],
                                                                   in1=idg[:], op0=ALU.mult, op1=ALU.mult),
                     reads=[C.psb[5], st2_b, cs_b], writes=[ki_b[o]])
                S.dma("sp", SC["ckv"][tk, :], ck[o][:], pfx + "ck%d" % o, reads=[ck_b[o]])
                tp = C.psv_bf[6]

                def tr(e, o=o):
                    e.transpose(out=tp[:, 0, :], in_=ck[o][:], identity=C.ident[:])
                    return e.transpose(out=tp[0:64, 1, :], in_=ki[o][:], identity=C.ident[:])
                S.op("pe", tr, reads=[ck_b[o], ki_b[o]], writes=[C.psb[6]])
                S.op("act", lambda e, j=j: e.activation(out=ckvT[:, j * 128:(j + 1) * 128], in_=tp[:, 0, :], func=AF.Copy),
                     reads=[C.psb[6]], writes=[ckvT_b])
                S.op("act", lambda e, j=j: e.activation(out=kiT[:, j * 128:(j + 1) * 128], in_=tp[0:64, 1, :], func=AF.Copy),
                     reads=[C.psb[6]], writes=[kiT_b])
            S.dma("sp", SC["ckvT"][:, t0:t0 + SEQ], ckvT[:], pfx + "ckT", reads=[ckvT_b])
            S.dma("sp", SC["kiT"][:, t0:t0 + SEQ], kiT[:], pfx + "kiT", reads=[kiT_b])
            for ci, (c0, kind, dch) in enumerate(fm):
                if ci % 2 == 0:
                    sl = blk % 2
                    blk += 1
                    S.dma("pool", wf[sl][:], wv[:, :, c0:c0 + 256], pfx + "wf%d" % sl, writes=[wf_b[sl]])
                f = foi % 2
                foi += 1
                for s in range(4):
                    pb = s % 2

                    def mm(e, sl=sl, ci=ci, s=s, pb=pb):
                        ins = None
                        for kc in range(8):
                            ins = e.matmul(C.ps[pb][:], lhsT=wf[sl][:, kc, (ci % 2) * 128:(ci % 2 + 1) * 128],
                                           rhs=uT[:, kc, s * 512:(s + 1) * 512], start=(kc == 0), stop=(kc == 7))
                        return ins
                    S.op("pe", mm, reads=[wf_b[sl], uT_b], writes=[C.psb[pb]])
                    if kind == "conv":
                        S.op("act", lambda e, pb=pb, s=s: e.activation(out=zc[:, 3 + s * 512: 3 + (s + 1) * 512],
                                                                     in_=C.ps[pb][:], func=AF.Copy),
                             reads=[C.psb[pb]], writes=[zc_b])
                    elif kind == "dq":
                        S.op("act", lambda e, pb=pb, s=s, f=f: e.activation(out=fo[f][:, s * 512:(s + 1) * 512],
                                                                          in_=C.ps[pb][:], func=AF.Copy,
                                                                          scale=float(128 ** -0.5)),
                             reads=[C.psb[pb]], writes=[fo_b[f]])
                    else:
                        S.op("act", lambda e, pb=pb, s=s, f=f: e.activation(out=fo[f][:, s * 512:(s + 1) * 512],
                                                                          in_=C.ps[pb][:], func=AF.Copy),
                             reads=[C.psb[pb]], writes=[fo_b[f]])
                if kind == "conv":
                    S.op("dve", lambda e, dch=dch: e.tensor_scalar(out=acc[:], in0=zc[:, 0:SEQ], scalar1=cw[:, dch, 0:1],
                                                                    scalar2=cb[:, dch:dch + 1], op0=ALU.mult, op1=ALU.add),
                         reads=[zc_b, cs_b], writes=[acc_b])
                    for jj in range(1, 4):
                        S.op("dve", lambda e, dch=dch, jj=jj: e.scalar_tensor_tensor(
                            out=acc[:], in0=zc[:, jj:jj + SEQ], scalar=cw[:, dch, jj:jj + 1], in1=acc[:],
                            op0=ALU.mult, op1=ALU.add), reads=[zc_b, cs_b], writes=[acc_b])
                    S.op("act", lambda e, f=f: e.activation(out=fo[f][:], in_=acc[:], func=AF.Silu),
                         reads=[acc_b], writes=[fo_b[f]])
                    dstd = SC["qkT"][dch, :, t0:t0 + SEQ]
                elif kind == "dq":
                    dstd = SC["dqT"][dch, :, t0:t0 + SEQ]
                else:
                    dstd = SC["iqT"][dch, :, t0:t0 + SEQ]
                S.dma("sp", dstd, fo[f][:], pfx + "fo%d" % f, reads=[fo_b[f]])
        allb = [uT_b, wt_b, zc_b, acc_b, st2_b, ckvT_b, kiT_b, cs_b, W["junk_b"], W["ss_b"], W["xs_b"]] + ht_b + wf_b + \
            fo_b + vt_b + og_b + sm_b + ck_b + ki_b
        phase_barrier(C, allb)


def mlstm_phase(C, SC, w):
    nc, S = C.nc, C.S
    pfx = "ml"
    SC_ = float(128 ** -0.5)
    with ExitStack() as st:
        def sb(name, shape, dt):
            return st.enter_context(nc.sbuf_tensor(pfx + name, shape, dt))
        qT = sb("qT", [128, 4, SEQ], BF16)
        kT = sb("kT", [128, 4, SEQ], BF16)
        v = sb("v", [128, 16, 512], BF16)
        G = sb("G", [128, 16, 512], BF16)
        smt = sb("smt", [128, 16, 16], F32)
        ld_b = Buf()
        hg = sb("hg", [128, 512], F32)
        triu = sb("triu", [128, 128], F32)
        onesf = sb("onesf", [128, 128], F32)
        masku = sb("masku", [128, 128], F32)
        cs_b = Buf()
        ktm = sb("ktm", [128, 16, 512], BF16)
        ktm_b = Buf()
        lf = sb("lf", [128, 64], F32)
        ebp = sb("ebp", [128, 64], F32)
        eg = sb("eg", [128, 64], F32)
        ebL = sb("ebL", [128, 64], F32)
        pre_b = Buf()
        vaug = [sb("vaug%d" % i, [128, 4, 130], BF16) for i in range(2)]
        vaug_b = [Buf(), Buf()]
        Pm = [sb("Pm%d" % i, [128, 4, 128], BF16) for i in range(2)]
        Pm_b = [Buf(), Buf()]
        C32 = [sb("C32_%d" % h, [128, 130], F32) for h in range(4)]
        C32_b = [Buf() for _ in range(4)]
        Cbf = [sb("Cbf_%d" % h, [128, 130], BF16) for h in range(4)]
        Cbf_b = [Buf() for _ in range(4)]
        t4 = sb("t4", [128, 16], F32)
        t4_b = Buf()
        hv = sb("hv", [128, 4, 128], F32)
        hv_b = Buf()
        sq = sb("sq", [128, 4, 128], F32)
        sq_b = Buf()
        yb = [sb("yb%d" % i, [128, 512], BF16) for i in range(2)]
        yb_b = [Buf(), Buf()]
        ymT = sb("ymT", [128, 4, SEQ], BF16)
        ymT_b = Buf()
        S.dma("sp", hg[:], w["hg_bc"], pfx + "cs", writes=[cs_b])
        S.dma("sp", triu[:], w["c_triu"], pfx + "cs", writes=[cs_b])
        S.dma("sp", onesf[:], w["c_ones"], pfx + "cs", writes=[cs_b])
        S.dma("sp", masku[:], w["c_triu"], pfx + "cs", writes=[cs_b])
        vi = 0
        for sq_i in range(2):
            t0 = sq_i * SEQ
            for h in range(4):
                S.dma("sp", qT[:, h, :], SC["qkT"][h, :, t0:t0 + SEQ], pfx + "ld", writes=[ld_b])
                S.dma("sp", kT[:, h, :], SC["qkT"][4 + h, :, t0:t0 + SEQ], pfx + "ld", writes=[ld_b])
            S.dma("sp", v[:], SC["v"][t0:t0 + SEQ, :].rearrange("(c p) f -> p c f", p=128), pfx + "ld", writes=[ld_b])
            S.dma("sp", G[:], SC["og"][t0:t0 + SEQ, :].rearrange("(c p) f -> p c f", p=128), pfx + "ld", writes=[ld_b])
            S.dma("sp", smt[:], SC["small"][t0:t0 + SEQ, :].rearrange("(c p) f -> p c f", p=128), pfx + "ld",
                  writes=[ld_b])
            lf3 = lf[:].rearrange("p (c h) -> p c h", h=4)
            S.op("act", lambda e: e.activation(out=lf3, in_=smt[:, :, 4:8], func=AF.Exp, scale=-1.0),
                 reads=[ld_b], writes=[pre_b])
            S.op("dve", lambda e: e.tensor_scalar(out=lf[:], in0=lf[:], scalar1=1.0, scalar2=None, op0=ALU.add),
                 writes=[pre_b])
            S.op("act", lambda e: e.activation(out=lf[:], in_=lf[:], func=AF.Ln), writes=[pre_b])
            S.op("dve", lambda e: e.tensor_scalar(out=lf[:], in0=lf[:], scalar1=-1.0, scalar2=None, op0=ALU.mult),
                 writes=[pre_b])
            pbb, pbl = C.ps[6], C.ps[7]
            S.op("pe", lambda e: e.matmul(pbb[:, 0:64], lhsT=triu[:], rhs=lf[:], start=True, stop=True),
                 reads=[pre_b, cs_b], writes=[C.psb[6]])
            S.op("pe", lambda e: e.matmul(pbl[:, 0:64], lhsT=onesf[:], rhs=lf[:], start=True, stop=True),
                 reads=[pre_b, cs_b], writes=[C.psb[7]])
            S.op("act", lambda e: e.activation(out=ebp[:], in_=pbb[:, 0:64], func=AF.Exp),
                 reads=[C.psb[6]], writes=[pre_b])
            S.op("dve", lambda e: e.tensor_scalar(out=ebp[:], in0=ebp[:], scalar1=SC_, scalar2=None, op0=ALU.mult),
                 writes=[pre_b])
            eg3 = eg[:].rearrange("p (c h) -> p c h", h=4)
            S.op("dve", lambda e: e.tensor_tensor(out=eg3, in0=smt[:, :, 0:4],
                                                  in1=pbb[:, 0:64].rearrange("p (c h) -> p c h", h=4), op=ALU.subtract),
                 reads=[ld_b, C.psb[6]], writes=[pre_b])
            S.op("act", lambda e: e.activation(out=eg[:], in_=eg[:], func=AF.Exp), writes=[pre_b])
            S.op("act", lambda e: e.activation(out=ebL[:], in_=pbl[:, 0:64], func=AF.Exp),
                 reads=[C.psb[7]], writes=[pre_b])
            S.op("dve", lambda e: e.tensor_tensor(out=G[:], in0=G[:], in1=hg[:].unsqueeze(1).to_broadcast([128, 16, 512]),
                                                  op=ALU.mult), reads=[cs_b], writes=[ld_b])
            tp = C.psv_bf[5]
            for c in range(16):
                def trk(e, c=c):
                    ins = None
                    for h in range(4):
                        ins = e.transpose(out=tp[:, h, :], in_=kT[:, h, c * 128:(c + 1) * 128], identity=C.ident[:])
                    return ins
                S.op("pe", trk, reads=[ld_b], writes=[C.psb[5]])
                S.op("act", lambda e, c=c: e.activation(out=ktm[:, c, :], in_=tp[:, 0:4, :].rearrange("p a b -> p (a b)"),
                                                        func=AF.Copy),
                     reads=[C.psb[5]], writes=[ktm_b])
            def prefix(c):
                cs4 = slice(c * 4, (c + 1) * 4)
                tsl = slice(c * 128, (c + 1) * 128)
                vb = c % 2
                ub = 3 if c % 2 == 0 else 6
                S.op("dve", lambda e: e.tensor_tensor(
                    out=vaug[vb][:, :, 0:128], in0=v[:, c, :].rearrange("p (h d) -> p h d", d=128),
                    in1=bc_mid(eg[:, cs4], 128), op=ALU.mult), reads=[ld_b, pre_b], writes=[vaug_b[vb]])
                S.op("dve", lambda e: e.tensor_copy(out=vaug[vb][:, :, 128:129], in_=eg[:, cs4].unsqueeze(2)),
                     reads=[pre_b], writes=[vaug_b[vb]])
                pa = C.ps[0]

                def mst(e):
                    ins = None
                    for h in range(4):
                        ins = e.matmul(pa[:, h * 128:(h + 1) * 128], lhsT=kT[:, h, tsl], rhs=qT[:, h, tsl],
                                       start=True, stop=True)
                    return ins
                S.op("pe", mst, reads=[ld_b], writes=[C.psb[0]])
                S.op("dve", lambda e: e.tensor_tensor(
                    out=Pm[vb][:], in0=pa[:].rearrange("p (h j) -> p h j", j=128),
                    in1=masku[:].unsqueeze(1).to_broadcast([128, 4, 128]), op=ALU.mult),
                     reads=[C.psb[0], cs_b], writes=[Pm_b[vb]])
                for k2 in range(2):
                    def mu(e, k2=k2):
                        ins = None
                        for hh in range(2):
                            h = k2 * 2 + hh
                            ins = e.matmul(C.ps[ub + k2][:, hh * 256: hh * 256 + 129],
                                           lhsT=ktm[:, c, h * 128:(h + 1) * 128], rhs=vaug[vb][:, h, 0:129],
                                           start=True, stop=True)
                        return ins
                    S.op("pe", mu, reads=[ktm_b, vaug_b[vb]], writes=[C.psb[ub + k2]])

            def rest(c):
                cs4 = slice(c * 4, (c + 1) * 4)
                tsl = slice(c * 128, (c + 1) * 128)
                vb = c % 2
                ub = 3 if c % 2 == 0 else 6
                for k2 in range(2):
                    def mr(e, k2=k2):
                        ins = None
                        for hh in range(2):
                            h = k2 * 2 + hh
                            o_ = C.ps[1 + k2][:, hh * 256: hh * 256 + 129]
                            if c > 0:
                                e.matmul(o_, lhsT=qT[:, h, tsl], rhs=Cbf[h][:, 0:129], start=True, stop=False)
                            ins = e.matmul(o_, lhsT=Pm[vb][:, h, :], rhs=vaug[vb][:, h, 0:129], start=(c == 0), stop=True)
                        return ins
                    S.op("pe", mr, reads=[ld_b, Pm_b[vb], vaug_b[vb]] + [Cbf_b[k2 * 2], Cbf_b[k2 * 2 + 1]],
                         writes=[C.psb[1 + k2]])
                R3 = [C.ps[1 + k2][:].rearrange("p (a b) -> p a b", b=256) for k2 in range(2)]
                U3 = [C.ps[ub + k2][:].rearrange("p (a b) -> p a b", b=256) for k2 in range(2)]
                for h in range(4):
                    k2, hh = h // 2, h % 2
                    if c == 0:
                        S.op("act", lambda e, h=h, k2=k2, hh=hh: e.activation(out=C32[h][:, 0:129],
                                                                              in_=U3[k2][:, hh, 0:129], func=AF.Copy),
                             reads=[C.psb[ub + k2]], writes=[C32_b[h]])
                    else:
                        S.op("dve", lambda e, h=h, k2=k2, hh=hh: e.scalar_tensor_tensor(
                            out=C32[h][:, 0:129], in0=C32[h][:, 0:129], scalar=ebL[:, (c - 1) * 4 + h:(c - 1) * 4 + h + 1],
                            in1=U3[k2][:, hh, 0:129], op0=ALU.mult, op1=ALU.add),
                             reads=[C.psb[ub + k2], pre_b], writes=[C32_b[h]])
                    if c < 15:
                        S.op("act", lambda e, h=h: e.activation(out=Cbf[h][:, 0:129], in_=C32[h][:, 0:129],
                                                                func=AF.Copy, scale=ebL[:, c * 4 + h: c * 4 + h + 1]),
                             reads=[C32_b[h], pre_b], writes=[Cbf_b[h]])
                for k2 in range(2):
                    S.op("dve", lambda e, k2=k2: e.tensor_tensor(
                        out=t4[:, 2 * k2:2 * k2 + 2].unsqueeze(2), in0=R3[k2][:, :, 128:129],
                        in1=ebp[:, c * 4 + 2 * k2: c * 4 + 2 * k2 + 2].unsqueeze(2), op=ALU.mult),
                         reads=[C.psb[1 + k2], pre_b], writes=[t4_b])
                S.op("dve", lambda e: e.tensor_scalar(out=t4[:, 4:8], in0=t4[:, 0:4], scalar1=-1.0, scalar2=None,
                                                      op0=ALU.mult), writes=[t4_b])
                S.op("dve", lambda e: e.tensor_tensor(out=t4[:, 4:8], in0=t4[:, 4:8], in1=t4[:, 0:4], op=ALU.max),
                     writes=[t4_b])
                S.op("dve", lambda e: e.tensor_scalar(out=t4[:, 4:8], in0=t4[:, 4:8], scalar1=1.0, scalar2=None,
                                                      op0=ALU.max), writes=[t4_b])
                S.op("dve", lambda e: e.reciprocal(out=t4[:, 8:12], in_=t4[:, 4:8]), writes=[t4_b])
                S.op("dve", lambda e: e.tensor_tensor(out=t4[:, 12:16], in0=t4[:, 8:12], in1=ebp[:, cs4],
                                                      op=ALU.mult), reads=[pre_b], writes=[t4_b])
                for k2 in range(2):
                    S.op("dve", lambda e, k2=k2: e.tensor_tensor(
                        out=hv[:, 2 * k2:2 * k2 + 2, :], in0=R3[k2][:, :, 0:128],
                        in1=bc_mid(t4[:, 12 + 2 * k2: 14 + 2 * k2], 128), op=ALU.mult),
                         reads=[C.psb[1 + k2], t4_b], writes=[hv_b])
                S.op("pool", lambda e: e.tensor_tensor(out=sq[:], in0=hv[:], in1=hv[:], op=ALU.mult),
                     reads=[hv_b], writes=[sq_b])
                S.op("dve", lambda e: e.tensor_reduce(out=t4[:, 0:4], in_=sq[:], axis=AX.X, op=ALU.add),
                     reads=[sq_b], writes=[t4_b])
                S.op("act", lambda e: e.activation(out=t4[:, 4:8], in_=t4[:, 0:4], func=AF.Sqrt,
                                                   bias=C.eps_t[:, 0:1], scale=1.0 / 128), writes=[t4_b])
                S.op("dve", lambda e: e.reciprocal(out=t4[:, 8:12], in_=t4[:, 4:8]), writes=[t4_b])
                S.op("dve", lambda e: e.tensor_tensor(out=hv[:], in0=hv[:], in1=bc_mid(t4[:, 8:12], 128), op=ALU.mult),
                     reads=[t4_b, sq_b], writes=[hv_b])
                S.op("dve", lambda e: e.tensor_tensor(out=yb[vb][:], in0=hv[:].rearrange("p a b -> p (a b)"),
                                                      in1=G[:, c, :], op=ALU.mult),
                     reads=[hv_b, ld_b], writes=[yb_b[vb]])

                def try_(e):
                    ins = None
                    for h in range(4):
                        ins = e.transpose(out=tp[:, h, :], in_=yb[vb][:, h * 128:(h + 1) * 128], identity=C.ident[:])
                    return ins
                S.op("pe", try_, reads=[yb_b[vb]], writes=[C.psb[5]])
                S.op("act", lambda e: e.activation(out=ymT[:, :, tsl], in_=tp[:, 0:4, :], func=AF.Copy),
                     reads=[C.psb[5]], writes=[ymT_b])

            prefix(0)
            for c in range(16):
                if c < 15:
                    prefix(c + 1)
                rest(c)
            for h in range(4):
                S.dma("sp", SC["ymT"][h, :, t0:t0 + SEQ], ymT[:, h, :], pfx + "st", reads=[ymT_b])
        allb = [ld_b, cs_b, ktm_b, pre_b, t4_b, hv_b, sq_b, ymT_b] + vaug_b + Pm_b + C32_b + Cbf_b + yb_b
        phase_barrier(C, allb)


TOPK_BISECT = True
NBIS = 24
GS = 4


def dsa_phase(C, SC, w):
    nc, S = C.nc, C.S
    pfx = "ds"
    with ExitStack() as st:
        def sb(name, shape, dt):
            return st.enter_context(nc.sbuf_tensor(pfx + name, shape, dt))
        dqT = sb("dqT", [128, 8, SEQ], BF16)
        iqT = sb("iqT", [128, 4, SEQ], BF16)
        kiT2 = sb("kiT2", [128, SEQ], BF16)
        ckv = sb("ckv", [128, 16, 128], BF16)
        ckvT = sb("ckvT", [128, SEQ], BF16)
        smt = sb("smt", [128, 16, 16], F32)
        ld_b = Buf()
        wuv = sb("wuv", [128, 8, 64], BF16)
        cneg = sb("cneg", [128, 128], F32)
        onesf = sb("onesf", [128, 128], F32)
        onesb = sb("onesb", [128, 128], BF16)
        cs_b = Buf()
        score = [sb("score%d" % i, [128, SEQ], F32) for i in range(GS)]
        score_b = [Buf() for _ in range(GS)]
        work = [sb("work%d" % i, [128, SEQ], BF16) for i in range(GS)]
        work_b = [Buf() for _ in range(GS)]
        m8 = [sb("m8%d" % i, [128, 8], F32) for i in range(GS)]
        m8_b = [Buf() for _ in range(GS)]
        bs = [sb("bs%d" % i, [128, 8], F32) for i in range(GS)]
        bs_b = [Buf() for _ in range(GS)]
        wtab = [sb("wtab%d" % i, [128, 32], F32) for i in range(GS)]
        crow = sb("crow", [128, 32], F32)
        crow_b = Buf()
        for k in range(NBIS + 2):
            S.op("dve", lambda e, k=k: e.memset(crow[:, k:k + 1], float(2.0 ** (-k))), writes=[crow_b])
        rl = [sb("rl%d" % i, [128, 512], F32) for i in range(2)]
        rl_b = [Buf(), Buf()]
        sel = [sb("sel%d" % i, [128, SEQ], BF16) for i in range(2 * GS)]
        sel_b = [Buf() for _ in range(2 * GS)]
        negT = [sb("negT%d" % i, [128, 16, 128], BF16) for i in range(2)]
        negT_b = [Buf(), Buf()]
        pT = [sb("pT%d" % i, [128, 512], BF16) for i in range(2)]
        pT_b = [Buf(), Buf()]
        rec = sb("rec", [128, 512], F32)
        rec_b = Buf()
        oTn = sb("oTn", [128, 8, 128], BF16)
        oTn_b = Buf()
        ydT = sb("ydT", [128, 4, SEQ], BF16)
        ydT_b = Buf()
        S.dma("pool", wuv[:], w["wuv"], pfx + "cs", writes=[cs_b])
        S.dma("sp", cneg[:], w["c_cneg"], pfx + "cs", writes=[cs_b])
        S.dma("sp", onesf[:], w["c_ones"], pfx + "cs", writes=[cs_b])
        S.op("dve", lambda e: e.tensor_copy(out=onesb[:], in_=onesf[:]), reads=[cs_b], writes=[cs_b])
        cnts = {"pt": 0, "lt": 0}

        def s1_scores(bi, p):
            nk = (bi + 1) * 128
            tsl = slice(bi * 128, (bi + 1) * 128)
            sc = score[p]
            for ks in range((nk + 511) // 512):
                wd_ = min(512, nk - ks * 512)
                ksl = slice(ks * 512, ks * 512 + wd_)
                for h in range(8):
                    pb = h % 2
                    p0 = (h % 2) * 64
                    S.op("pe", lambda e, pb=pb, p0=p0, h=h, ksl=ksl, wd_=wd_: e.matmul(
                        C.ps[pb][:, 0:wd_], lhsT=iqT[p0:p0 + 64, h // 2, tsl], rhs=kiT2[p0:p0 + 64, ksl],
                        start=True, stop=True), reads=[ld_b], writes=[C.psb[pb]])
                    S.op("act", lambda e, pb=pb, wd_=wd_: e.activation(out=rl[pb][:, 0:wd_], in_=C.ps[pb][:, 0:wd_],
                                                                      func=AF.Relu),
                         reads=[C.psb[pb]], writes=[rl_b[pb]])
                    wcol = smt[:, bi, 8 + h: 9 + h]
                    if h == 0:
                        S.op("dve", lambda e, pb=pb, wd_=wd_, ksl=ksl, wcol=wcol: e.tensor_scalar(
                            out=sc[:, ksl], in0=rl[pb][:, 0:wd_], scalar1=wcol, scalar2=None, op0=ALU.mult),
                             reads=[rl_b[pb], ld_b], writes=[score_b[p]])
                    else:
                        S.op("dve", lambda e, pb=pb, wd_=wd_, ksl=ksl, wcol=wcol: e.scalar_tensor_tensor(
                            out=sc[:, ksl], in0=rl[pb][:, 0:wd_], scalar=wcol, in1=sc[:, ksl],
                            op0=ALU.mult, op1=ALU.add), reads=[rl_b[pb], ld_b], writes=[score_b[p]])
            if TOPK_BISECT and bi >= 2:
                S.op("dve", lambda e: e.tensor_reduce(out=bs[p][:, 5:6], in_=sc[:, 0:nk], axis=AX.X, op=ALU.min),
                     reads=[score_b[p]], writes=[bs_b[p]])
            S.op("dve", lambda e: e.tensor_tensor(out=sc[:, tsl], in0=sc[:, tsl], in1=cneg[:], op=ALU.add),
                 reads=[cs_b], writes=[score_b[p]])

        def s1_topk_pair(blocks, filler=()):
            for (bi, p) in blocks:
                nk = (bi + 1) * 128
                S.op("dve", lambda e, p=p, nk=nk: e.max(out=m8[p][:], in_=score[p][:, 0:nk]),
                     reads=[score_b[p]], writes=[m8_b[p]])
            for (bi, p) in blocks:
                S.op("dve", lambda e, p=p: e.tensor_tensor(out=bs[p][:, 1:2], in0=m8[p][:, 0:1], in1=bs[p][:, 5:6],
                                                           op=ALU.subtract), reads=[m8_b[p]], writes=[bs_b[p]])
            for (bi, p) in blocks:
                S.op("dve", lambda e, p=p: e.tensor_scalar(out=wtab[p][:, 0:NBIS + 2], in0=crow[:, 0:NBIS + 2],
                                                           scalar1=bs[p][:, 1:2], scalar2=None, op0=ALU.mult),
                     reads=[crow_b], writes=[bs_b[p]])
            for (bi, p) in blocks:
                S.op("dve", lambda e, p=p: e.tensor_tensor(out=bs[p][:, 2:3], in0=wtab[p][:, 1:2], in1=bs[p][:, 5:6],
                                                           op=ALU.add), writes=[bs_b[p]])
            filler = list(filler)
            per = (len(filler) + NBIS - 1) // NBIS
            for k in range(1, NBIS + 1):
                for _ in range(per):
                    if filler:
                        filler.pop(0)()
                for (bi, p) in blocks:
                    nk = (bi + 1) * 128
                    S.op("dve", lambda e, p=p, nk=nk: e.tensor_scalar(
                        out=work[p][:, 0:nk], in0=score[p][:, 0:nk], scalar1=bs[p][:, 2:3], scalar2=None,
                        op0=ALU.is_ge, op1=ALU.add, accum_out=bs[p][:, 3:4]),
                         reads=[score_b[p]], writes=[work_b[p], bs_b[p]])
                for (bi, p) in blocks:
                    S.op("dve", lambda e, p=p: e.tensor_scalar(
                        out=bs[p][:, 4:5], in0=bs[p][:, 3:4], scalar1=255.5, scalar2=-0.5,
                        op0=ALU.is_ge, op1=ALU.add), writes=[bs_b[p]])
                for (bi, p) in blocks:
                    S.op("dve", lambda e, p=p, k=k: e.scalar_tensor_tensor(
                        out=bs[p][:, 2:3], in0=bs[p][:, 4:5], scalar=wtab[p][:, k:k + 1], in1=bs[p][:, 2:3],
                        op0=ALU.mult, op1=ALU.add), writes=[bs_b[p]])
            while filler:
                filler.pop(0)()
            for (bi, p) in blocks:
                S.op("dve", lambda e, p=p: e.tensor_tensor(out=bs[p][:, 0:1], in0=bs[p][:, 2:3],
                                                           in1=wtab[p][:, NBIS + 1:NBIS + 2], op=ALU.subtract),
                     writes=[bs_b[p]])

        def s1_sel(bi, p, slot):
            nk = (bi + 1) * 128
            if bi >= 2:
                thr = bs[p][:, 0:1] if TOPK_BISECT else m8[p][:, 7:8]
                S.op("dve", lambda e: e.tensor_scalar(out=sel[slot][:, 0:nk], in0=score[p][:, 0:nk],
                                                      scalar1=thr, scalar2=None, op0=ALU.is_ge),
                     reads=[score_b[p], m8_b[p], bs_b[p]], writes=[sel_b[slot]])
            else:
                S.op("dve", lambda e: e.tensor_scalar(out=sel[slot][:, 0:nk], in0=score[p][:, 0:nk],
                                                      scalar1=-1.0e29, scalar2=None, op0=ALU.is_ge),
                     reads=[score_b[p]], writes=[sel_b[slot]])

        def s2_units(bi, slot):
            units = []
            nkc = bi + 1
            tsl = slice(bi * 128, (bi + 1) * 128)
            ng = bi % 2
            tp = C.psv_bf[2]
            for k0 in range(0, nkc, 8):
                n_ = min(8, nkc - k0)

                def u_tr(k0=k0, n_=n_):
                    def trs(e):
                        ins = None
                        for i in range(n_):
                            ins = e.transpose(out=tp[:, i, :], in_=sel[slot][:, (k0 + i) * 128:(k0 + i + 1) * 128],
                                              identity=C.ident[:])
                        return ins
                    S.op("pe", trs, reads=[sel_b[slot]], writes=[C.psb[2]])
                    S.op("dve", lambda e: e.tensor_scalar(
                        out=negT[ng][:, k0:k0 + n_, :], in0=tp[:, 0:n_, :], scalar1=-1.0, scalar2=30000.0,
                        op0=ALU.add, op1=ALU.mult), reads=[C.psb[2]], writes=[negT_b[ng]])
                units.append(u_tr)
            for g in range(2):
                ob = 5 + g
                for kc in range(nkc):
                    def u_att(g=g, ob=ob, kc=kc):
                        lb = 3 + (cnts["lt"] % 2)
                        cnts["lt"] += 1
                        ps_ = cnts["pt"] % 2
                        cnts["pt"] += 1
                        ksl = slice(kc * 128, (kc + 1) * 128)

                        def mlt(e):
                            o3 = C.ps[lb][:].rearrange("p (a b) -> p a b", b=128)
                            e.matmul(o3, lhsT=ckvT[:, ksl], rhs=dqT[:, 4 * g:4 * g + 4, tsl], start=True, stop=False)
                            return e.matmul(o3, lhsT=C.ident[:],
                                            rhs=negT[ng][:, kc, :].unsqueeze(1).to_broadcast([128, 4, 128]),
                                            start=False, stop=True)
                        S.op("pe", mlt, reads=[ld_b, negT_b[ng]], writes=[C.psb[lb]])
                        S.op("act", lambda e: e.activation(out=pT[ps_][:], in_=C.ps[lb][:], func=AF.Exp),
                             reads=[C.psb[lb]], writes=[pT_b[ps_]])

                        def mpv(e):
                            e.matmul(C.ps[ob][:], lhsT=ckv[:, kc, :], rhs=pT[ps_][:], start=(kc == 0),
                                     stop=(kc == nkc - 1))
                            return e.matmul(C.ps[7][:], lhsT=onesb[:], rhs=pT[ps_][:], start=(kc == 0),
                                            stop=(kc == nkc - 1))
                        S.op("pe", mpv, reads=[ld_b, cs_b, pT_b[ps_]], writes=[C.psb[ob], C.psb[7]])
                        if kc == nkc - 1:
                            S.op("dve", lambda e: e.reciprocal(out=rec[:], in_=C.ps[7][:]), reads=[C.psb[7]],
                                 writes=[rec_b])
                            S.op("dve", lambda e: e.tensor_tensor(
                                out=oTn[:, 4 * g:4 * g + 4, :].rearrange("p a b -> p (a b)"), in0=C.ps[ob][:],
                                in1=rec[:], op=ALU.mult), reads=[C.psb[ob], rec_b], writes=[oTn_b])
                    units.append(u_att)

            def u_up():
                def mup(e):
                    ins = None
                    for h in range(8):
                        p0 = (h % 2) * 64
                        ins = e.matmul(C.ps[2][p0:p0 + 64, (h // 2) * 128:(h // 2 + 1) * 128], lhsT=wuv[:, h, :],
                                       rhs=oTn[:, h, :], start=True, stop=True)
                    return ins
                S.op("pe", mup, reads=[oTn_b, cs_b], writes=[C.psb[2]])
                S.op("act", lambda e: e.activation(out=ydT[:, :, tsl],
                                                   in_=C.ps[2][:].rearrange("p (a b) -> p a b", b=128),
                                                   func=AF.Copy), reads=[C.psb[2]], writes=[ydT_b])
            units.append(u_up)
            return units

        for sq_i in range(2):
            t0 = sq_i * SEQ
            for h in range(8):
                S.dma("sp", dqT[:, h, :], SC["dqT"][h, :, t0:t0 + SEQ], pfx + "ld", writes=[ld_b])
            for c in range(4):
                S.dma("sp", iqT[:, c, :], SC["iqT"][c, :, t0:t0 + SEQ], pfx + "ld", writes=[ld_b])
            S.dma("sp", kiT2[0:64, :], SC["kiT"][:, t0:t0 + SEQ], pfx + "ld", writes=[ld_b])
            S.dma("sp", kiT2[64:128, :], SC["kiT"][:, t0:t0 + SEQ], pfx + "ld", writes=[ld_b])
            S.dma("sp", ckvT[:], SC["ckvT"][:, t0:t0 + SEQ], pfx + "ld", writes=[ld_b])
            S.dma("sp", ckv[:], SC["ckv"][t0:t0 + SEQ, :].rearrange("(c p) f -> p c f", p=128), pfx + "ld", writes=[ld_b])
            S.dma("sp", smt[:], SC["small"][t0:t0 + SEQ, :].rearrange("(c p) f -> p c f", p=128), pfx + "ld",
                  writes=[ld_b])

            def stage2_units(g):
                us = []
                for p in range(GS):
                    us += s2_units(GS * g + p, (g % 2) * GS + p)
                return us

            def stage1(g, filler=()):
                blocks = [(GS * g + p, p) for p in range(GS)]
                for (bi, p) in blocks:
                    s1_scores(bi, p)
                tk = [(bi, p) for (bi, p) in blocks if bi >= 2]
                s1_topk_pair(tk, filler)
                for (bi, p) in blocks:
                    s1_sel(bi, p, (g % 2) * GS + p)
            order = list(range(16 // GS - 1, -1, -1))
            stage1(order[0])
            for i_ in range(1, len(order)):
                stage1(order[i_], stage2_units(order[i_ - 1]))
            for u in stage2_units(order[-1]):
                u()
            for c in range(4):
                S.dma("sp", SC["ydT"][c, :, t0:t0 + SEQ], ydT[:, c, :], pfx + "st", reads=[ydT_b])
        allb = [ld_b, cs_b, rec_b, oTn_b, ydT_b, crow_b] + rl_b + pT_b + score_b + work_b + m8_b + bs_b + sel_b + negT_b
        phase_barrier(C, allb)


def post_phase(C, h1, mem, h3, SC, w):
    nc, S = C.nc, C.S
    pfx = "po"
    with ExitStack() as st:
        def sb(name, shape, dt):
            return st.enter_context(nc.sbuf_tensor(pfx + name, shape, dt))
        make_work(C, st, pfx)
        wout = sb("wout", [128, 8, 1024], BF16)
        wq = sb("wq", [128, 8, 1024], BF16)
        wo = sb("wo", [128, 8, 1024], BF16)
        wr_b = Buf()
        wkv = [sb("wkv%d" % i, [128, 8, 512], BF16) for i in range(2)]
        wkv_b = [Buf(), Buf()]
        onesf = sb("onesf", [128, 128], F32)
        onesb = sb("onesb", [128, 128], BF16)
        cs_b = Buf()
        mt = [sb("mt%d" % i, [128, 1024], F32) for i in range(2)]
        mt_b = [Buf(), Buf()]
        memT = sb("memT", [128, 8, 256], BF16)
        memT_b = Buf()
        kTx = sb("kTx", [128, 8, 256], BF16)
        kTx_b = Buf()
        vx = sb("vx", [128, 2, 1024], BF16)
        vx_b = Buf()
        h2t = sb("h2t", [128, 4, 1024], F32)
        h2t_b = [Buf() for _ in range(4)]
        ycat = sb("ycat", [128, 8, 512], BF16)
        ycat_b = Buf()
        u3T = sb("u3T", [128, 8, 512], BF16)
        u3T_b = Buf()
        qTx = sb("qTx", [128, 8, 512], BF16)
        qTx_b = Buf()
        pTx = [sb("pTx%d" % i, [128, 512], BF16) for i in range(2)]
        pTx_b = [Buf(), Buf()]
        rec = sb("rec", [128, 512], F32)
        rec_b = Buf()
        oTx = sb("oTx", [128, 8, 512], BF16)
        oTx_b = Buf()
        ot = [sb("ot%d" % i, [128, 1024], F32) for i in range(2)]
        ot_b = [Buf(), Buf()]
        vw = lambda a: a.rearrange("(kc p) n -> p kc n", p=128)
        S.dma("pool", wout[:], vw(w["w_out"]), pfx + "wr", writes=[wr_b])
        S.dma("pool", wq[:], vw(w["xattn_w_q"]), pfx + "wr", writes=[wr_b])
        S.dma("pool", wo[:], vw(w["xattn_w_o"]), pfx + "wr", writes=[wr_b])
        S.dma("sp", onesf[:], w["c_ones"], pfx + "cs", writes=[cs_b])
        S.op("dve", lambda e: e.tensor_copy(out=onesb[:], in_=onesf[:]), reads=[cs_b], writes=[cs_b])
        wkvv = vw(w["xattn_w_kv"])
        oi = 0
        for sq_i in range(2):
            t0 = sq_i * SEQ
            for m in range(2):
                S.dma("sp", mt[m][:], mem[sq_i * 256 + m * 128: sq_i * 256 + (m + 1) * 128, :], pfx + "mt%d" % m,
                      writes=[mt_b[m]])
                norm_transpose(C, st, mt[m][:], mt_b[m], C.gts[:, 3, :], memT, memT_b, m * 128, pfx)
            for piece in range(4):
                sl = piece % 2
                S.dma("pool", wkv[sl][:], wkvv[:, :, piece * 512:(piece + 1) * 512], pfx + "wkv%d" % sl,
                      writes=[wkv_b[sl]])
                if piece < 2:
                    for c4 in range(4):
                        ch = piece * 4 + c4
                        pb = ch % 2

                        def mk(e, sl=sl, c4=c4, pb=pb):
                            ins = None
                            for kc in range(8):
                                ins = e.matmul(C.ps[pb][:, 0:256], lhsT=wkv[sl][:, kc, c4 * 128:(c4 + 1) * 128],
                                               rhs=memT[:, kc, :], start=(kc == 0), stop=(kc == 7))
                            return ins
                        S.op("pe", mk, reads=[wkv_b[sl], memT_b], writes=[C.psb[pb]])
                        S.op("act", lambda e, ch=ch, pb=pb: e.activation(out=kTx[:, ch, :], in_=C.ps[pb][:, 0:256],
                                                                         func=AF.Copy, scale=float(256 ** -0.5)),
                             reads=[C.psb[pb]], writes=[kTx_b])
                else:
                    half = piece - 2
                    for mc in range(2):
                        pb = mc

                        def mv_(e, sl=sl, mc=mc, pb=pb):
                            ins = None
                            for kc in range(8):
                                ins = e.matmul(C.ps[pb][:], lhsT=memT[:, kc, mc * 128:(mc + 1) * 128],
                                               rhs=wkv[sl][:, kc, :], start=(kc == 0), stop=(kc == 7))
                            return ins
                        S.op("pe", mv_, reads=[wkv_b[sl], memT_b], writes=[C.psb[pb]])
                        S.op("act", lambda e, mc=mc, pb=pb, half=half: e.activation(
                            out=vx[:, mc, half * 512:(half + 1) * 512], in_=C.ps[pb][:], func=AF.Copy),
                             reads=[C.psb[pb]], writes=[vx_b])
            for s4 in range(4):
                ts0 = t0 + s4 * 512
                for c in range(4):
                    S.dma("sp", ycat[:, c, :], SC["ymT"][c, :, ts0:ts0 + 512], pfx + "yc", writes=[ycat_b])
                    S.dma("sp", ycat[:, 4 + c, :], SC["ydT"][c, :, ts0:ts0 + 512], pfx + "yc", writes=[ycat_b])
                for j in range(4):
                    S.dma("sp", h2t[:, j, :], h1[ts0 + j * 128: ts0 + (j + 1) * 128, :], pfx + "h%d" % j,
                          writes=[h2t_b[j]])
                for j in range(4):
                    for half in range(2):
                        pb = half

                        def mo_(e, j=j, half=half, pb=pb):
                            ins = None
                            for kc in range(8):
                                ins = e.matmul(C.ps[pb][:], lhsT=ycat[:, kc, j * 128:(j + 1) * 128],
                                               rhs=wout[:, kc, half * 512:(half + 1) * 512], start=(kc == 0),
                                               stop=(kc == 7))
                            return ins
                        S.op("pe", mo_, reads=[ycat_b, wr_b], writes=[C.psb[pb]])
                        S.op("dve", lambda e, j=j, half=half, pb=pb: e.tensor_tensor(
                            out=h2t[:, j, half * 512:(half + 1) * 512], in0=C.ps[pb][:],
                            in1=h2t[:, j, half * 512:(half + 1) * 512], op=ALU.add),
                             reads=[C.psb[pb]], writes=[h2t_b[j]])
                    norm_transpose(C, st, h2t[:, j, :], h2t_b[j], C.gts[:, 2, :], u3T, u3T_b, j * 128, pfx)
                for ch in range(8):
                    pb = ch % 2

                    def mq_(e, ch=ch, pb=pb):
                        ins = None
                        for kc in range(8):
                            ins = e.matmul(C.ps[pb][:], lhsT=wq[:, kc, ch * 128:(ch + 1) * 128], rhs=u3T[:, kc, :],
                                           start=(kc == 0), stop=(kc == 7))
                        return ins
                    S.op("pe", mq_, reads=[wr_b, u3T_b], writes=[C.psb[pb]])
                    S.op("act", lambda e, ch=ch, pb=pb: e.activation(out=qTx[:, ch, :], in_=C.ps[pb][:], func=AF.Copy),
                         reads=[C.psb[pb]], writes=[qTx_b])
                for h in range(4):
                    for mc in range(2):
                        pb = 3 + mc

                        def ml_(e, h=h, mc=mc, pb=pb):
                            ins = None
                            for dc in range(2):
                                ins = e.matmul(C.ps[pb][:], lhsT=kTx[:, 2 * h + dc, mc * 128:(mc + 1) * 128],
                                               rhs=qTx[:, 2 * h + dc, :], start=(dc == 0), stop=(dc == 1))
                            return ins
                        S.op("pe", ml_, reads=[kTx_b, qTx_b], writes=[C.psb[pb]])
                        S.op("act", lambda e, mc=mc, pb=pb: e.activation(out=pTx[mc][:], in_=C.ps[pb][:], func=AF.Exp),
                             reads=[C.psb[pb]], writes=[pTx_b[mc]])

                    def md_(e):
                        e.matmul(C.ps[7][:], lhsT=onesb[:], rhs=pTx[0][:], start=True, stop=False)
                        return e.matmul(C.ps[7][:], lhsT=onesb[:], rhs=pTx[1][:], start=False, stop=True)
                    S.op("pe", md_, reads=[cs_b] + pTx_b, writes=[C.psb[7]])
                    S.op("dve", lambda e: e.reciprocal(out=rec[:], in_=C.ps[7][:]), reads=[C.psb[7]], writes=[rec_b])
                    for dc in range(2):
                        pb = 5 + dc

                        def mo2(e, h=h, dc=dc, pb=pb):
                            ins = None
                            for mc in range(2):
                                ins = e.matmul(C.ps[pb][:], lhsT=vx[:, mc, (2 * h + dc) * 128:(2 * h + dc + 1) * 128],
                                               rhs=pTx[mc][:], start=(mc == 0), stop=(mc == 1))
                            return ins
                        S.op("pe", mo2, reads=[vx_b] + pTx_b, writes=[C.psb[pb]])
                        S.op("dve", lambda e, h=h, dc=dc, pb=pb: e.tensor_tensor(
                            out=oTx[:, 2 * h + dc, :], in0=C.ps[pb][:], in1=rec[:], op=ALU.mult),
                             reads=[C.psb[pb], rec_b], writes=[oTx_b])
                for j in range(4):
                    o = oi % 2
                    oi += 1
                    for half in range(2):
                        pb = half

                        def mf_(e, j=j, half=half, pb=pb):
                            ins = None
                            for kc in range(8):
                                ins = e.matmul(C.ps[pb][:], lhsT=oTx[:, kc, j * 128:(j + 1) * 128],
                                               rhs=wo[:, kc, half * 512:(half + 1) * 512], start=(kc == 0),
                                               stop=(kc == 7))
                            return ins
                        S.op("pe", mf_, reads=[oTx_b, wr_b], writes=[C.psb[pb]])
                        S.op("dve", lambda e, j=j, half=half, pb=pb, o=o: e.tensor_tensor(
                            out=ot[o][:, half * 512:(half + 1) * 512], in0=C.ps[pb][:],
                            in1=h2t[:, j, half * 512:(half + 1) * 512], op=ALU.add),
                             reads=[C.psb[pb], h2t_b[j]], writes=[ot_b[o]])
                    S.dma("sp", h3[ts0 + j * 128: ts0 + (j + 1) * 128, :], ot[o][:], pfx + "o%d" % o, reads=[ot_b[o]])
        W = C.work
        allb = [wr_b, cs_b, memT_b, kTx_b, vx_b, ycat_b, u3T_b, qTx_b, rec_b, oTx_b, W["junk_b"], W["ss_b"], W["xs_b"]] + \
            wkv_b + mt_b + h2t_b + pTx_b + ot_b
        phase_barrier(C, allb)


DBG_OUT = [("h1", [NTOK, D], F32), ("h3", [NTOK, D], F32), ("qkT", [8, 128, NTOK], BF16), ("dqT", [8, 128, NTOK], BF16),
           ("iqT", [4, 128, NTOK], BF16), ("v", [NTOK, 512], BF16), ("og", [NTOK, 512], BF16),
           ("ckv", [NTOK, 128], BF16), ("small", [NTOK, 16], F32), ("ckvT", [128, NTOK], BF16),
           ("kiT", [64, NTOK], BF16), ("ymT", [4, 128, NTOK], BF16), ("ydT", [4, 128, NTOK], BF16)]


def build(stop):
    nc = bass.Bass("TRN2", target_bir_lowering=False)
    C = Ctx()
    C.nc = nc

    def din(name, shape):
        return nc.dram_tensor(name, shape, F32, kind="ExternalInput").ap()

    x = din("x", [NTOK, D])
    mem = din("mem", [512, D])
    w = {}
    for name, shape in [("ffn1_w_gate", [D, DFF]), ("ffn1_w_up", [D, DFF]), ("ffn1_w_down", [DFF, D]),
                        ("ffn2_w_gate", [D, DFF]), ("ffn2_w_up", [D, DFF]), ("ffn2_w_down", [DFF, D]),
                        ("w_in", [D, 3792]), ("w_out", [D, D]), ("xattn_w_q", [D, D]),
                        ("xattn_w_kv", [D, 2 * D]), ("xattn_w_o", [D, D]),
                        ("gts", [128, 5, 8]), ("fin_g_bc", [128, D]), ("hg_bc", [128, 512]),
                        ("kvg_bc", [128, 128]), ("idxg_bc", [128, 64]), ("gbias_bc", [128, 8]),
                        ("convw", [128, 8, 4]), ("convb", [128, 8]), ("wuv", [128, 8, 64]),
                        ("c_ident", [128, 128]), ("c_triu", [128, 128]), ("c_ones", [128, 128]),
                        ("c_cneg", [128, 128])]:
        w[name] = din(name, shape)
    y = nc.dram_tensor("y", [NTOK, D], F32, kind="ExternalOutput").ap()
    SC = {}
    for name, shape, dt in DBG_OUT:
        kind = "ExternalOutput" if stop == 9 else "Internal"
        SC[name] = nc.dram_tensor("s_" + name, shape, dt, kind=kind).ap()
    h1, h3 = SC["h1"], SC["h3"]

    with ExitStack() as gst:
        S = Sync(nc, gst)
        C.S = S
        C.ps = [gst.enter_context(nc.psum_tensor("psb%d" % i, [128, 512], F32)) for i in range(8)]
        C.psb = [Buf() for _ in range(8)]
        C.psv_bf = [p[:].bitcast(BF16).rearrange("p (a b) -> p a b", b=128) for p in C.ps]
        cst = Buf()
        identf = gst.enter_context(nc.sbuf_tensor("identf", [128, 128], F32))
        C.ident = gst.enter_context(nc.sbuf_tensor("ident", [128, 128], BF16))
        C.gts = gst.enter_context(nc.sbuf_tensor("gts_sb", [128, 5, 8], F32))
        C.eps_t = gst.enter_context(nc.sbuf_tensor("eps_t", [128, 1], F32))
        S.dma("sp", identf[:], w["c_ident"], "c0", writes=[cst])
        S.dma("sp", C.gts[:], w["gts"], "c1", writes=[cst])
        S.op("dve", lambda e: e.tensor_copy(out=C.ident[:], in_=identf[:]), reads=[cst], writes=[cst])
        S.op("dve", lambda e: e.memset(C.eps_t[:], EPS), reads=[], writes=[cst])
        phase_barrier(C, [cst])

        ffn_phase(C, "f1", x, h1, C.gts[:, 0, :], w["ffn1_w_gate"], w["ffn1_w_up"], w["ffn1_w_down"])
        inproj_phase(C, h1, w["w_in"], SC, w)
        mlstm_phase(C, SC, w)
        dsa_phase(C, SC, w)
        post_phase(C, h1, mem, h3, SC, w)
        ffn_phase(C, "f2", h3, y, C.gts[:, 4, :], w["ffn2_w_gate"], w["ffn2_w_up"], w["ffn2_w_down"],
                  fin_g=w["fin_g_bc"])
    return nc


def host_layout(inp):
    f = lambda a: np.ascontiguousarray(np.asarray(a, dtype=np.float32))
    sh = {}
    for k in ["ffn1_w_gate", "ffn1_w_up", "ffn1_w_down", "ffn2_w_gate", "ffn2_w_up", "ffn2_w_down",
              "w_in", "w_out", "xattn_w_q", "xattn_w_kv", "xattn_w_o"]:
        sh[k] = f(inp[k][0])
    gt = lambda g: f(np.asarray(g).reshape(8, 128).T)
    sh["gts"] = f(np.stack([gt(inp["ffn1_norm_g"][0]), gt(inp["mix_norm_g"][0]), gt(inp["xattn_norm_g"][0]),
                            gt(inp["mem_norm_g"][0]), gt(inp["ffn2_norm_g"][0])], axis=1))
    bc = lambda v: f(np.broadcast_to(np.asarray(v).reshape(1, -1), (128, np.asarray(v).size)))
    sh["fin_g_bc"] = bc(inp["final_norm_g"])
    sh["hg_bc"] = bc(inp["mlstm_head_norm_g"][0])
    sh["kvg_bc"] = bc(inp["dsa_kv_norm_g"][0])
    sh["idxg_bc"] = bc(inp["idx_k_norm_g"][0])
    sh["gbias_bc"] = bc(np.concatenate([np.asarray(inp["mlstm_i_bias"][0]), np.asarray(inp["mlstm_f_bias"][0])]))
    sh["convw"] = f(np.asarray(inp["mlstm_conv_w"][0]).reshape(4, 8, 128).transpose(2, 1, 0))
    sh["convb"] = f(np.asarray(inp["mlstm_conv_b"][0]).reshape(8, 128).T)
    sh["wuv"] = f(np.asarray(inp["dsa_w_uv"][0]).transpose(1, 0, 2))
    p = np.arange(128)
    sh["c_ident"] = f(np.eye(128))
    sh["c_triu"] = f(p[:, None] <= p[None, :])
    sh["c_ones"] = f(np.ones((128, 128)))
    sh["c_cneg"] = f(np.where(p[None, :] <= p[:, None], 0.0, NEG))
    return sh


def kernel(**inputs):
    stop = 9 if DEBUG_HOOK is not None else 0
    shared = host_layout(inputs)
    xs = np.asarray(inputs["x"], dtype=np.float32).reshape(8, NTOK, D)
    ms = np.asarray(inputs["mem"], dtype=np.float32).reshape(8, 512, D)
    nc = build(stop)
    in_maps = []
    for c in range(8):
        m = dict(shared)
        m["x"] = np.ascontiguousarray(xs[c])
        m["mem"] = np.ascontiguousarray(ms[c])
        in_maps.append(m)
    res = run_bass_kernel_spmd(nc, in_maps, core_ids=list(range(8)))
    if DEBUG_HOOK is not None:
        DEBUG_HOOK(res)
    out = np.stack([np.asarray(r["y"], dtype=np.float32) for r in res.results], axis=0)
    return out.reshape(16, SEQ, D)
```
